# Optimizing a Trainium2 kernel written in Bass

```python
import math
import jax, jax.numpy as jnp
from jax import lax
import numpy as np

D_MODEL = 1024
BATCH = 8
SEQ = 8192
DEPTH = 4

N_MIXERS = 2
N_HEADS = 8
HEAD_DIM = D_MODEL // (2 * N_HEADS)
V_HEAD_DIM = 2 * HEAD_DIM
ROPE_THETA = 10000.0
Q_BLOCK = 128
SSM_GROUP = 16
N_GROUPS = D_MODEL // SSM_GROUP
SSM_STATE = 64
DT_MIN = 1e-3
DT_MAX = 1e-1
D_FF = 4 * D_MODEL
RMS_EPS = 1e-6
N_ATTN_LAYERS = (DEPTH + 1) // 2
N_SSM_LAYERS = DEPTH // 2

kernel_name = "hybrid_diffattn_s5_encoder"


def rms_norm(x, gain):
    xf = x.astype(jnp.float32)
    y = xf * lax.rsqrt(jnp.mean(xf * xf, axis=-1, keepdims=True) + RMS_EPS)
    return (y * gain.astype(jnp.float32)).astype(x.dtype)


def rope_tables(seq):
    pos = jnp.arange(seq, dtype=jnp.float32)
    inv_freq = ROPE_THETA ** (-jnp.arange(0, HEAD_DIM, 2, dtype=jnp.float32) / HEAD_DIM)
    ang = pos[:, None] * inv_freq[None, :]
    return jnp.cos(ang), jnp.sin(ang)


def apply_rope(t, cos, sin):
    tf = t.astype(jnp.float32)
    t1, t2 = jnp.split(tf, 2, axis=-1)
    c = cos[None, :, None, None, :]
    s = sin[None, :, None, None, :]
    out = jnp.concatenate([t1 * c - t2 * s, t2 * c + t1 * s], axis=-1)
    return out.astype(t.dtype)


def lambda_init(layer_idx):
    return 0.8 - 0.6 * math.exp(-0.3 * layer_idx)


def diff_attention(u, w_qkv, q_gain, k_gain, lam_vecs, subln_gain, w_o, lam_init, cos, sin):
    b, s, _ = u.shape
    qkv = u @ w_qkv
    q, k, v = jnp.split(qkv, 3, axis=-1)
    q = q.reshape(b, s, N_HEADS, 2, HEAD_DIM)
    k = k.reshape(b, s, N_HEADS, 2, HEAD_DIM)
    v = v.reshape(b, s, N_HEADS, V_HEAD_DIM)
    q = apply_rope(rms_norm(q, q_gain), cos, sin) * (HEAD_DIM ** -0.5)
    k = apply_rope(rms_norm(k, k_gain), cos, sin)
    lv = lam_vecs.astype(jnp.float32)
    lam = jnp.exp(jnp.sum(lv[0] * lv[1])) - jnp.exp(jnp.sum(lv[2] * lv[3])) + lam_init
    n_blocks = s // Q_BLOCK
    qb = q.reshape(b, n_blocks, Q_BLOCK, N_HEADS, 2, HEAD_DIM).transpose(1, 0, 2, 3, 4, 5)

    def block(q_blk):
        scores = jnp.einsum('bqhcd,bkhcd->bhcqk', q_blk, k,
                            preferred_element_type=jnp.float32)
        p = jax.nn.softmax(scores, axis=-1)
        attn = (p[:, :, 0] - lam * p[:, :, 1]).astype(v.dtype)
        return jnp.einsum('bhqk,bkhe->bqhe', attn, v)

    o = lax.map(block, qb)
    o = o.transpose(1, 0, 2, 3, 4).reshape(b, s, N_HEADS, V_HEAD_DIM)
    o = rms_norm(o, subln_gain) * (1.0 - lam_init)
    return o.reshape(b, s, D_MODEL) @ w_o


def _ssm_combine(e1, e2):
    a1, h1 = e1
    a2, h2 = e2
    return a1 * a2, a2 * h1 + h2


def s5_layer(u, a_re, a_im, log_dt, b_re, b_im, c_re, c_im, d_skip, w_glu):
    b, s, _ = u.shape
    f32 = jnp.float32
    lam = lax.complex(a_re.astype(f32), a_im.astype(f32))
    dt = jnp.exp(log_dt.astype(f32))[..., None]
    a_bar = jnp.exp(lam * dt)
    b_mat = lax.complex(b_re.astype(f32), b_im.astype(f32))
    b_bar = ((a_bar - 1.0) / lam)[..., None] * b_mat
    c_mat = lax.complex(c_re.astype(f32), c_im.astype(f32))
    ug = u.astype(f32).reshape(b, s, N_GROUPS, SSM_GROUP)

    def scan_dir(u_row, r, reverse):
        bu = jnp.einsum('sgh,gph->sgp', u_row.astype(jnp.complex64), b_bar[r])
        a = jnp.broadcast_to(a_bar[r], bu.shape)
        _, states = lax.associative_scan(_ssm_combine, (a, bu), reverse=reverse, axis=0)
        return jnp.einsum('sgp,ghp->sgh', states, c_mat[r]).real

    def row(u_row):
        return scan_dir(u_row, 0, False) + scan_dir(u_row, 1, True)

    y = lax.map(row, ug)
    y = y.reshape(b, s, D_MODEL) + d_skip.astype(f32) * u.astype(f32)
    g = jax.nn.gelu(y).astype(u.dtype)
    val, gate = jnp.split(g @ w_glu, 2, axis=-1)
    return val * jax.nn.sigmoid(gate)


def squared_relu_mlp(u, w_up, w_down):
    return jnp.square(jax.nn.relu(u @ w_up)) @ w_down


def setup_inputs(seed: int = 0) -> dict:
    key = jax.random.key(seed)
    ks = jax.random.split(key, 24)
    f32 = jnp.float32
    D, NA, NS = D_MODEL, N_ATTN_LAYERS, N_SSM_LAYERS
    G, P, HG = N_GROUPS, SSM_STATE, SSM_GROUP
    nrm = lambda k, shape, scale: jax.random.normal(k, shape, f32) * scale
    gain = lambda k, shape: 1.0 + 0.02 * jax.random.normal(k, shape, f32)
    x = jax.random.normal(ks[0], (BATCH, SEQ, D), f32)
    a_re = -0.5 * jnp.exp(0.02 * jax.random.normal(ks[9], (NS, 2, G, P), f32))
    a_im = (math.pi * jnp.arange(P, dtype=f32))[None, None, None, :] + \
        0.01 * jax.random.normal(ks[10], (NS, 2, G, P), f32)
    log_dt = jax.random.uniform(ks[11], (NS, 2, G), f32,
                                minval=math.log(DT_MIN), maxval=math.log(DT_MAX))
    return {
        "x": x,
        "norm_mix": gain(ks[1], (DEPTH, D)),
        "norm_ffn": gain(ks[2], (DEPTH, D)),
        "attn_w_qkv": nrm(ks[3], (NA, D, 3 * D), D ** -0.5),
        "attn_q_gain": gain(ks[4], (NA, HEAD_DIM)),
        "attn_k_gain": gain(ks[5], (NA, HEAD_DIM)),
        "attn_lambda": nrm(ks[6], (NA, 4, HEAD_DIM), 0.1),
        "attn_subln": gain(ks[7], (NA, V_HEAD_DIM)),
        "attn_w_o": nrm(ks[8], (NA, D, D), D ** -0.5),
        "ssm_a_re": a_re,
        "ssm_a_im": a_im,
        "ssm_log_dt": log_dt,
        "ssm_b_re": nrm(ks[12], (NS, 2, G, P, HG), (2 * HG) ** -0.5),
        "ssm_b_im": nrm(ks[13], (NS, 2, G, P, HG), (2 * HG) ** -0.5),
        "ssm_c_re": nrm(ks[14], (NS, 2, G, HG, P), P ** -0.5),
        "ssm_c_im": nrm(ks[15], (NS, 2, G, HG, P), P ** -0.5),
        "ssm_d": nrm(ks[16], (NS, D), 1.0),
        "ssm_w_glu": nrm(ks[17], (NS, D, 2 * D), D ** -0.5),
        "ffn_w_up": nrm(ks[18], (DEPTH, D, D_FF), D ** -0.5),
        "ffn_w_down": nrm(ks[19], (DEPTH, D_FF, D), D_FF ** -0.5),
    }


def reference(x, norm_mix, norm_ffn, attn_w_qkv, attn_q_gain, attn_k_gain, attn_lambda,
              attn_subln, attn_w_o, ssm_a_re, ssm_a_im, ssm_log_dt, ssm_b_re, ssm_b_im,
              ssm_c_re, ssm_c_im, ssm_d, ssm_w_glu, ffn_w_up, ffn_w_down):
    cos, sin = rope_tables(x.shape[1])
    h = x
    for i in range(DEPTH):
        u = rms_norm(h, norm_mix[i])
        j = i // N_MIXERS
        if i % N_MIXERS == 0:
            mix = diff_attention(u, attn_w_qkv[j], attn_q_gain[j], attn_k_gain[j],
                                 attn_lambda[j], attn_subln[j], attn_w_o[j],
                                 lambda_init(i), cos, sin)
        else:
            mix = s5_layer(u, ssm_a_re[j], ssm_a_im[j], ssm_log_dt[j], ssm_b_re[j],
                           ssm_b_im[j], ssm_c_re[j], ssm_c_im[j], ssm_d[j], ssm_w_glu[j])
        h = h + mix.astype(h.dtype)
        h = h + squared_relu_mlp(rms_norm(h, norm_ffn[i]), ffn_w_up[i], ffn_w_down[i]).astype(h.dtype)
    return h
```

```python
import contextlib
import math
import numpy as np
import ml_dtypes
import concourse.bass as bass
import concourse.mybir as mybir
from concourse.bass_utils import run_bass_kernel_spmd

F32 = mybir.dt.float32
BF16 = mybir.dt.bfloat16
I32 = mybir.dt.int32
AF = mybir.ActivationFunctionType
ALU = mybir.AluOpType
AX = mybir.AxisListType

D = 1024
DFF = 4096
NH = 8
EPS = 1e-6
ENGS = ("pe", "act", "dve", "pool", "sp")
SEM_MAX = 30000
NDMA_SLOTS = 24
FENCE_DIST = 0
FENCE_REPS = 2


class Op:
    __slots__ = ("eng", "fn", "deps", "idx", "need_inc", "is_dma", "semid", "semval", "slot_prev")

    def __init__(self, eng, fn, deps, is_dma):
        self.eng = eng
        self.fn = fn
        self.deps = deps
        self.is_dma = is_dma
        self.need_inc = False
        self.semid = None
        self.semval = None
        self.slot_prev = None


class Emitter:
    def __init__(self, nc, same_engine_sync=True):
        self.nc = nc
        self.ops = []
        self.last_w = {}
        self.readers = {}
        self.same_engine_sync = same_engine_sync
        self.nosync = set()
        self.cur_barrier = None
        self.dma_since_barrier = []
        self.last_on_eng = {}
        self.bar_a = nc.dram_tensor("bar_a", [1, 16], F32).ap()
        self.bar_b = nc.dram_tensor("bar_b", [1, 16], F32).ap()

    def barrier(self):
        deps = set(self.last_on_eng.values()) | set(self.dma_since_barrier)
        if self.cur_barrier is not None:
            deps.add(self.cur_barrier)
        a, b = self.bar_a, self.bar_b
        idx = self.op("sp", lambda e: e.dma_start(out=b, in_=a), (), (), is_dma=True)
        self.ops[idx].deps |= deps
        self.cur_barrier = idx
        self.dma_since_barrier = []

    def I(self, eng, name, reads=(), writes=(), nosync=False, **kw):
        idx = self.op(eng, (name, kw), reads, writes)
        if nosync:
            self.nosync.add(idx)
        return idx

    def op(self, eng, fn, reads=(), writes=(), is_dma=False):
        deps = set()
        for k in reads:
            w = self.last_w.get(k)
            if w is not None:
                deps.add(w)
        for k in writes:
            w = self.last_w.get(k)
            if w is not None:
                deps.add(w)
            for r in self.readers.get(k, ()):
                deps.add(r)
        if self.cur_barrier is not None:
            deps.add(self.cur_barrier)
        idx = len(self.ops)
        o = Op(eng, fn, deps, is_dma)
        o.idx = idx
        self.ops.append(o)
        if is_dma:
            self.dma_since_barrier.append(idx)
        elif fn is not None:
            self.last_on_eng[eng] = idx
        for k in reads:
            self.readers.setdefault(k, []).append(idx)
        for k in writes:
            self.last_w[k] = idx
            self.readers[k] = []
        return idx

    def dma(self, eng, out, in_, reads=(), writes=(), slow=False):
        if slow:
            fn = lambda e: e.dma_start(out=out, in_=in_, allow_slow_non_contiguous=True)
        else:
            fn = lambda e: e.dma_start(out=out, in_=in_)
        return self.op(eng, fn, reads, writes, is_dma=True)

    def finalize(self):
        nc = self.nc
        ops = self.ops
        per_eng = {e: [] for e in ENGS}
        for o in ops:
            per_eng[o.eng].append(o)
        pos = {}
        for e in ENGS:
            for i, o in enumerate(per_eng[e]):
                pos[o.idx] = i
        waited = {e: {p: -1 for p in ENGS} for e in ENGS}
        waited_dma = {e: set() for e in ENGS}
        wait_lists = {}
        for o in ops:
            wl = {}
            dl = []
            for d in o.deps:
                po = ops[d]
                if po.is_dma:
                    if d not in waited_dma[o.eng]:
                        waited_dma[o.eng].add(d)
                        dl.append(d)
                    continue
                pe_ = po.eng
                if pe_ == o.eng and not o.is_dma:
                    if pe_ == "pe" or not self.same_engine_sync or o.idx in self.nosync:
                        continue
                if pos[d] <= waited[o.eng][pe_]:
                    continue
                if pe_ not in wl or pos[d] > pos[wl[pe_]]:
                    wl[pe_] = d
            for pe_, d in wl.items():
                waited[o.eng][pe_] = pos[d]
                ops[d].need_inc = True
            wait_lists[o.idx] = list(wl.values()) + dl
        n_sems = {}
        dma_cnt = {e: 0 for e in ENGS}
        dma_last = {}
        for e in ENGS:
            cnt = 0
            semi = 0
            for o in per_eng[e]:
                if o.is_dma:
                    k = dma_cnt[e]
                    dma_cnt[e] += 1
                    slot = k % NDMA_SLOTS
                    o.semid = ("dma", e, slot)
                    o.semval = 16 * (k // NDMA_SLOTS + 1)
                    o.slot_prev = dma_last.get((e, slot))
                    dma_last[(e, slot)] = o
                    o.need_inc = True
                elif o.need_inc:
                    if cnt >= SEM_MAX:
                        semi += 1
                        cnt = 0
                    cnt += 1
                    o.semid = (e, semi)
                    o.semval = cnt
            n_sems[e] = semi + 1
        self.n_inst = {e: len(per_eng[e]) for e in ENGS}
        fence_need = {}
        for o in ops:
            if o.is_dma:
                for d in wait_lists[o.idx]:
                    if ops[d].is_dma and (o.idx - d) < FENCE_DIST:
                        fence_need[o.idx] = True
        self.n_fence = len(fence_need)
        for nm, ix in getattr(self, "debug_ops", {}).items():
            o = ops[ix]
            print("DEBUGOP", nm, "idx", ix, "eng", o.eng, "deps", sorted(o.deps), "waits", [(d, ops[d].eng, ops[d].is_dma, ops[d].semid, ops[d].semval) for d in wait_lists[ix]], "fenced", ix in fence_need, flush=True)
        fa = nc.dram_tensor("fence_a", [1, 16], F32).ap()
        fb = {e: nc.dram_tensor(f"fence_b_{e}", [1, 16], F32).ap() for e in ENGS}
        fence_cnt = {e: 0 for e in ENGS}
        with contextlib.ExitStack() as st:
            sems = {}
            fsem = {e: st.enter_context(nc.semaphore(f"fence_{e}")) for e in ("sp", "pool")}
            for e in ENGS:
                for i in range(n_sems[e]):
                    sems[(e, i)] = st.enter_context(nc.semaphore(f"s_{e}_{i}"))
                for s in range(min(NDMA_SLOTS, dma_cnt[e])):
                    sems[("dma", e, s)] = st.enter_context(nc.semaphore(f"d_{e}_{s}"))
            block = st.enter_context(nc.Block())

            def run(e):
                def body(eng):
                    for o in per_eng[e]:
                        if o.is_dma and o.slot_prev is not None:
                            p = o.slot_prev
                            eng.wait_ge(sems[p.semid], p.semval)
                        for d in wait_lists[o.idx]:
                            po = ops[d]
                            eng.wait_ge(sems[po.semid], po.semval)
                        if o.idx in fence_need:
                            for _ in range(FENCE_REPS):
                                fence_cnt[e] += 1
                                eng.dma_start(out=fb[e], in_=fa).then_inc(fsem[e], 16)
                                eng.wait_ge(fsem[e], 16 * fence_cnt[e])
                        if o.fn is None:
                            continue
                        if isinstance(o.fn, tuple):
                            ins = getattr(eng, o.fn[0])(**o.fn[1])
                        else:
                            ins = o.fn(eng)
                        if o.need_inc:
                            ins.then_inc(sems[o.semid], 16 if o.is_dma else 1)
                return body

            block.tensor(run("pe"))
            block.scalar(run("act"))
            block.vector(run("dve"))
            block.gpsimd(run("pool"))
            block.sync(run("sp"))


INPUT_SPECS = [
    ("norm_mix", (4, 1024)), ("norm_ffn", (4, 1024)), ("attn_w_qkv", (2, 1024, 3072)),
    ("attn_q_gain", (2, 64)), ("attn_k_gain", (2, 64)), ("attn_lambda", (2, 4, 64)),
    ("attn_subln", (2, 128)), ("attn_w_o", (2, 1024, 1024)), ("ssm_a_re", (2, 2, 64, 64)),
    ("ssm_a_im", (2, 2, 64, 64)), ("ssm_log_dt", (2, 2, 64)), ("ssm_b_re", (2, 2, 64, 64, 16)),
    ("ssm_b_im", (2, 2, 64, 64, 16)), ("ssm_c_re", (2, 2, 64, 16, 64)), ("ssm_c_im", (2, 2, 64, 16, 64)),
    ("ssm_d", (2, 1024)), ("ssm_w_glu", (2, 1024, 2048)), ("ffn_w_up", (4, 1024, 4096)),
    ("ffn_w_down", (4, 4096, 1024)),
]


class Prog:
    def __init__(self, S, phases):
        self.S = S
        self.NT = S // 128
        self.phases = phases
        self.nc = bass.Bass("TRN2", target_bir_lowering=False)
        self.em = Emitter(self.nc)
        self.uid = 0
        self.dbg_names = []

    def dram_in(self, name, shape, dt=F32):
        return self.nc.dram_tensor(name, list(shape), dt, kind="ExternalInput").ap()

    def sb(self, st, shape, dt, name=None):
        self.uid += 1
        return st.enter_context(self.nc.sbuf_tensor(f"{name or 't'}_{self.uid}", list(shape), dt))

    def ps(self, st, shape, dt, name=None):
        self.uid += 1
        return st.enter_context(self.nc.psum_tensor(f"{name or 'p'}_{self.uid}", list(shape), dt))

    def build(self):
        nc, S = self.nc, self.S
        self.x = self.dram_in("x", (S, D))
        self.w = {n: self.dram_in(n, shp) for n, shp in INPUT_SPECS}
        self.ident_f_d = self.dram_in("ident_f", (128, 128))
        self.ident_b_d = self.dram_in("ident_b", (128, 128), BF16)
        self.rope_d = self.dram_in("rope", (S, 64))
        self.jb_d = self.dram_in("anti_b", (128, 128), BF16)
        self.jf_d = self.dram_in("anti_f", (64, 64))
        self.out = nc.dram_tensor("out", [S, D], F32, kind="ExternalOutput").ap()
        self.qT_s = nc.dram_tensor("qT_s", [NH, 128, S], BF16).ap()
        self.kT_s = nc.dram_tensor("kT_s", [NH, 128, S], BF16).ap()
        self.v_s = nc.dram_tensor("v_s", [S, D], BF16).ap()
        self.oT_s = nc.dram_tensor("oT_s", [NH, 128, S], BF16).ap()
        self.uT_s = nc.dram_tensor("uT_s", [8, 128, S], BF16).ap()
        self.yb_s = nc.dram_tensor("yb_s", [8, 128, S], F32).ap()
        self.yf_s = nc.dram_tensor("yf_s", [8, 128, S], F32).ap()
        self.uTr_s = nc.dram_tensor("uTr_s", [8, 128, S], BF16).ap()
        em = self.em
        with contextlib.ExitStack() as st:
            self.ident_f = self.sb(st, [128, 128], F32, "identf")
            self.ident_b = self.sb(st, [128, 128], BF16, "identb")
            em.dma("sp", self.ident_f[:], self.ident_f_d, writes=["identf"])
            em.dma("sp", self.ident_b[:], self.ident_b_d, writes=["identb"])
            self.anti_b = self.sb(st, [128, 128], BF16, "antib")
            self.anti_f = self.sb(st, [64, 64], F32, "antif")
            em.dma("sp", self.anti_b[:], self.jb_d, writes=["antib"])
            em.dma("sp", self.anti_f[:], self.jf_d, writes=["antif"])
            for ph in self.phases:
                kind = ph[0]
                if kind == "copy":
                    self.phase_copy()
                elif kind == "ffn":
                    self.phase_ffn(ph[1], ph[2] if len(ph) > 2 else "out")
                elif kind == "attn":
                    self.phase_attn(ph[1], ph[2] if len(ph) > 2 else "out")
                elif kind == "s5":
                    self.phase_s5(ph[1])
            em.op("sp", None, reads=[f"out{i}" for i in range(self.NT)] + ["dbgout_" + n for n in self.dbg_names])
            em.finalize()
        return nc

    def dump(self, name, ap, keys):
        if not getattr(self, "dbg", 0):
            return
        if DUMPS is not None and name not in DUMPS:
            return
        shp = list(ap.shape)
        d = self.nc.dram_tensor("dbg_" + name, shp, ap.dtype, kind="ExternalOutput").ap()
        self.em.dma("sp", d, ap, reads=keys, writes=["dbgout_" + name])
        self.dbg_names.append(name)

    def src_ap(self, src):
        return self.x if src == "x" else self.out

    def phase_copy(self):
        em = self.em
        for i in range(self.NT):
            em.dma("sp", self.out[i * 128:(i + 1) * 128, :], self.x[i * 128:(i + 1) * 128, :],
                   writes=[f"out{i}"])


    def rms_rstd(self, hb_aps, junk, ss, ms, sd, rstd, keys_in, kp, dim=1024):
        em = self.em
        n = len(hb_aps)
        for a, hap in enumerate(hb_aps):
            em.I("act", "activation", reads=keys_in, writes=[kp + "junk", kp + "ss"],
                 out=junk[:], in_=hap, func=AF.Square, accum_out=ss[:, a:a + 1])
        em.I("dve", "tensor_scalar", reads=[kp + "ss"], writes=[kp + "ms"],
             out=ms[:, 0:n], in0=ss[:, 0:n], scalar1=1.0 / dim, scalar2=EPS, op0=ALU.mult, op1=ALU.add)
        em.I("act", "activation", reads=[kp + "ms"], writes=[kp + "sd"], out=sd[:, 0:n], in_=ms[:, 0:n], func=AF.Sqrt)
        em.I("dve", "reciprocal", reads=[kp + "sd"], writes=[kp + "rstd"], out=rstd[:, 0:n], in_=sd[:, 0:n])

    def load_weight(self, dsts, srcs, stg, gains, key_dst, engs=("act", "pool", "dve")):
        em = self.em
        for i, (dst, sa) in enumerate(zip(dsts, srcs)):
            b = i % 2
            em.dma("sp", stg[b], sa, writes=[f"stg{b}"])
            eng = engs[i % len(engs)]
            g = gains[i] if gains is not None else None
            rk = [f"stg{b}"] + (["gain"] if g is not None else [])
            wk = [f"{key_dst}{i}"]
            if eng == "act":
                if g is not None:
                    em.I("act", "activation", reads=rk, writes=wk, out=dst, in_=stg[b], func=AF.Copy, scale=g)
                else:
                    em.I("act", "activation", reads=rk, writes=wk, out=dst, in_=stg[b], func=AF.Copy)
            else:
                if g is not None:
                    em.I(eng, "tensor_scalar", reads=rk, writes=wk, out=dst, in0=stg[b], scalar1=g, scalar2=None, op0=ALU.mult)
                else:
                    em.I(eng, "tensor_copy", reads=rk, writes=wk, out=dst, in_=stg[b])

    def phase_ffn(self, L, src="out"):
        em, nc, S = self.em, self.nc, self.S
        ST = 256
        NST = S // ST
        src_t = self.src_ap(src)
        with contextlib.ExitStack() as st:
            self.em.barrier()
            Wup = self.sb(st, [128, 8, DFF], BF16, "wup")
            Wdn = self.sb(st, [128, 32, D], BF16, "wdn")
            gain = self.sb(st, [128, 8], F32, "gain")
            stgt = self.sb(st, [128, 2, 2048], F32, "stg")
            hb = [self.sb(st, [128, 2, D], F32, "hb") for _ in range(2)]
            u = self.sb(st, [128, 2, D], BF16, "u")
            uT = self.sb(st, [128, 8, ST], BF16, "uT")
            hid = self.sb(st, [128, 32, ST], BF16, "hid")
            r = [self.sb(st, [128, ST], BF16, "r") for _ in range(2)]
            junk = self.sb(st, [128, D], BF16, "junk")
            ss = self.sb(st, [128, 2], F32, "ss")
            ms = self.sb(st, [128, 2], F32, "ms")
            sd = self.sb(st, [128, 2], F32, "sd")
            rstd = self.sb(st, [128, 2], F32, "rstd")
            tpp = [self.ps(st, [128, 1024], BF16, "tpp") for _ in range(2)]
            upp = [self.ps(st, [128, 512], F32, "upp") for _ in range(2)]
            dnp = [self.ps(st, [128, 512], F32, "dnp") for _ in range(2)]
            stg = [stgt[:, 0, :], stgt[:, 1, :]]
            em.dma("sp", gain[:], self.w["norm_ffn"][L].rearrange("(kt p) -> p kt", p=128), writes=["gain"], slow=True)
            wu = self.w["ffn_w_up"][L]
            wd = self.w["ffn_w_down"][L]
            self.load_weight([Wup[:, i // 2, (i % 2) * 2048:(i % 2 + 1) * 2048] for i in range(16)],
                             [wu[(i // 2) * 128:(i // 2 + 1) * 128, (i % 2) * 2048:(i % 2 + 1) * 2048] for i in range(16)],
                             stg, [gain[:, i // 2:i // 2 + 1] for i in range(16)], "Wup")
            stg3 = [stgt[:, 0, :].rearrange("p (a d) -> p a d", a=2), stgt[:, 1, :].rearrange("p (a d) -> p a d", a=2)]
            self.load_weight([Wdn[:, 2 * i:2 * i + 2, :] for i in range(16)],
                             [wd[i * 256:(i + 1) * 256, :].rearrange("(a p) d -> p a d", p=128) for i in range(16)],
                             stg3, None, "Wdn")

            def load(sti):
                b = sti % 2
                em.dma("sp", hb[b][:], src_t[sti * ST:(sti + 1) * ST, :].rearrange("(a p) d -> p a d", p=128),
                       reads=[f"out{2 * sti}", f"out{2 * sti + 1}"], writes=[f"hb{b}"])

            load(0)
            for sti in range(NST):
                b = sti % 2
                if sti + 1 < NST:
                    load(sti + 1)
                self.rms_rstd([hb[b][:, a, :] for a in range(2)], junk, ss, ms, sd, rstd, [f"hb{b}"], "f")
                for a in range(2):
                    em.I("act", "activation", reads=[f"hb{b}", "frstd"], writes=[f"u{a}"],
                         out=u[:, a, :], in_=hb[b][:, a, :], func=AF.Copy, scale=rstd[:, a:a + 1])
                for a in range(2):
                    tp = tpp[a]
                    for kt in range(8):
                        em.I("pe", "transpose", reads=[f"u{a}", "identb"], writes=[f"tpp{a}"],
                             out=tp[:, kt * 128:(kt + 1) * 128], in_=u[:, a, kt * 128:(kt + 1) * 128], identity=self.ident_b[:])
                    em.I("dve", "tensor_copy", reads=[f"tpp{a}"], writes=["uT"],
                         out=uT[:, :, a * 128:(a + 1) * 128], in_=tp[:].rearrange("p (k t) -> p k t", k=8))
                for ft in range(32):
                    pb = ft % 2
                    for kt in range(8):
                        em.I("pe", "matmul", reads=[f"Wup{kt * 2 + ft // 16}", "uT"], writes=[f"upp{pb}"],
                             out=upp[pb][:, 0:ST], lhsT=Wup[:, kt, ft * 128:(ft + 1) * 128], rhs=uT[:, kt, :],
                             start=(kt == 0), stop=(kt == 7))
                    em.I("act", "activation", reads=[f"upp{pb}"], writes=[f"r{pb}"], out=r[pb][:], in_=upp[pb][:, 0:ST], func=AF.Relu)
                    em.I("dve", "tensor_tensor", reads=[f"r{pb}"], writes=[f"hid{ft}"],
                         out=hid[:, ft, :], in0=r[pb][:], in1=r[pb][:], op=ALU.mult)
                for a in range(2):
                    for nch in range(2):
                        pb = nch
                        for ft in range(32):
                            em.I("pe", "matmul", reads=[f"Wdn{ft // 2}", f"hid{ft}"], writes=[f"dnp{pb}"],
                                 out=dnp[pb][:], lhsT=hid[:, ft, a * 128:(a + 1) * 128],
                                 rhs=Wdn[:, ft, nch * 512:(nch + 1) * 512], start=(ft == 0), stop=(ft == 31))
                        em.I("dve", "tensor_tensor", reads=[f"dnp{pb}", f"hb{b}"], writes=[f"hb{b}"],
                             out=hb[b][:, a, nch * 512:(nch + 1) * 512], in0=dnp[pb][:],
                             in1=hb[b][:, a, nch * 512:(nch + 1) * 512], op=ALU.add)
                em.dma(STORE_Q, self.out[sti * ST:(sti + 1) * ST, :].rearrange("(a p) d -> p a d", p=128), hb[b][:],
                       reads=[f"hb{b}"], writes=[f"out{2 * sti}", f"out{2 * sti + 1}"])


    def phase_attn(self, L, src="out"):
        em, nc, S, w, NT = self.em, self.nc, self.S, self.w, self.NT
        j = L // 2
        lam_init = 0.8 - 0.6 * math.exp(-0.3 * L)
        src_t = self.src_ap(src)
        with contextlib.ExitStack() as st:
            self.em.barrier()
            Wq = self.sb(st, [128, 8, 3072], BF16, "Wq")
            gain = self.sb(st, [128, 8], F32, "gain")
            stgt = self.sb(st, [128, 2, 1536], F32, "stg")
            stg = [stgt[:, 0, :], stgt[:, 1, :]]
            hb = [self.sb(st, [128, D], F32, "hb") for _ in range(2)]
            rp = [self.sb(st, [128, 64], F32, "rp") for _ in range(2)]
            gq = self.sb(st, [128, 2, 64], F32, "gq")
            u = self.sb(st, [128, D], BF16, "u")
            uT = self.sb(st, [128, 8, 128], BF16, "uT")
            qk = self.sb(st, [128, 2048], F32, "qk")
            sq = self.sb(st, [128, 2048], F32, "sq")
            qr = self.sb(st, [128, 2048], BF16, "qr")
            vt = [self.sb(st, [128, D], BF16, "vt") for _ in range(2)]
            qkT = [self.sb(st, [128, 16, 128], BF16, "qkT") for _ in range(2)]
            t1 = self.sb(st, [128, 32, 32], F32, "t1")
            t2 = self.sb(st, [128, 32, 32], F32, "t2")
            t3 = self.sb(st, [128, 32, 32], F32, "t3")
            t4 = self.sb(st, [128, 32, 32], F32, "t4")
            junk = self.sb(st, [128, D], BF16, "junk")
            ss = self.sb(st, [128, 1], F32, "ss")
            ms = self.sb(st, [128, 1], F32, "ms")
            sd = self.sb(st, [128, 1], F32, "sd")
            rstd = self.sb(st, [128, 1], F32, "rstd")
            m32 = self.sb(st, [128, 32], F32, "m32")
            s32 = self.sb(st, [128, 32], F32, "s32")
            r32 = self.sb(st, [128, 32], F32, "r32")
            tpp = self.ps(st, [128, 1024], BF16, "tpp")
            mp = [self.ps(st, [128, 512], F32, "mp") for _ in range(2)]
            tq = self.ps(st, [128, 2048], BF16, "tq")
            em.dma("sp", gain[:], w["norm_mix"][L].rearrange("(kt p) -> p kt", p=128), writes=["gain"], slow=True)
            em.dma("sp", gq[:, 0, :], w["attn_q_gain"][j].partition_broadcast(128), writes=["gq0"], slow=True)
            em.dma("sp", gq[:, 1, :], w["attn_k_gain"][j].partition_broadcast(128), writes=["gq1"], slow=True)
            wq = w["attn_w_qkv"][j]
            self.load_weight([Wq[:, i // 2, (i % 2) * 1536:(i % 2 + 1) * 1536] for i in range(16)],
                             [wq[(i // 2) * 128:(i // 2 + 1) * 128, (i % 2) * 1536:(i % 2 + 1) * 1536] for i in range(16)],
                             stg, [gain[:, i // 2:i // 2 + 1] for i in range(16)], "Wq")

            def load(i):
                b = i % 2
                em.dma("sp", hb[b][:], src_t[i * 128:(i + 1) * 128, :], reads=[f"out{i}"] if src == "out" else [], writes=[f"hb{b}"])
                em.dma("sp", rp[b][:], self.rope_d[i * 128:(i + 1) * 128, :], writes=[f"rp{b}"])

            load(0)
            for i in range(NT):
                b = i % 2
                if i + 1 < NT:
                    load(i + 1)
                self.rms_rstd([hb[b][:]], junk, ss, ms, sd, rstd, [f"hb{b}"], "a")
                em.I("act", "activation", reads=[f"hb{b}", "arstd"], writes=["u"], out=u[:], in_=hb[b][:], func=AF.Copy, scale=rstd[:, 0:1])
                for kt in range(8):
                    em.I("pe", "transpose", reads=["u", "identb"], writes=["tpp"], out=tpp[:, kt * 128:(kt + 1) * 128],
                         in_=u[:, kt * 128:(kt + 1) * 128], identity=self.ident_b[:])
                em.I("dve", "tensor_copy", reads=["tpp"], writes=["uT"], out=uT[:], in_=tpp[:].rearrange("p (k t) -> p k t", k=8))
                for c in range(6):
                    pb = c % 2
                    for kt in range(8):
                        em.I("pe", "matmul", reads=[f"Wq{kt * 2 + c // 3}", "uT"], writes=[f"mp{pb}"], out=mp[pb][:], lhsT=uT[:, kt, :],
                             rhs=Wq[:, kt, c * 512:(c + 1) * 512], start=(kt == 0), stop=(kt == 7))
                    if c < 4:
                        em.I("act", "activation", reads=[f"mp{pb}"], writes=[f"qk{c}"], out=qk[:, c * 512:(c + 1) * 512], in_=mp[pb][:], func=AF.Copy)
                    else:
                        em.I("act", "activation", reads=[f"mp{pb}"], writes=[f"vt{b}"], out=vt[b][:, (c - 4) * 512:(c - 3) * 512], in_=mp[pb][:], func=AF.Copy)
                em.dma(STORE_Q, self.v_s[i * 128:(i + 1) * 128, :], vt[b][:], reads=[f"vt{b}"], writes=[f"vs{i}"])
                qkk = [f"qk{c}" for c in range(4)]
                em.I("act", "activation", reads=qkk, writes=["sq"], out=sq[:], in_=qk[:], func=AF.Square)
                em.I("dve", "tensor_reduce", reads=["sq"], writes=["m32"], out=m32[:], in_=sq[:].rearrange("p (g d) -> p g d", d=64), axis=AX.X, op=ALU.add)
                em.I("dve", "tensor_scalar", reads=["m32"], writes=["m32"], out=m32[:], in0=m32[:], scalar1=1.0 / 64, scalar2=EPS, op0=ALU.mult, op1=ALU.add)
                em.I("act", "activation", reads=["m32"], writes=["s32"], out=s32[:], in_=m32[:], func=AF.Sqrt)
                em.I("dve", "reciprocal", reads=["s32"], writes=["r32"], out=r32[:], in_=s32[:])
                qk3 = qk[:].rearrange("p (g d) -> p g d", d=64)
                em.I("dve", "tensor_tensor", reads=qkk + ["r32"], writes=["qkn"], out=qk3, in0=qk3, in1=r32[:].unsqueeze(2).to_broadcast([128, 32, 64]), op=ALU.mult)
                qk4 = qk[:].rearrange("p (a g d) -> p a g d", a=2, d=64)
                em.I("dve", "tensor_tensor", reads=["qkn", "gq0", "gq1"], writes=["qkn"], out=qk4, in0=qk4, in1=gq[:].unsqueeze(2).to_broadcast([128, 2, 16, 64]), op=ALU.mult)
                x1 = qk3[:, :, 0:32]
                x2 = qk3[:, :, 32:64]
                cosb = rp[b][:, 0:32].unsqueeze(1).to_broadcast([128, 32, 32])
                sinb = rp[b][:, 32:64].unsqueeze(1).to_broadcast([128, 32, 32])
                qr3 = qr[:].rearrange("p (g d) -> p g d", d=64)
                em.I("dve", "tensor_tensor", reads=["qkn", f"rp{b}"], writes=["t1"], out=t1[:], in0=x1, in1=cosb, op=ALU.mult)
                em.I("dve", "tensor_tensor", reads=["qkn", f"rp{b}"], writes=["t2"], out=t2[:], in0=x2, in1=sinb, op=ALU.mult)
                em.I("dve", "tensor_tensor", reads=["t1", "t2"], writes=["qr1"], out=qr3[:, :, 0:32], in0=t1[:], in1=t2[:], op=ALU.subtract)
                em.I("pool", "tensor_tensor", reads=["qkn", f"rp{b}"], writes=["t3"], out=t3[:], in0=x2, in1=cosb, op=ALU.mult)
                em.I("dve", "tensor_tensor", reads=["qkn", f"rp{b}"], writes=["t4"], out=t4[:], in0=x1, in1=sinb, op=ALU.mult)
                em.I("dve", "tensor_tensor", reads=["t3", "t4"], writes=["qr2"], out=qr3[:, :, 32:64], in0=t3[:], in1=t4[:], op=ALU.add)
                for jj in range(16):
                    em.I("pe", "transpose", reads=["qr1", "qr2", "identb"], writes=["tq"], out=tq[:, jj * 128:(jj + 1) * 128],
                         in_=qr[:, jj * 128:(jj + 1) * 128], identity=self.ident_b[:])
                em.I("act", "activation", reads=["tq"], writes=[f"qkT{b}"], out=qkT[b][:], in_=tq[:].rearrange("p (k t) -> p k t", k=16), func=AF.Copy)
                em.dma(STORE_Q, self.qT_s[:, :, i * 128:(i + 1) * 128].rearrange("h p t -> p h t"), qkT[b][:, 0:8, :],
                       reads=[f"qkT{b}"], writes=[f"qTs{i}"])
                em.dma(STORE_Q, self.kT_s[:, :, i * 128:(i + 1) * 128].rearrange("h p t -> p h t"), qkT[b][:, 8:16, :],
                       reads=[f"qkT{b}"], writes=[f"kTs{i}"])
        QC = 512
        NQC = S // QC
        with contextlib.ExitStack() as st:
            self.em.barrier()
            kTh = [self.sb(st, [128, S], BF16, "kTh") for _ in range(2)]
            qTh = [self.sb(st, [128, S], BF16, "qTh") for _ in range(2)]
            vh = [self.sb(st, [128, NT, 128], BF16, "vh") for _ in range(2)]
            pT = [self.sb(st, [128, 1024], BF16, "pT") for _ in range(3)]
            onesb = self.sb(st, [128, 128], BF16, "onesb")
            onesf = self.sb(st, [128, 128], F32, "onesf")
            lv = self.sb(st, [128, 4, 64], F32, "lv")
            lp = self.sb(st, [128, 64], F32, "lp")
            lsm = self.sb(st, [128, 2], F32, "lsm")
            lex = self.sb(st, [128, 2], F32, "lex")
            neglam = self.sb(st, [128, 1], F32, "neglam")
            gsc = self.sb(st, [128, 1], F32, "gsc")
            R = self.sb(st, [128, 1024], F32, "R")
            o0 = self.sb(st, [128, QC], F32, "o0")
            o1 = self.sb(st, [128, QC], F32, "o1")
            od = self.sb(st, [128, QC], F32, "od")
            osq = self.sb(st, [128, QC], F32, "osq")
            rs = self.sb(st, [128, QC], F32, "rs")
            oTt = [self.sb(st, [128, QC], BF16, "oTt") for _ in range(2)]
            sc = [self.ps(st, [128, 2, 512], F32, "sc") for _ in range(2)]
            O = [self.ps(st, [128, 512], F32, "O") for _ in range(2)]
            Lp = [self.ps(st, [128, 512], F32, "Lp") for _ in range(2)]
            em.I("pool", "memset", writes=["onesb"], ap=onesb[:], constant=1.0)
            em.I("pool", "memset", writes=["onesf"], ap=onesf[:], constant=1.0 / 128)
            onesf1 = self.sb(st, [128, 128], F32, "onesf1")
            em.I("pool", "memset", writes=["onesf1"], ap=onesf1[:], constant=1.0)
            em.dma("sp", lv[:], w["attn_lambda"][j].partition_broadcast(128), writes=["lv"], slow=True)
            em.dma("sp", gsc[:], w["attn_subln"][j].rearrange("(p o) -> p o", o=1), writes=["gsc"], slow=True)
            for k in range(2):
                em.I("dve", "tensor_tensor", reads=["lv"], writes=["lp"], out=lp[:], in0=lv[:, 2 * k, :], in1=lv[:, 2 * k + 1, :], op=ALU.mult)
                em.I("dve", "tensor_reduce", reads=["lp"], writes=["lsm"], out=lsm[:, k:k + 1], in_=lp[:], axis=AX.X, op=ALU.add)
            em.I("act", "activation", reads=["lsm"], writes=["lex"], out=lex[:], in_=lsm[:], func=AF.Exp)
            em.I("dve", "tensor_tensor", reads=["lex"], writes=["neglam"], out=neglam[:], in0=lex[:, 1:2], in1=lex[:, 0:1], op=ALU.subtract)
            em.I("dve", "tensor_scalar", reads=["neglam"], writes=["neglam"], out=neglam[:], in0=neglam[:], scalar1=-lam_init, scalar2=None, op0=ALU.add)
            em.I("dve", "tensor_scalar", reads=["gsc"], writes=["gsc"], out=gsc[:], in0=gsc[:], scalar1=(1.0 - lam_init), scalar2=None, op0=ALU.mult)

            def loadh(h):
                hb_ = h % 2
                em.dma("sp", kTh[hb_][:], self.kT_s[h], reads=[f"kTs{i}" for i in range(NT)], writes=[f"kTh{hb_}"])
                em.dma("sp", qTh[hb_][:], self.qT_s[h], reads=[f"qTs{i}" for i in range(NT)], writes=[f"qTh{hb_}"])
                em.dma("sp", vh[hb_][:], self.v_s[:, h * 128:(h + 1) * 128].rearrange("(kt p) e -> p kt e", p=128),
                       reads=[f"vs{i}" for i in range(NT)], writes=[f"vh{hb_}"], slow=True)

            loadh(0)
            it = 0
            for h in range(NH):
                hb_ = h % 2
                if h + 1 < NH:
                    loadh(h + 1)
                for qc in range(NQC):
                    qs = slice(qc * QC, (qc + 1) * QC)

                    def emit_S(kt, itn):
                        for c in range(2):
                            em.I("pe", "matmul", reads=[f"kTh{hb_}", f"qTh{hb_}"], writes=[f"sc{itn % 2}"], out=sc[itn % 2][:, c, :],
                                 lhsT=kTh[hb_][64 * c:64 * c + 64, kt * 128:(kt + 1) * 128], rhs=qTh[hb_][64 * c:64 * c + 64, qs],
                                 start=True, stop=True)

                    emit_S(0, it)
                    for kt in range(NT):
                        if kt + 1 < NT:
                            emit_S(kt + 1, it + 1)
                        pi = it % 3
                        em.I("act", "activation", reads=[f"sc{it % 2}"], writes=[f"pT{pi}"], out=pT[pi][:],
                             in_=sc[it % 2][:].rearrange("p c q -> p (c q)"), func=AF.Exp, scale=0.125)
                        for c in range(2):
                            em.I("pe", "matmul", reads=[f"vh{hb_}", f"pT{pi}"], writes=[f"O{c}"], out=O[c][:], lhsT=vh[hb_][:, kt, :],
                                 rhs=pT[pi][:, c * 512:(c + 1) * 512], start=(kt == 0), stop=(kt == NT - 1))
                        for c in range(2):
                            em.I("pe", "matmul", reads=["onesb", f"pT{pi}"], writes=[f"L{c}"], out=Lp[c][:], lhsT=onesb[:],
                                 rhs=pT[pi][:, c * 512:(c + 1) * 512], start=(kt == 0), stop=(kt == NT - 1))
                        it += 1
                    for c in range(2):
                        em.I("dve", "reciprocal", reads=[f"L{c}"], writes=[f"R{c}"], out=R[:, c * 512:(c + 1) * 512], in_=Lp[c][:])
                    em.I("dve", "tensor_tensor", reads=["O0", "R0"], writes=["o0"], out=o0[:], in0=O[0][:], in1=R[:, 0:512], op=ALU.mult)
                    em.I("dve", "tensor_tensor", reads=["O1", "R1"], writes=["o1"], out=o1[:], in0=O[1][:], in1=R[:, 512:1024], op=ALU.mult)
                    em.I("dve", "scalar_tensor_tensor", reads=["o0", "o1", "neglam"], writes=["od"], out=od[:], in0=o1[:], scalar=neglam[:, 0:1],
                         in1=o0[:], op0=ALU.mult, op1=ALU.add)
                    em.I("pool", "tensor_tensor", reads=["od"], writes=["osq"], out=osq[:], in0=od[:], in1=od[:], op=ALU.mult)
                    em.I("pe", "matmul", reads=["onesf", "osq", "R0"], writes=["L0"], out=Lp[0][:], lhsT=onesf[:], rhs=osq[:], start=True, stop=True)
                    em.I("dve", "tensor_scalar", reads=["L0"], writes=["rs"], out=rs[:], in0=Lp[0][:], scalar1=EPS, scalar2=None, op0=ALU.add)
                    em.I("act", "activation", reads=["rs"], writes=["rs"], out=rs[:], in_=rs[:], func=AF.Ln)
                    em.I("act", "activation", reads=["rs"], writes=["rs"], out=rs[:], in_=rs[:], func=AF.Exp, scale=-0.5)
                    ob = (h * NQC + qc) % 2
                    em.I("dve", "scalar_tensor_tensor", reads=["od", "gsc", "rs"], writes=[f"oTt{ob}"], out=oTt[ob][:], in0=od[:], scalar=gsc[:, 0:1],
                         in1=rs[:], op0=ALU.mult, op1=ALU.mult)
                    em.dma(STORE_Q, self.oT_s[h, :, qs], oTt[ob][:], reads=[f"oTt{ob}"], writes=[f"oTs{h}_{qc}"])
        with contextlib.ExitStack() as st:
            self.em.barrier()
            Wo = self.sb(st, [128, 8, D], BF16, "Wo")
            stgt = self.sb(st, [128, 2, 2048], F32, "stg")
            stg3 = [stgt[:, 0, :].rearrange("p (a d) -> p a d", a=2), stgt[:, 1, :].rearrange("p (a d) -> p a d", a=2)]
            hb = [self.sb(st, [128, D], F32, "hb") for _ in range(2)]
            oTi = [self.sb(st, [128, 8, 128], BF16, "oTi") for _ in range(2)]
            wop = [self.ps(st, [128, 512], F32, "wop") for _ in range(2)]
            wo = w["attn_w_o"][j]
            self.load_weight([Wo[:, 2 * i:2 * i + 2, :] for i in range(4)],
                             [wo[i * 256:(i + 1) * 256, :].rearrange("(a p) d -> p a d", p=128) for i in range(4)], stg3, None, "Wo")

            def load3(i):
                b = i % 2
                em.dma("sp", hb[b][:], src_t[i * 128:(i + 1) * 128, :], reads=[f"out{i}"] if src == "out" else [], writes=[f"hb{b}"])
                qcs = (i * 128) // QC
                em.dma("sp", oTi[b][:], self.oT_s[:, :, i * 128:(i + 1) * 128].rearrange("h p t -> p h t"),
                       reads=[f"oTs{h}_{qcs}" for h in range(NH)], writes=[f"oTi{b}"])

            load3(0)
            for i in range(NT):
                b = i % 2
                if i + 1 < NT:
                    load3(i + 1)
                for nch in range(2):
                    for h in range(NH):
                        em.I("pe", "matmul", reads=[f"Wo{h // 2}", f"oTi{b}"], writes=[f"wop{nch}"], out=wop[nch][:], lhsT=oTi[b][:, h, :],
                             rhs=Wo[:, h, nch * 512:(nch + 1) * 512], start=(h == 0), stop=(h == NH - 1))
                    em.I("dve", "tensor_tensor", reads=[f"wop{nch}", f"hb{b}"], writes=[f"hb{b}"], out=hb[b][:, nch * 512:(nch + 1) * 512], in0=wop[nch][:],
                         in1=hb[b][:, nch * 512:(nch + 1) * 512], op=ALU.add)
                em.dma(STORE_Q, self.out[i * 128:(i + 1) * 128, :], hb[b][:], reads=[f"hb{b}"], writes=[f"out{i}"])

    def s5_prep_dir(self, st, j, r, T):
        em, nc, w = self.em, self.nc, self.w
        kp = "s5p_"
        are, aim, ldt = T["are"], T["aim"], T["ldt"]
        em.dma("sp", are[:], w["ssm_a_re"][j, r].rearrange("(gp gpar) p -> (gpar p) gp", gpar=2), writes=[kp + "are"], slow=True)
        em.dma("sp", aim[:], w["ssm_a_im"][j, r].rearrange("(gp gpar) p -> (gpar p) gp", gpar=2), writes=[kp + "aim"], slow=True)
        for gpar in range(2):
            em.dma("sp", ldt[gpar * 64:(gpar + 1) * 64, :],
                   w["ssm_log_dt"][j, r].rearrange("(gp gpar) -> gpar gp", gpar=2)[gpar].partition_broadcast(64),
                   writes=[kp + f"ldt{gpar}"], slow=True)
        dt_, rho, th = T["dt"], T["rho"], T["th"]
        em.I("act", "activation", reads=[kp + "ldt0", kp + "ldt1"], writes=[kp + "dt"], out=dt_[:], in_=ldt[:], func=AF.Exp)
        em.I("dve", "tensor_tensor", reads=[kp + "dt", kp + "are"], writes=[kp + "rho"], out=rho[:], in0=are[:], in1=dt_[:], op=ALU.mult)
        em.I("dve", "tensor_tensor", reads=[kp + "dt", kp + "aim"], writes=[kp + "th"], out=th[:], in0=aim[:], in1=dt_[:], op=ALU.mult)
        mag = T["mag"]
        em.I("act", "activation", reads=[kp + "rho"], writes=[kp + "mag"], out=mag[:], in_=rho[:], func=AF.Exp)
        kf, z = T["kf"], T["z"]
        MAGIC = 12582912.0
        em.I("dve", "tensor_scalar", reads=[kp + "th"], writes=[kp + "kf"], out=kf[:], in0=th[:], scalar1=1.0 / (2 * math.pi), scalar2=MAGIC,
             op0=ALU.mult, op1=ALU.add)
        em.I("dve", "tensor_scalar", reads=[kp + "kf"], writes=[kp + "kf"], out=kf[:], in0=kf[:], scalar1=-MAGIC, scalar2=None, op0=ALU.add)
        em.I("dve", "scalar_tensor_tensor", reads=[kp + "kf", kp + "th"], writes=[kp + "z"], out=z[:], in0=kf[:], scalar=-2 * math.pi, in1=th[:],
             op0=ALU.mult, op1=ALU.add)
        sw, sw2, cw = T["sw"], T["sw2"], T["cw"]
        em.I("act", "activation", reads=[kp + "z"], writes=[kp + "sw"], out=sw[:], in_=z[:], func=AF.Sin, scale=0.5)
        em.I("act", "activation", reads=[kp + "z"], writes=[kp + "sw2"], out=sw2[:], in_=z[:], func=AF.Sin, scale=0.25)
        em.I("dve", "tensor_tensor", reads=[kp + "sw2"], writes=[kp + "cw"], out=cw[:], in0=sw2[:], in1=sw2[:], op=ALU.mult)
        em.I("dve", "tensor_scalar", reads=[kp + "cw"], writes=[kp + "cw"], out=cw[:], in0=cw[:], scalar1=-2.0, scalar2=1.0, op0=ALU.mult, op1=ALU.add)
        sn, cs = T["sn"], T["cs"]
        em.I("dve", "scalar_tensor_tensor", reads=[kp + "sw", kp + "cw"], writes=[kp + "sn"], out=sn[:], in0=sw[:], scalar=2.0, in1=cw[:], op0=ALU.mult, op1=ALU.mult)
        em.I("dve", "tensor_tensor", reads=[kp + "sw"], writes=[kp + "cs"], out=cs[:], in0=sw[:], in1=sw[:], op=ALU.mult)
        em.I("dve", "tensor_scalar", reads=[kp + "cs"], writes=[kp + "cs"], out=cs[:], in0=cs[:], scalar1=-2.0, scalar2=1.0, op0=ALU.mult, op1=ALU.add)
        ar, ai = T["ar"], T["ai"]
        em.I("dve", "tensor_tensor", reads=[kp + "mag", kp + "cs"], writes=[kp + "ar"], out=ar[:], in0=mag[:], in1=cs[:], op=ALU.mult)
        em.I("dve", "tensor_tensor", reads=[kp + "mag", kp + "sn"], writes=[kp + "ai"], out=ai[:], in0=mag[:], in1=sn[:], op=ALU.mult)
        C1, C2 = T["C1"], T["C2"]
        o_ = r * 64
        em.I("dve", "tensor_copy", reads=[kp + "ar"], writes=["C1"], out=C1[:, o_:o_ + 32], in_=ar[:])
        em.I("dve", "tensor_copy", reads=[kp + "ar"], writes=["C1"], out=C1[:, o_ + 32:o_ + 64], in_=ar[:])
        em.I("dve", "tensor_scalar", reads=[kp + "ai"], writes=["C2"], out=C2[:, o_:o_ + 32], in0=ai[:], scalar1=-1.0, scalar2=None, op0=ALU.mult)
        em.I("dve", "tensor_copy", reads=[kp + "ai"], writes=["C2"], out=C2[:, o_ + 32:o_ + 64], in_=ai[:])
        nr, den, t1, t2, qre, qim = T["nr"], T["den"], T["t1"], T["t2"], T["qre"], T["qim"]
        em.I("dve", "tensor_scalar", reads=[kp + "ar"], writes=[kp + "nr"], out=nr[:], in0=ar[:], scalar1=-1.0, scalar2=None, op0=ALU.add)
        em.I("dve", "tensor_tensor", reads=[kp + "are"], writes=[kp + "den"], out=den[:], in0=are[:], in1=are[:], op=ALU.mult)
        em.I("dve", "tensor_tensor", reads=[kp + "aim"], writes=[kp + "t1"], out=t1[:], in0=aim[:], in1=aim[:], op=ALU.mult)
        em.I("dve", "tensor_tensor", reads=[kp + "den", kp + "t1"], writes=[kp + "den"], out=den[:], in0=den[:], in1=t1[:], op=ALU.add)
        em.I("dve", "reciprocal", reads=[kp + "den"], writes=[kp + "den"], out=den[:], in_=den[:])
        em.I("dve", "tensor_tensor", reads=[kp + "nr"], writes=[kp + "t1"], out=t1[:], in0=nr[:], in1=are[:], op=ALU.mult)
        em.I("dve", "tensor_tensor", reads=[kp + "ai"], writes=[kp + "t2"], out=t2[:], in0=ai[:], in1=aim[:], op=ALU.mult)
        em.I("dve", "tensor_tensor", reads=[kp + "t1", kp + "t2"], writes=[kp + "qre"], out=qre[:], in0=t1[:], in1=t2[:], op=ALU.add)
        em.I("dve", "tensor_tensor", reads=[kp + "qre", kp + "den"], writes=[kp + "qre"], out=qre[:], in0=qre[:], in1=den[:], op=ALU.mult)
        em.I("dve", "tensor_tensor", reads=[kp + "ai"], writes=[kp + "t1"], out=t1[:], in0=ai[:], in1=are[:], op=ALU.mult)
        em.I("dve", "tensor_tensor", reads=[kp + "nr"], writes=[kp + "t2"], out=t2[:], in0=nr[:], in1=aim[:], op=ALU.mult)
        em.I("dve", "tensor_tensor", reads=[kp + "t1", kp + "t2"], writes=[kp + "qim"], out=qim[:], in0=t1[:], in1=t2[:], op=ALU.subtract)
        em.I("dve", "tensor_tensor", reads=[kp + "qim", kp + "den"], writes=[kp + "qim"], out=qim[:], in0=qim[:], in1=den[:], op=ALU.mult)
        if r == 1 and self.dbg == 3:
            for nm in ("th", "z", "sn", "cs", "ar", "ai", "qre", "qim", "mag", "dt"):
                self.dump(nm, T[nm][:], [kp + nm])
            self.dump("C1", C1[:], ["C1"])
            self.dump("C2", C2[:], ["C2"])
        bre, bim, bbr, bbi, tb = T["bre"], T["bim"], T["bbr"], T["bbi"], T["tb"]
        em.dma("sp", bre[:], w["ssm_b_re"][j, r].rearrange("(gp gpar) p hi -> (gpar p) gp hi", gpar=2), writes=[kp + "bre"], slow=True)
        em.dma("sp", bim[:], w["ssm_b_im"][j, r].rearrange("(gp gpar) p hi -> (gpar p) gp hi", gpar=2), writes=[kp + "bim"], slow=True)
        qre_b = qre[:].unsqueeze(2).to_broadcast([128, 32, 16])
        qim_b = qim[:].unsqueeze(2).to_broadcast([128, 32, 16])
        em.I("dve", "tensor_tensor", reads=[kp + "bre", kp + "qre"], writes=[kp + "bbr"], out=bbr[:], in0=bre[:], in1=qre_b, op=ALU.mult)
        em.I("dve", "tensor_tensor", reads=[kp + "bim", kp + "qim"], writes=[kp + "tb"], out=tb[:], in0=bim[:], in1=qim_b, op=ALU.mult)
        em.I("dve", "tensor_tensor", reads=[kp + "bbr", kp + "tb"], writes=[kp + "bbr"], out=bbr[:], in0=bbr[:], in1=tb[:], op=ALU.subtract)
        em.I("dve", "tensor_tensor", reads=[kp + "bim", kp + "qre"], writes=[kp + "bbi"], out=bbi[:], in0=bim[:], in1=qre_b, op=ALU.mult)
        em.I("dve", "tensor_tensor", reads=[kp + "bre", kp + "qim"], writes=[kp + "tb"], out=tb[:], in0=bre[:], in1=qim_b, op=ALU.mult)
        em.I("dve", "tensor_tensor", reads=[kp + "bbi", kp + "tb"], writes=[kp + "bbi"], out=bbi[:], in0=bbi[:], in1=tb[:], op=ALU.add)
        Bq, WB, tps = T["Bq"], T["WB"][r], T["tps"]
        for ri, bb in enumerate((bbr, bbi)):
            for gp in range(32):
                k = gp % 4
                col = ri * 32 + gp
                em.I("dve", "tensor_copy", reads=[kp + "bbr", kp + "bbi", f"Bqz{k}"], writes=[f"Bq{k}"],
                     out=Bq[k][0:64, 32 * k:32 * k + 16], in_=bb[0:64, gp, :])
                em.I("dve", "tensor_copy", reads=[kp + "bbr", kp + "bbi"], writes=[f"Bq{k}"],
                     out=Bq[k][64:128, 32 * k + 16:32 * k + 32], in_=bb[64:128, gp, :])
                pb = col % 2
                em.I("pe", "transpose", reads=[f"Bq{k}", "identf"], writes=[f"tps{pb}"], out=tps[pb][:, 0:128], in_=Bq[k][:], identity=self.ident_f[:])
                em.I("act", "activation", reads=[f"tps{pb}"], writes=[f"WB{r}_{col}"], out=WB[:, col, :], in_=tps[pb][:, 0:128], func=AF.Copy)
        if r == 1 and self.dbg == 3:
            self.dump("bbr", bbr[:], [kp + "bbr"])
            self.dump("bbi", bbi[:], [kp + "bbi"])
            self.dump("WB", WB[:], [f"WB{r}_{c}" for c in range(64)])
        cin, Cq, WC = T["cin"], T["Cq"], T["WC"][r]
        for ri, nm in enumerate(("ssm_c_re", "ssm_c_im")):
            csrc = w[nm][j, r].rearrange("(gp gpar) ho p -> gp ho gpar p", gpar=2)
            for i in range(4):
                b = (ri * 4 + i) % 2
                for gl in range(8):
                    em.dma("sp", cin[b][gl * 16:(gl + 1) * 16, :].rearrange("ho (gpar p) -> ho gpar p", gpar=2), csrc[8 * i + gl],
                           writes=[f"cin{b}_{gl}"])
                em.I("pe", "transpose", reads=[f"cin{b}_{gl}" for gl in range(8)] + ["identf"], writes=[f"tps{b}"], out=tps[b][:, 0:128], in_=cin[b][:], identity=self.ident_f[:])
                em.I("act", "activation", reads=[f"tps{b}"], writes=[kp + f"Cq{ri}"], out=Cq[ri][:, 8 * i:8 * i + 8, :],
                     in_=tps[b][:, 0:128].rearrange("q (g h) -> q g h", h=16), func=AF.Copy, scale=(1.0 if ri == 0 else -1.0))
            for k in range(4):
                c0 = 32 * k
                em.I("dve", "tensor_copy", reads=[kp + f"Cq{ri}", "WCz"], writes=[f"WC{r}_{ri}"],
                     out=WC[0:64, ri * 32 + k:ri * 32 + 32:4, c0:c0 + 16], in_=Cq[ri][0:64, k:32:4, :])
                em.I("dve", "tensor_copy", reads=[kp + f"Cq{ri}", "WCz"], writes=[f"WC{r}_{ri}"],
                     out=WC[64:128, ri * 32 + k:ri * 32 + 32:4, c0 + 16:c0 + 32], in_=Cq[ri][64:128, k:32:4, :])

    def phase_s5(self, L):
        em, nc, S, w = self.em, self.nc, self.S, self.w
        j = L // 2
        NT = self.NT
        GC = 2.0 * math.sqrt(2.0 / math.pi)
        with contextlib.ExitStack() as st:
            self.em.barrier()
            G = self.sb(st, [128, D], F32, "G")
            hb = [self.sb(st, [128, D], F32, "hb") for _ in range(2)]
            un = self.sb(st, [128, D], F32, "un")
            u = self.sb(st, [128, D], BF16, "u")
            uTt = [self.sb(st, [128, 8, 128], BF16, "uTt") for _ in range(2)]
            junk = self.sb(st, [128, D], BF16, "junk")
            ss = self.sb(st, [128, 1], F32, "ss")
            ms = self.sb(st, [128, 1], F32, "ms")
            sd = self.sb(st, [128, 1], F32, "sd")
            rstd = self.sb(st, [128, 1], F32, "rstd")
            tpp = [self.ps(st, [128, 1024], BF16, "tpp") for _ in range(2)]
            rvp = [self.ps(st, [128, 1024], F32, "rvp") for _ in range(2)]
            uTrt = [self.sb(st, [128, 8, 128], BF16, "uTrt") for _ in range(2)]
            em.dma("sp", G[:], w["norm_mix"][L].partition_broadcast(128), writes=["G"], slow=True)
            em.dma("sp", hb[0][:], self.out[0:128, :], reads=["out0"], writes=["hb0"])
            for i in range(NT):
                b = i % 2
                if i + 1 < NT:
                    em.dma("sp", hb[1 - b][:], self.out[(i + 1) * 128:(i + 2) * 128, :], reads=[f"out{i + 1}"], writes=[f"hb{1 - b}"])
                self.rms_rstd([hb[b][:]], junk, ss, ms, sd, rstd, [f"hb{b}"], "p")
                em.I("act", "activation", reads=[f"hb{b}", "prstd"], writes=["un"], out=un[:], in_=hb[b][:], func=AF.Copy, scale=rstd[:, 0:1])
                em.I("dve", "tensor_tensor", reads=["un", "G"], writes=["u"], out=u[:], in0=un[:], in1=G[:], op=ALU.mult)
                for kt in range(8):
                    em.I("pe", "transpose", reads=["u", "identb"], writes=[f"tpp{b}"], out=tpp[b][:, kt * 128:(kt + 1) * 128],
                         in_=u[:, kt * 128:(kt + 1) * 128], identity=self.ident_b[:])
                em.I("dve", "tensor_copy", reads=[f"tpp{b}"], writes=[f"uTt{b}"], out=uTt[b][:], in_=tpp[b][:].rearrange("p (k t) -> p k t", k=8))
                em.dma(STORE_Q, self.uT_s[:, :, i * 128:(i + 1) * 128].rearrange("k p t -> p k t"), uTt[b][:],
                       reads=[f"uTt{b}"], writes=[f"uTs{i}"])
                for kt in range(8):
                    em.I("pe", "matmul", reads=["u", "antib"], writes=[f"rvp{b}"], out=rvp[b][:, kt * 128:(kt + 1) * 128],
                         lhsT=u[:, kt * 128:(kt + 1) * 128], rhs=self.anti_b[:], start=True, stop=True)
                em.I("act", "activation", reads=[f"rvp{b}"], writes=[f"uTr{b}"], out=uTrt[b][:], in_=rvp[b][:].rearrange("p (k t) -> p k t", k=8), func=AF.Copy)
                ir = NT - 1 - i
                if i == NT - 1:
                    self.dump("uTrt", uTrt[b][:], [f"uTr{b}"])
                    self.dump("uTt", uTt[b][:], [f"uTt{b}"])
                em.dma(STORE_Q, self.uTr_s[:, :, ir * 128:(ir + 1) * 128].rearrange("k p t -> p k t"), uTrt[b][:],
                       reads=[f"uTr{b}"], writes=[f"uTrs{ir}"])
        TB = 64
        NB = S // TB
        with contextlib.ExitStack() as st:
            self.em.barrier()
            T = {}
            for nm in ("are", "aim", "ldt", "dt", "rho", "th", "mag", "kf", "z", "sw", "sw2", "cw", "sn", "cs", "ar", "ai",
                       "nr", "den", "t1", "t2", "qre", "qim"):
                T[nm] = self.sb(st, [128, 32], F32, nm)
            T["C1"] = self.sb(st, [128, 128], F32, "C1")
            T["C2"] = self.sb(st, [128, 128], F32, "C2")
            for nm in ("bre", "bim", "bbr", "bbi", "tb"):
                T[nm] = self.sb(st, [128, 32, 16], F32, nm)
            T["Bq"] = [self.sb(st, [128, 128], F32, "Bq") for _ in range(4)]
            T["WB"] = [self.sb(st, [128, 64, 128], BF16, "WB") for _ in range(2)]
            T["WC"] = [self.sb(st, [128, 64, 128], BF16, "WC") for _ in range(2)]
            T["cin"] = [self.sb(st, [128, 128], F32, "cin") for _ in range(2)]
            T["Cq"] = [self.sb(st, [128, 32, 16], F32, "Cq") for _ in range(2)]
            T["tps"] = [self.ps(st, [128, 512], F32, "tps") for _ in range(2)]
            for k in range(4):
                em.I("pool", "memset", writes=[f"Bqz{k}", f"Bq{k}"], ap=T["Bq"][k][:], constant=0.0)
            for r in range(2):
                em.I("pool", "memset", writes=["WCz", f"WC{r}_0", f"WC{r}_1"], ap=T["WC"][r][:], constant=0.0)
            for r in range(2):
                self.s5_prep_dir(st, j, r, T)
            C1, C2, WB, WC = T["C1"], T["C2"], T["WB"], T["WC"]
            BUH = [self.sb(st, [128, TB, 128], F32, "BUH") for _ in range(2)]
            HB = self.sb(st, [128, TB, 128], BF16, "HB")
            uTb = [[self.sb(st, [128, 8, TB], BF16, "uTb") for _ in range(2)] for _ in range(2)]
            P1 = self.sb(st, [128, 128], F32, "P1")
            P2 = self.sb(st, [128, 128], F32, "P2")
            carry = self.sb(st, [128, 128], F32, "carry")
            yt = [[self.sb(st, [128, 8, TB], F32, "yt") for _ in range(2)] for _ in range(2)]
            bup = [self.ps(st, [128, 8, TB], F32, "bup") for _ in range(2)]
            yp = [self.ps(st, [128, 512], F32, "yp") for _ in range(2)]
            yq = [self.ps(st, [128, 512], F32, "yq") for _ in range(2)]
            ytm = [self.sb(st, [TB, 128], F32, "ytm") for _ in range(2)]
            em.I("pool", "memset", writes=["carry"], ap=carry[:], constant=0.0)
            ysc = (self.yf_s, self.yb_s)

            def blk(r, n):
                return n if r == 0 else NB - 1 - n

            def emit_load(n):
                b = n % 2
                for r in range(2):
                    srcT = self.uT_s if r == 0 else self.uTr_s
                    kn = f"uTs{n * TB // 128}" if r == 0 else f"uTrs{n * TB // 128}"
                    em.dma("sp", uTb[r][b][:], srcT[:, :, n * TB:(n + 1) * TB].rearrange("k p t -> p k t"),
                           reads=[kn], writes=[f"uTb{r}{b}"], slow=True)

            def emit_bu(n):
                b = n % 2
                for r in range(2):
                    for c8 in range(8):
                        pb = c8 % 2
                        for cc in range(8):
                            col = c8 * 8 + cc
                            kt = (col % 32) // 4
                            em.I("pe", "matmul", reads=[f"WB{r}_{col}", f"uTb{r}{b}"], writes=[f"bup{pb}"],
                                 out=bup[pb][:, cc, :], lhsT=WB[r][:, col, :], rhs=uTb[r][b][:, kt, :], start=True, stop=True)
                        dst = BUH[b][:, :, r * 64 + c8 * 8:r * 64 + c8 * 8 + 8]
                        em.I("act", "activation", reads=[f"bup{pb}"], writes=[f"buh{b}"],
                             out=dst.rearrange("p t c -> p c t"), in_=bup[pb][:], func=AF.Copy)

            emit_load(0)
            emit_bu(0)
            for n in range(NB):
                b = n % 2
                if n + 1 < NB:
                    emit_load(n + 1)
                    emit_bu(n + 1)
                for t in range(TB):
                    if t == 0:
                        hp = carry[:]
                        rk = [f"buh{b}", "carry", "C1", "C2"]
                    else:
                        hp = BUH[b][:, t - 1, :]
                        rk = [f"buh{b}", "C1", "C2"]
                    hps = hp.rearrange("p (d two c) -> p d two c", d=2, two=2)[:, :, ::-1, :]
                    em.I("dve", "tensor_tensor", reads=rk, writes=["P1"], nosync=(t > 0), out=P1[:], in0=hp, in1=C1[:], op=ALU.mult)
                    em.I("dve", "tensor_tensor", reads=rk, writes=["P2"], nosync=True,
                         out=P2[:].rearrange("p (d two c) -> p d two c", d=2, two=2), in0=hps,
                         in1=C2[:].rearrange("p (d two c) -> p d two c", d=2, two=2), op=ALU.mult)
                    em.I("dve", "tensor_tensor", reads=["P1", "P2"], writes=["P1"], nosync=True, out=P1[:], in0=P1[:], in1=P2[:], op=ALU.add)
                    em.I("dve", "tensor_tensor", reads=["P1", f"buh{b}"], writes=[f"buh{b}"], nosync=True, out=BUH[b][:, t, :], in0=P1[:],
                         in1=BUH[b][:, t, :], op=ALU.add)
                em.I("dve", "tensor_copy", reads=[f"buh{b}"], writes=["carry"], out=carry[:], in_=BUH[b][:, TB - 1, :])
                if n == 0:
                    self.dump("BUH0", BUH[0][:], ["buh0"])
                    self.dump("BUH1pre", BUH[1][:], ["buh1"])
                    self.dump("uTb10", uTb[1][0][:], ["uTb10"])
                    self.dump("Wc0", WC[0][:], ["WC0_0", "WC0_1"])
                em.I("act", "activation", reads=[f"buh{b}"], writes=["HB"], out=HB[:], in_=BUH[b][:], func=AF.Copy)
                for r in range(2):
                    i = blk(r, n)
                    for kt in range(8):
                        pb = kt % 2
                        n_mm = 0
                        for ri in range(2):
                            for g4 in range(4):
                                col = ri * 32 + kt * 4 + g4
                                if r == 0:
                                    em.I("pe", "matmul", reads=[f"WC{r}_{ri}", "HB"], writes=[f"yp{pb}"], out=yp[pb][:, 0:TB],
                                         lhsT=WC[r][:, col, :], rhs=HB[:, :, col], start=(n_mm == 0), stop=(n_mm == 7))
                                else:
                                    em.I("pe", "matmul", reads=[f"WC{r}_{ri}", "HB"], writes=[f"yp{pb}"], out=yp[pb][0:TB, 0:128],
                                         lhsT=HB[:, :, 64 + col], rhs=WC[r][:, col, :], start=(n_mm == 0), stop=(n_mm == 7))
                                n_mm += 1
                        if r == 0:
                            em.I("act", "activation", reads=[f"yp{pb}"], writes=[f"yt{r}{b}"], out=yt[r][b][:, kt, :], in_=yp[pb][:, 0:TB], func=AF.Copy)
                        else:
                            em.I("act", "activation", reads=[f"yp{pb}"], writes=[f"ytm{pb}"], out=ytm[pb][:], in_=yp[pb][0:TB, 0:128], func=AF.Copy)
                            em.I("pe", "matmul", reads=[f"ytm{pb}", "antif"], writes=[f"yq{pb}"], out=yq[pb][:, 0:TB], lhsT=ytm[pb][:],
                                 rhs=self.anti_f[:], start=True, stop=True)
                            em.I("act", "activation", reads=[f"yq{pb}"], writes=[f"yt{r}{b}"], out=yt[r][b][:, kt, :], in_=yq[pb][:, 0:TB], func=AF.Copy)
                    if n == 0:
                        self.dump(f"yt{r}", yt[r][b][:], [f"yt{r}{b}"])
                    if n == NB - 1:
                        self.dump(f"ytL{r}", yt[r][b][:], [f"yt{r}{b}"])
                        if r == 1:
                            self.dump("BUHL", BUH[b][:], [f"buh{b}"])
                    em.dma(STORE_Q, ysc[r][:, :, i * TB:(i + 1) * TB].rearrange("k p t -> p k t"), yt[r][b][:],
                           reads=[f"yt{r}{b}"], writes=[f"ys{r}_{i}"], slow=True)
        with contextlib.ExitStack() as st:
            self.em.barrier()
            Wg = self.sb(st, [128, 8, 2048], BF16, "Wg")
            stgt = self.sb(st, [128, 2, 2048], F32, "stg")
            stg = [stgt[:, 0, :], stgt[:, 1, :]]
            uTt = [self.sb(st, [128, 8, 128], BF16, "uTt") for _ in range(2)]
            yft = [self.sb(st, [128, 8, 128], F32, "yft") for _ in range(2)]
            ybt = [self.sb(st, [128, 8, 128], F32, "ybt") for _ in range(2)]
            hb = [self.sb(st, [128, D], F32, "hb") for _ in range(2)]
            dT = self.sb(st, [128, 8], F32, "dT")
            yall = self.sb(st, [128, 8, 128], F32, "yall")
            g2 = self.sb(st, [128, 8, 128], F32, "g2")
            gT = self.sb(st, [128, 8, 128], BF16, "gT")
            sig = self.sb(st, [128, D], F32, "sig")
            mix = self.sb(st, [128, D], F32, "mix")
            glp = [self.ps(st, [128, 512], F32, "glp") for _ in range(4)]
            wg = w["ssm_w_glu"][j]
            em.dma("sp", dT[:], w["ssm_d"][j].rearrange("(kt p) -> p kt", p=128), writes=["dT"], slow=True)
            self.load_weight([Wg[:, i, :] for i in range(8)], [wg[i * 128:(i + 1) * 128, :] for i in range(8)], stg, None, "Wg")

            def load(i):
                b = i % 2
                sl = slice(i * 128, (i + 1) * 128)
                ykeys = [f"ys{r}_{k}" for r in range(2) for k in (2 * i, 2 * i + 1)]
                em.dma("sp", uTt[b][:], self.uT_s[:, :, sl].rearrange("k p t -> p k t"), reads=[f"uTs{i}"], writes=[f"uTt{b}"], slow=True)
                em.dma("sp", yft[b][:], self.yf_s[:, :, sl].rearrange("k p t -> p k t"), reads=ykeys, writes=[f"yft{b}"], slow=True)
                ix = em.dma("sp", ybt[b][:], self.yb_s[:, :, sl].rearrange("k p t -> p k t"), reads=ykeys, writes=[f"ybt{b}"], slow=True)
                if i == 0:
                    em.debug_ops = {"ybt_load0": ix, "yft_load0": ix - 1}
                em.dma("sp", hb[b][:], self.out[sl, :], reads=[f"out{i}"], writes=[f"hb{b}"])

            load(0)
            for i in range(NT):
                b = i % 2
                if i + 1 < NT:
                    load(i + 1)
                em.I("pool", "tensor_tensor", reads=[f"yft{b}", f"ybt{b}"], writes=["yall"], out=yall[:], in0=yft[b][:], in1=ybt[b][:], op=ALU.add)
                em.I("dve", "tensor_tensor", reads=[f"uTt{b}", "dT"], writes=["g2"], out=g2[:], in0=uTt[b][:],
                     in1=dT[:].unsqueeze(2).to_broadcast([128, 8, 128]), op=ALU.mult)
                em.I("dve", "tensor_tensor", reads=["g2", "yall"], writes=["yall"], out=yall[:], in0=yall[:], in1=g2[:], op=ALU.add)
                em.I("dve", "tensor_tensor", reads=["yall"], writes=["g2"], out=g2[:], in0=yall[:], in1=yall[:], op=ALU.mult)
                em.I("dve", "tensor_scalar", reads=["g2"], writes=["g2"], out=g2[:], in0=g2[:], scalar1=0.044715, scalar2=1.0, op0=ALU.mult, op1=ALU.add)
                em.I("dve", "tensor_tensor", reads=["g2", "yall"], writes=["g2"], out=g2[:], in0=g2[:], in1=yall[:], op=ALU.mult)
                em.I("act", "activation", reads=["g2"], writes=["g2"], out=g2[:], in_=g2[:], func=AF.Sigmoid, scale=GC)
                em.I("dve", "tensor_tensor", reads=["g2", "yall"], writes=["gT"], out=gT[:], in0=g2[:], in1=yall[:], op=ALU.mult)
                if i == 0:
                    self.dump("yall", yall[:], ["yall"])
                    self.dump("yft", yft[b][:], [f"yft{b}"])
                    self.dump("ybt", ybt[b][:], [f"ybt{b}"])
                for c in range(4):
                    for kt in range(8):
                        em.I("pe", "matmul", reads=[f"Wg{kt}", "gT"], writes=[f"glp{c}"], out=glp[c][:], lhsT=gT[:, kt, :],
                             rhs=Wg[:, kt, c * 512:(c + 1) * 512], start=(kt == 0), stop=(kt == 7))
                for c in range(2):
                    em.I("act", "activation", reads=[f"glp{2 + c}"], writes=[f"sig{c}"], out=sig[:, c * 512:(c + 1) * 512], in_=glp[2 + c][:], func=AF.Sigmoid)
                    em.I("dve", "tensor_tensor", reads=[f"glp{c}", f"sig{c}"], writes=[f"mix{c}"], out=mix[:, c * 512:(c + 1) * 512], in0=glp[c][:],
                         in1=sig[:, c * 512:(c + 1) * 512], op=ALU.mult)
                em.I("pool", "tensor_tensor", reads=["mix0", "mix1", f"hb{b}"], writes=[f"hb{b}"], out=hb[b][:], in0=hb[b][:], in1=mix[:], op=ALU.add)
                em.dma(STORE_Q, self.out[i * 128:(i + 1) * 128, :], hb[b][:], reads=[f"hb{b}"], writes=[f"out{i}"])
            if "ybsdump" in FLAGS:
                self.dump("ybs_dram", self.yb_s[:, :, :], [f"out{NT - 1}"])

def make_consts(S):
    ident = np.eye(128, dtype=np.float32)
    pos = np.arange(S, dtype=np.float32)
    inv_freq = (10000.0 ** (-np.arange(0, 64, 2, dtype=np.float32) / 64)).astype(np.float32)
    ang = pos[:, None] * inv_freq[None, :]
    rope = np.concatenate([np.cos(ang), np.sin(ang)], axis=1).astype(np.float32)
    anti = np.ascontiguousarray(ident[::-1])
    return {"ident_f": ident, "ident_b": ident.astype(ml_dtypes.bfloat16), "rope": rope,
            "anti_b": anti.astype(ml_dtypes.bfloat16), "anti_f": np.ascontiguousarray(np.eye(64, dtype=np.float32)[::-1])}


_CACHE = {}
DUMPS = None
FLAGS = set()
DBG = 0
STORE_Q = "pool"


def get_prog(S, phases):
    key = (S, tuple(phases))
    if key not in _CACHE:
        p = Prog(S, phases)
        p.dbg = DBG
        p.build()
        _CACHE[key] = p
    return _CACHE[key]


FULL_PHASES = (("attn", 0, "x"), ("ffn", 0), ("s5", 1), ("ffn", 1), ("attn", 2), ("ffn", 2), ("s5", 3), ("ffn", 3))


def run(inputs, phases, n_cores=8, trace=False):
    x = np.asarray(inputs["x"], dtype=np.float32)
    B, S, _ = x.shape
    prog = get_prog(S, phases)
    consts = make_consts(S)
    base = {n: np.ascontiguousarray(np.asarray(inputs[n], dtype=np.float32)) for n, _ in INPUT_SPECS}
    base.update(consts)
    in_maps = []
    for c in range(n_cores):
        m = dict(base)
        m["x"] = np.ascontiguousarray(x[c])
        in_maps.append(m)
    res = run_bass_kernel_spmd(prog.nc, in_maps, core_ids=list(range(n_cores)), trace=trace)
    outs = np.stack([np.asarray(r["out"]) for r in res.results], axis=0)
    return outs, res


def kernel(**inputs):
    outs, _ = run(inputs, FULL_PHASES, n_cores=8)
    return outs.astype(np.float32)
```

```python
import contextlib
import math
import numpy as np
import ml_dtypes
import concourse.bass as bass
import concourse.mybir as mybir
from concourse.bass_utils import run_bass_kernel_spmd

F32 = mybir.dt.float32
BF16 = mybir.dt.bfloat16
I32 = mybir.dt.int32
AF = mybir.ActivationFunctionType
ALU = mybir.AluOpType
AX = mybir.AxisListType

D = 1024
DFF = 4096
NH = 8
EPS = 1e-6
ENGS = ("pe", "act", "dve", "pool", "sp")
SEM_MAX = 30000
NDMA_SLOTS = 24
FENCE_DIST = 0
FENCE_REPS = 2


class Op:
    __slots__ = ("eng", "fn", "deps", "idx", "need_inc", "is_dma", "semid", "semval", "slot_prev")

    def __init__(self, eng, fn, deps, is_dma):
        self.eng = eng
        self.fn = fn
        self.deps = deps
        self.is_dma = is_dma
        self.need_inc = False
        self.semid = None
        self.semval = None
        self.slot_prev = None


class Emitter:
    def __init__(self, nc, same_engine_sync=True):
        self.nc = nc
        self.ops = []
        self.last_w = {}
        self.readers = {}
        self.same_engine_sync = same_engine_sync
        self.nosync = set()
        self.cur_barrier = None
        self.dma_since_barrier = []
        self.last_on_eng = {}
        self.bar_a = nc.dram_tensor("bar_a", [1, 16], F32).ap()
        self.bar_b = nc.dram_tensor("bar_b", [1, 16], F32).ap()

    def barrier(self):
        deps = set(self.last_on_eng.values()) | set(self.dma_since_barrier)
        if self.cur_barrier is not None:
            deps.add(self.cur_barrier)
        a, b = self.bar_a, self.bar_b
        idx = self.op("sp", lambda e: e.dma_start(out=b, in_=a), (), (), is_dma=True)
        self.ops[idx].deps |= deps
        self.cur_barrier = idx
        self.dma_since_barrier = []

    def I(self, eng, name, reads=(), writes=(), nosync=False, **kw):
        idx = self.op(eng, (name, kw), reads, writes)
        if nosync:
            self.nosync.add(idx)
        return idx

    def op(self, eng, fn, reads=(), writes=(), is_dma=False):
        deps = set()
        for k in reads:
            w = self.last_w.get(k)
            if w is not None:
                deps.add(w)
        for k in writes:
            w = self.last_w.get(k)
            if w is not None:
                deps.add(w)
            for r in self.readers.get(k, ()):
                deps.add(r)
        if self.cur_barrier is not None:
            deps.add(self.cur_barrier)
        idx = len(self.ops)
        o = Op(eng, fn, deps, is_dma)
        o.idx = idx
        self.ops.append(o)
        if is_dma:
            self.dma_since_barrier.append(idx)
        elif fn is not None:
            self.last_on_eng[eng] = idx
        for k in reads:
            self.readers.setdefault(k, []).append(idx)
        for k in writes:
            self.last_w[k] = idx
            self.readers[k] = []
        return idx

    def dma(self, eng, out, in_, reads=(), writes=(), slow=False):
        if slow:
            fn = lambda e: e.dma_start(out=out, in_=in_, allow_slow_non_contiguous=True)
        else:
            fn = lambda e: e.dma_start(out=out, in_=in_)
        return self.op(eng, fn, reads, writes, is_dma=True)

    def finalize(self):
        nc = self.nc
        ops = self.ops
        per_eng = {e: [] for e in ENGS}
        for o in ops:
            per_eng[o.eng].append(o)
        pos = {}
        for e in ENGS:
            for i, o in enumerate(per_eng[e]):
                pos[o.idx] = i
        waited = {e: {p: -1 for p in ENGS} for e in ENGS}
        waited_dma = {e: set() for e in ENGS}
        wait_lists = {}
        for o in ops:
            wl = {}
            dl = []
            for d in o.deps:
                po = ops[d]
                if po.is_dma:
                    if d not in waited_dma[o.eng]:
                        waited_dma[o.eng].add(d)
                        dl.append(d)
                    continue
                pe_ = po.eng
                if pe_ == o.eng and not o.is_dma:
                    if pe_ == "pe" or not self.same_engine_sync or o.idx in self.nosync:
                        continue
                if pos[d] <= waited[o.eng][pe_]:
                    continue
                if pe_ not in wl or pos[d] > pos[wl[pe_]]:
                    wl[pe_] = d
            for pe_, d in wl.items():
                waited[o.eng][pe_] = pos[d]
                ops[d].need_inc = True
            wait_lists[o.idx] = list(wl.values()) + dl
        n_sems = {}
        dma_cnt = {e: 0 for e in ENGS}
        dma_last = {}
        for e in ENGS:
            cnt = 0
            semi = 0
            for o in per_eng[e]:
                if o.is_dma:
                    k = dma_cnt[e]
                    dma_cnt[e] += 1
                    slot = k % NDMA_SLOTS
                    o.semid = ("dma", e, slot)
                    o.semval = 16 * (k // NDMA_SLOTS + 1)
                    o.slot_prev = dma_last.get((e, slot))
                    dma_last[(e, slot)] = o
                    o.need_inc = True
                elif o.need_inc:
                    if cnt >= SEM_MAX:
                        semi += 1
                        cnt = 0
                    cnt += 1
                    o.semid = (e, semi)
                    o.semval = cnt
            n_sems[e] = semi + 1
        self.n_inst = {e: len(per_eng[e]) for e in ENGS}
        fence_need = {}
        for o in ops:
            if o.is_dma:
                for d in wait_lists[o.idx]:
                    if ops[d].is_dma and (o.idx - d) < FENCE_DIST:
                        fence_need[o.idx] = True
        self.n_fence = len(fence_need)
        for nm, ix in getattr(self, "debug_ops", {}).items():
            o = ops[ix]
            print("DEBUGOP", nm, "idx", ix, "eng", o.eng, "deps", sorted(o.deps), "waits", [(d, ops[d].eng, ops[d].is_dma, ops[d].semid, ops[d].semval) for d in wait_lists[ix]], "fenced", ix in fence_need, flush=True)
        fa = nc.dram_tensor("fence_a", [1, 16], F32).ap()
        fb = {e: nc.dram_tensor(f"fence_b_{e}", [1, 16], F32).ap() for e in ENGS}
        fence_cnt = {e: 0 for e in ENGS}
        with contextlib.ExitStack() as st:
            sems = {}
            fsem = {e: st.enter_context(nc.semaphore(f"fence_{e}")) for e in ("sp", "pool")}
            for e in ENGS:
                for i in range(n_sems[e]):
                    sems[(e, i)] = st.enter_context(nc.semaphore(f"s_{e}_{i}"))
                for s in range(min(NDMA_SLOTS, dma_cnt[e])):
                    sems[("dma", e, s)] = st.enter_context(nc.semaphore(f"d_{e}_{s}"))
            block = st.enter_context(nc.Block())

            def run(e):
                def body(eng):
                    for o in per_eng[e]:
                        if o.is_dma and o.slot_prev is not None:
                            p = o.slot_prev
                            eng.wait_ge(sems[p.semid], p.semval)
                        for d in wait_lists[o.idx]:
                            po = ops[d]
                            eng.wait_ge(sems[po.semid], po.semval)
                        if o.idx in fence_need:
                            for _ in range(FENCE_REPS):
                                fence_cnt[e] += 1
                                eng.dma_start(out=fb[e], in_=fa).then_inc(fsem[e], 16)
                                eng.wait_ge(fsem[e], 16 * fence_cnt[e])
                        if o.fn is None:
                            continue
                        if isinstance(o.fn, tuple):
                            ins = getattr(eng, o.fn[0])(**o.fn[1])
                        else:
                            ins = o.fn(eng)
                        if o.need_inc:
                            ins.then_inc(sems[o.semid], 16 if o.is_dma else 1)
                return body

            block.tensor(run("pe"))
            block.scalar(run("act"))
            block.vector(run("dve"))
            block.gpsimd(run("pool"))
            block.sync(run("sp"))


INPUT_SPECS = [
    ("norm_mix", (4, 1024)), ("norm_ffn", (4, 1024)), ("attn_w_qkv", (2, 1024, 3072)),
    ("attn_q_gain", (2, 64)), ("attn_k_gain", (2, 64)), ("attn_lambda", (2, 4, 64)),
    ("attn_subln", (2, 128)), ("attn_w_o", (2, 1024, 1024)), ("ssm_a_re", (2, 2, 64, 64)),
    ("ssm_a_im", (2, 2, 64, 64)), ("ssm_log_dt", (2, 2, 64)), ("ssm_b_re", (2, 2, 64, 64, 16)),
    ("ssm_b_im", (2, 2, 64, 64, 16)), ("ssm_c_re", (2, 2, 64, 16, 64)), ("ssm_c_im", (2, 2, 64, 16, 64)),
    ("ssm_d", (2, 1024)), ("ssm_w_glu", (2, 1024, 2048)), ("ffn_w_up", (4, 1024, 4096)),
    ("ffn_w_down", (4, 4096, 1024)),
]


class Prog:
    def __init__(self, S, phases):
        self.S = S
        self.NT = S // 128
        self.phases = phases
        self.nc = bass.Bass("TRN2", target_bir_lowering=False)
        self.em = Emitter(self.nc)
        self.uid = 0
        self.dbg_names = []

    def dram_in(self, name, shape, dt=F32):
        return self.nc.dram_tensor(name, list(shape), dt, kind="ExternalInput").ap()

    def sb(self, st, shape, dt, name=None):
        self.uid += 1
        return st.enter_context(self.nc.sbuf_tensor(f"{name or 't'}_{self.uid}", list(shape), dt))

    def ps(self, st, shape, dt, name=None):
        self.uid += 1
        return st.enter_context(self.nc.psum_tensor(f"{name or 'p'}_{self.uid}", list(shape), dt))

    def build(self):
        nc, S = self.nc, self.S
        self.x = self.dram_in("x", (S, D))
        self.w = {n: self.dram_in(n, shp) for n, shp in INPUT_SPECS}
        self.ident_f_d = self.dram_in("ident_f", (128, 128))
        self.ident_b_d = self.dram_in("ident_b", (128, 128), BF16)
        self.rope_d = self.dram_in("rope", (S, 64))
        self.jb_d = self.dram_in("anti_b", (128, 128), BF16)
        self.jf_d = self.dram_in("anti_f", (64, 64))
        self.out = nc.dram_tensor("out", [S, D], F32, kind="ExternalOutput").ap()
        self.qT_s = nc.dram_tensor("qT_s", [NH, 128, S], BF16).ap()
        self.kT_s = nc.dram_tensor("kT_s", [NH, 128, S], BF16).ap()
        self.v_s = nc.dram_tensor("v_s", [S, D], BF16).ap()
        self.oT_s = nc.dram_tensor("oT_s", [NH, 128, S], BF16).ap()
        self.uT_s = nc.dram_tensor("uT_s", [8, 128, S], BF16).ap()
        self.yb_s = nc.dram_tensor("yb_s", [8, 128, S], F32).ap()
        self.yf_s = nc.dram_tensor("yf_s", [8, 128, S], F32).ap()
        self.uTr_s = nc.dram_tensor("uTr_s", [8, 128, S], BF16).ap()
        em = self.em
        with contextlib.ExitStack() as st:
            self.ident_f = self.sb(st, [128, 128], F32, "identf")
            self.ident_b = self.sb(st, [128, 128], BF16, "identb")
            em.dma("sp", self.ident_f[:], self.ident_f_d, writes=["identf"])
            em.dma("sp", self.ident_b[:], self.ident_b_d, writes=["identb"])
            self.anti_b = self.sb(st, [128, 128], BF16, "antib")
            self.anti_f = self.sb(st, [64, 64], F32, "antif")
            em.dma("sp", self.anti_b[:], self.jb_d, writes=["antib"])
            em.dma("sp", self.anti_f[:], self.jf_d, writes=["antif"])
            for ph in self.phases:
                kind = ph[0]
                if kind == "copy":
                    self.phase_copy()
                elif kind == "ffn":
                    self.phase_ffn(ph[1], ph[2] if len(ph) > 2 else "out")
                elif kind == "attn":
                    self.phase_attn(ph[1], ph[2] if len(ph) > 2 else "out")
                elif kind == "s5":
                    self.phase_s5(ph[1])
            em.op("sp", None, reads=[f"out{i}" for i in range(self.NT)] + ["dbgout_" + n for n in self.dbg_names])
            em.finalize()
        return nc

    def dump(self, name, ap, keys):
        if not getattr(self, "dbg", 0):
            return
        if DUMPS is not None and name not in DUMPS:
            return
        shp = list(ap.shape)
        d = self.nc.dram_tensor("dbg_" + name, shp, ap.dtype, kind="ExternalOutput").ap()
        self.em.dma("sp", d, ap, reads=keys, writes=["dbgout_" + name])
        self.dbg_names.append(name)

    def src_ap(self, src):
        return self.x if src == "x" else self.out

    def phase_copy(self):
        em = self.em
        for i in range(self.NT):
            em.dma("sp", self.out[i * 128:(i + 1) * 128, :], self.x[i * 128:(i + 1) * 128, :],
                   writes=[f"out{i}"])


    def rms_rstd(self, hb_aps, junk, ss, ms, sd, rstd, keys_in, kp, dim=1024):
        em = self.em
        n = len(hb_aps)
        for a, hap in enumerate(hb_aps):
            em.I("act", "activation", reads=keys_in, writes=[kp + "junk", kp + "ss"],
                 out=junk[:], in_=hap, func=AF.Square, accum_out=ss[:, a:a + 1])
        em.I("dve", "tensor_scalar", reads=[kp + "ss"], writes=[kp + "ms"],
             out=ms[:, 0:n], in0=ss[:, 0:n], scalar1=1.0 / dim, scalar2=EPS, op0=ALU.mult, op1=ALU.add)
        em.I("act", "activation", reads=[kp + "ms"], writes=[kp + "sd"], out=sd[:, 0:n], in_=ms[:, 0:n], func=AF.Sqrt)
        em.I("dve", "reciprocal", reads=[kp + "sd"], writes=[kp + "rstd"], out=rstd[:, 0:n], in_=sd[:, 0:n])

    def load_weight(self, dsts, srcs, stg, gains, key_dst, engs=("act", "pool", "dve")):
        em = self.em
        for i, (dst, sa) in enumerate(zip(dsts, srcs)):
            b = i % 2
            em.dma("sp", stg[b], sa, writes=[f"stg{b}"])
            eng = engs[i % len(engs)]
            g = gains[i] if gains is not None else None
            rk = [f"stg{b}"] + (["gain"] if g is not None else [])
            wk = [f"{key_dst}{i}"]
            if eng == "act":
                if g is not None:
                    em.I("act", "activation", reads=rk, writes=wk, out=dst, in_=stg[b], func=AF.Copy, scale=g)
                else:
                    em.I("act", "activation", reads=rk, writes=wk, out=dst, in_=stg[b], func=AF.Copy)
            else:
                if g is not None:
                    em.I(eng, "tensor_scalar", reads=rk, writes=wk, out=dst, in0=stg[b], scalar1=g, scalar2=None, op0=ALU.mult)
                else:
                    em.I(eng, "tensor_copy", reads=rk, writes=wk, out=dst, in_=stg[b])

    def phase_ffn(self, L, src="out"):
        em, nc, S = self.em, self.nc, self.S
        ST = 256
        NST = S // ST
        src_t = self.src_ap(src)
        with contextlib.ExitStack() as st:
            self.em.barrier()
            Wup = self.sb(st, [128, 8, DFF], BF16, "wup")
            Wdn = self.sb(st, [128, 32, D], BF16, "wdn")
            gain = self.sb(st, [128, 8], F32, "gain")
            stgt = self.sb(st, [128, 2, 2048], F32, "stg")
            hb = [self.sb(st, [128, 2, D], F32, "hb") for _ in range(2)]
            u = self.sb(st, [128, 2, D], BF16, "u")
            uT = self.sb(st, [128, 8, ST], BF16, "uT")
            hid = self.sb(st, [128, 32, ST], BF16, "hid")
            r = [self.sb(st, [128, ST], BF16, "r") for _ in range(2)]
            junk = self.sb(st, [128, D], BF16, "junk")
            ss = self.sb(st, [128, 2], F32, "ss")
            ms = self.sb(st, [128, 2], F32, "ms")
            sd = self.sb(st, [128, 2], F32, "sd")
            rstd = self.sb(st, [128, 2], F32, "rstd")
            tpp = [self.ps(st, [128, 1024], BF16, "tpp") for _ in range(2)]
            upp = [self.ps(st, [128, 512], F32, "upp") for _ in range(2)]
            dnp = [self.ps(st, [128, 512], F32, "dnp") for _ in range(2)]
            stg = [stgt[:, 0, :], stgt[:, 1, :]]
            em.dma("sp", gain[:], self.w["norm_ffn"][L].rearrange("(kt p) -> p kt", p=128), writes=["gain"], slow=True)
            wu = self.w["ffn_w_up"][L]
            wd = self.w["ffn_w_down"][L]
            self.load_weight([Wup[:, i // 2, (i % 2) * 2048:(i % 2 + 1) * 2048] for i in range(16)],
                             [wu[(i // 2) * 128:(i // 2 + 1) * 128, (i % 2) * 2048:(i % 2 + 1) * 2048] for i in range(16)],
                             stg, [gain[:, i // 2:i // 2 + 1] for i in range(16)], "Wup")
            stg3 = [stgt[:, 0, :].rearrange("p (a d) -> p a d", a=2), stgt[:, 1, :].rearrange("p (a d) -> p a d", a=2)]
            self.load_weight([Wdn[:, 2 * i:2 * i + 2, :] for i in range(16)],
                             [wd[i * 256:(i + 1) * 256, :].rearrange("(a p) d -> p a d", p=128) for i in range(16)],
                             stg3, None, "Wdn")

            def load(sti):
                b = sti % 2
                em.dma("sp", hb[b][:], src_t[sti * ST:(sti + 1) * ST, :].rearrange("(a p) d -> p a d", p=128),
                       reads=[f"out{2 * sti}", f"out{2 * sti + 1}"], writes=[f"hb{b}"])

            load(0)
            for sti in range(NST):
                b = sti % 2
                if sti + 1 < NST:
                    load(sti + 1)
                self.rms_rstd([hb[b][:, a, :] for a in range(2)], junk, ss, ms, sd, rstd, [f"hb{b}"], "f")
                for a in range(2):
                    em.I("act", "activation", reads=[f"hb{b}", "frstd"], writes=[f"u{a}"],
                         out=u[:, a, :], in_=hb[b][:, a, :], func=AF.Copy, scale=rstd[:, a:a + 1])
                for a in range(2):
                    tp = tpp[a]
                    for kt in range(8):
                        em.I("pe", "transpose", reads=[f"u{a}", "identb"], writes=[f"tpp{a}"],
                             out=tp[:, kt * 128:(kt + 1) * 128], in_=u[:, a, kt * 128:(kt + 1) * 128], identity=self.ident_b[:])
                    em.I("dve", "tensor_copy", reads=[f"tpp{a}"], writes=["uT"],
                         out=uT[:, :, a * 128:(a + 1) * 128], in_=tp[:].rearrange("p (k t) -> p k t", k=8))
                for ft in range(32):
                    pb = ft % 2
                    for kt in range(8):
                        em.I("pe", "matmul", reads=[f"Wup{kt * 2 + ft // 16}", "uT"], writes=[f"upp{pb}"],
                             out=upp[pb][:, 0:ST], lhsT=Wup[:, kt, ft * 128:(ft + 1) * 128], rhs=uT[:, kt, :],
                             start=(kt == 0), stop=(kt == 7))
                    em.I("act", "activation", reads=[f"upp{pb}"], writes=[f"r{pb}"], out=r[pb][:], in_=upp[pb][:, 0:ST], func=AF.Relu)
                    em.I("dve", "tensor_tensor", reads=[f"r{pb}"], writes=[f"hid{ft}"],
                         out=hid[:, ft, :], in0=r[pb][:], in1=r[pb][:], op=ALU.mult)
                for a in range(2):
                    for nch in range(2):
                        pb = nch
                        for ft in range(32):
                            em.I("pe", "matmul", reads=[f"Wdn{ft // 2}", f"hid{ft}"], writes=[f"dnp{pb}"],
                                 out=dnp[pb][:], lhsT=hid[:, ft, a * 128:(a + 1) * 128],
                                 rhs=Wdn[:, ft, nch * 512:(nch + 1) * 512], start=(ft == 0), stop=(ft == 31))
                        em.I("dve", "tensor_tensor", reads=[f"dnp{pb}", f"hb{b}"], writes=[f"hb{b}"],
                             out=hb[b][:, a, nch * 512:(nch + 1) * 512], in0=dnp[pb][:],
                             in1=hb[b][:, a, nch * 512:(nch + 1) * 512], op=ALU.add)
                em.dma(STORE_Q, self.out[sti * ST:(sti + 1) * ST, :].rearrange("(a p) d -> p a d", p=128), hb[b][:],
                       reads=[f"hb{b}"], writes=[f"out{2 * sti}", f"out{2 * sti + 1}"])


    def phase_attn(self, L, src="out"):
        em, nc, S, w, NT = self.em, self.nc, self.S, self.w, self.NT
        j = L // 2
        lam_init = 0.8 - 0.6 * math.exp(-0.3 * L)
        src_t = self.src_ap(src)
        with contextlib.ExitStack() as st:
            self.em.barrier()
            Wq = self.sb(st, [128, 8, 3072], BF16, "Wq")
            gain = self.sb(st, [128, 8], F32, "gain")
            stgt = self.sb(st, [128, 2, 1536], F32, "stg")
            stg = [stgt[:, 0, :], stgt[:, 1, :]]
            hb = [self.sb(st, [128, D], F32, "hb") for _ in range(2)]
            rp = [self.sb(st, [128, 64], F32, "rp") for _ in range(2)]
            gq = self.sb(st, [128, 2, 64], F32, "gq")
            u_ = [self.sb(st, [128, D], BF16, "u") for _ in range(2)]
            uT_ = [self.sb(st, [128, 8, 128], BF16, "uT") for _ in range(2)]
            qk_ = [self.sb(st, [128, 2048], F32, "qk") for _ in range(2)]
            sq_ = [self.sb(st, [128, 2048], F32, "sq") for _ in range(2)]
            qr_ = [self.sb(st, [128, 2048], BF16, "qr") for _ in range(2)]
            vt = [self.sb(st, [128, D], BF16, "vt") for _ in range(2)]
            qkT = [self.sb(st, [128, 16, 128], BF16, "qkT") for _ in range(2)]
            t1_ = [self.sb(st, [128, 32, 32], F32, "t1") for _ in range(2)]
            t2_ = [self.sb(st, [128, 32, 32], F32, "t2") for _ in range(2)]
            t3_ = [self.sb(st, [128, 32, 32], F32, "t3") for _ in range(2)]
            t4_ = [self.sb(st, [128, 32, 32], F32, "t4") for _ in range(2)]
            junk = self.sb(st, [128, D], BF16, "junk")
            ss = self.sb(st, [128, 1], F32, "ss")
            ms = self.sb(st, [128, 1], F32, "ms")
            sd = self.sb(st, [128, 1], F32, "sd")
            rstd = self.sb(st, [128, 1], F32, "rstd")
            m32_ = [self.sb(st, [128, 32], F32, "m32") for _ in range(2)]
            s32_ = [self.sb(st, [128, 32], F32, "s32") for _ in range(2)]
            r32_ = [self.sb(st, [128, 32], F32, "r32") for _ in range(2)]
            tpp_ = [self.ps(st, [128, 1024], BF16, "tpp") for _ in range(2)]
            mp = [self.ps(st, [128, 512], F32, "mp") for _ in range(2)]
            tq = self.ps(st, [128, 2048], BF16, "tq")
            em.dma("sp", gain[:], w["norm_mix"][L].rearrange("(kt p) -> p kt", p=128), writes=["gain"], slow=True)
            em.dma("sp", gq[:, 0, :], w["attn_q_gain"][j].partition_broadcast(128), writes=["gq0"], slow=True)
            em.dma("sp", gq[:, 1, :], w["attn_k_gain"][j].partition_broadcast(128), writes=["gq1"], slow=True)
            wq = w["attn_w_qkv"][j]
            self.load_weight([Wq[:, i // 2, (i % 2) * 1536:(i % 2 + 1) * 1536] for i in range(16)],
                             [wq[(i // 2) * 128:(i // 2 + 1) * 128, (i % 2) * 1536:(i % 2 + 1) * 1536] for i in range(16)],
                             stg, [gain[:, i // 2:i // 2 + 1] for i in range(16)], "Wq")

            def load(i):
                b = i % 2
                em.dma("sp", hb[b][:], src_t[i * 128:(i + 1) * 128, :], reads=[f"out{i}"] if src == "out" else [], writes=[f"hb{b}"])
                em.dma("sp", rp[b][:], self.rope_d[i * 128:(i + 1) * 128, :], writes=[f"rp{b}"])

            load(0)
            for i in range(NT):
                b = i % 2
                if i + 1 < NT:
                    load(i + 1)
                u, uT, qk, sq, qr, tpp = u_[b], uT_[b], qk_[b], sq_[b], qr_[b], tpp_[b]
                t1, t2, t3, t4, m32, s32, r32 = t1_[b], t2_[b], t3_[b], t4_[b], m32_[b], s32_[b], r32_[b]
                self.rms_rstd([hb[b][:]], junk, ss, ms, sd, rstd, [f"hb{b}"], "a")
                em.I("act", "activation", reads=[f"hb{b}", "arstd"], writes=[f"u{b}"], out=u[:], in_=hb[b][:], func=AF.Copy, scale=rstd[:, 0:1])
                for kt in range(8):
                    em.I("pe", "transpose", reads=[f"u{b}", "identb"], writes=[f"tpp{b}"], out=tpp[:, kt * 128:(kt + 1) * 128],
                         in_=u[:, kt * 128:(kt + 1) * 128], identity=self.ident_b[:])
                em.I("dve", "tensor_copy", reads=[f"tpp{b}"], writes=[f"uT{b}"], out=uT[:], in_=tpp[:].rearrange("p (k t) -> p k t", k=8))
                for c in range(6):
                    pb = c % 2
                    for kt in range(8):
                        em.I("pe", "matmul", reads=[f"Wq{kt * 2 + c // 3}", f"uT{b}"], writes=[f"mp{pb}"], out=mp[pb][:], lhsT=uT[:, kt, :],
                             rhs=Wq[:, kt, c * 512:(c + 1) * 512], start=(kt == 0), stop=(kt == 7))
                    if c < 4:
                        em.I("act", "activation", reads=[f"mp{pb}"], writes=[f"qk{b}_{c}"], out=qk[:, c * 512:(c + 1) * 512], in_=mp[pb][:], func=AF.Copy)
                    else:
                        em.I("act", "activation", reads=[f"mp{pb}"], writes=[f"vt{b}"], out=vt[b][:, (c - 4) * 512:(c - 3) * 512], in_=mp[pb][:], func=AF.Copy)
                em.dma(STORE_Q, self.v_s[i * 128:(i + 1) * 128, :], vt[b][:], reads=[f"vt{b}"], writes=[f"vs{i}"])
                qkk = [f"qk{b}_{c}" for c in range(4)]
                em.I("act", "activation", reads=qkk, writes=[f"sq{b}"], out=sq[:], in_=qk[:], func=AF.Square)
                em.I("dve", "tensor_reduce", reads=[f"sq{b}"], writes=[f"m32{b}"], out=m32[:], in_=sq[:].rearrange("p (g d) -> p g d", d=64), axis=AX.X, op=ALU.add)
                em.I("dve", "tensor_scalar", reads=[f"m32{b}"], writes=[f"m32{b}"], out=m32[:], in0=m32[:], scalar1=1.0 / 64, scalar2=EPS, op0=ALU.mult, op1=ALU.add)
                em.I("act", "activation", reads=[f"m32{b}"], writes=[f"s32{b}"], out=s32[:], in_=m32[:], func=AF.Sqrt)
                em.I("dve", "reciprocal", reads=[f"s32{b}"], writes=[f"r32{b}"], out=r32[:], in_=s32[:])
                qk3 = qk[:].rearrange("p (g d) -> p g d", d=64)
                em.I("dve", "tensor_tensor", reads=qkk + [f"r32{b}"], writes=[f"qkn{b}"], out=qk3, in0=qk3, in1=r32[:].unsqueeze(2).to_broadcast([128, 32, 64]), op=ALU.mult)
                qk4 = qk[:].rearrange("p (a g d) -> p a g d", a=2, d=64)
                em.I("dve", "tensor_tensor", reads=[f"qkn{b}", "gq0", "gq1"], writes=[f"qkn{b}"], out=qk4, in0=qk4, in1=gq[:].unsqueeze(2).to_broadcast([128, 2, 16, 64]), op=ALU.mult)
                x1 = qk3[:, :, 0:32]
                x2 = qk3[:, :, 32:64]
                cosb = rp[b][:, 0:32].unsqueeze(1).to_broadcast([128, 32, 32])
                sinb = rp[b][:, 32:64].unsqueeze(1).to_broadcast([128, 32, 32])
                qr3 = qr[:].rearrange("p (g d) -> p g d", d=64)
                em.I("dve", "tensor_tensor", reads=[f"qkn{b}", f"rp{b}"], writes=[f"t1{b}"], out=t1[:], in0=x1, in1=cosb, op=ALU.mult)
                em.I("dve", "tensor_tensor", reads=[f"qkn{b}", f"rp{b}"], writes=[f"t2{b}"], out=t2[:], in0=x2, in1=sinb, op=ALU.mult)
                em.I("dve", "tensor_tensor", reads=[f"t1{b}", f"t2{b}"], writes=[f"qr1{b}"], out=qr3[:, :, 0:32], in0=t1[:], in1=t2[:], op=ALU.subtract)
                em.I("pool", "tensor_tensor", reads=[f"qkn{b}", f"rp{b}"], writes=[f"t3{b}"], out=t3[:], in0=x2, in1=cosb, op=ALU.mult)
                em.I("dve", "tensor_tensor", reads=[f"qkn{b}", f"rp{b}"], writes=[f"t4{b}"], out=t4[:], in0=x1, in1=sinb, op=ALU.mult)
                em.I("dve", "tensor_tensor", reads=[f"t3{b}", f"t4{b}"], writes=[f"qr2{b}"], out=qr3[:, :, 32:64], in0=t3[:], in1=t4[:], op=ALU.add)
                for jj in range(16):
                    em.I("pe", "transpose", reads=[f"qr1{b}", f"qr2{b}", "identb"], writes=["tq"], out=tq[:, jj * 128:(jj + 1) * 128],
                         in_=qr[:, jj * 128:(jj + 1) * 128], identity=self.ident_b[:])
                em.I("act", "activation", reads=["tq"], writes=[f"qkT{b}"], out=qkT[b][:], in_=tq[:].rearrange("p (k t) -> p k t", k=16), func=AF.Copy)
                em.dma(STORE_Q, self.qT_s[:, :, i * 128:(i + 1) * 128].rearrange("h p t -> p h t"), qkT[b][:, 0:8, :],
                       reads=[f"qkT{b}"], writes=[f"qTs{i}"])
                em.dma(STORE_Q, self.kT_s[:, :, i * 128:(i + 1) * 128].rearrange("h p t -> p h t"), qkT[b][:, 8:16, :],
                       reads=[f"qkT{b}"], writes=[f"kTs{i}"])
        QC = 512
        NQC = S // QC
        with contextlib.ExitStack() as st:
            self.em.barrier()
            kTh = [self.sb(st, [128, S], BF16, "kTh") for _ in range(2)]
            qTh = [self.sb(st, [128, S], BF16, "qTh") for _ in range(2)]
            vh = [self.sb(st, [128, NT, 128], BF16, "vh") for _ in range(2)]
            pT = [self.sb(st, [128, 1024], BF16, "pT") for _ in range(3)]
            onesb = self.sb(st, [128, 128], BF16, "onesb")
            onesf = self.sb(st, [128, 128], F32, "onesf")
            lv = self.sb(st, [128, 4, 64], F32, "lv")
            lp = self.sb(st, [128, 64], F32, "lp")
            lsm = self.sb(st, [128, 2], F32, "lsm")
            lex = self.sb(st, [128, 2], F32, "lex")
            neglam = self.sb(st, [128, 1], F32, "neglam")
            gsc = self.sb(st, [128, 1], F32, "gsc")
            R = self.sb(st, [128, 1024], F32, "R")
            o0 = self.sb(st, [128, QC], F32, "o0")
            o1 = self.sb(st, [128, QC], F32, "o1")
            od = self.sb(st, [128, QC], F32, "od")
            osq = self.sb(st, [128, QC], F32, "osq")
            rs = self.sb(st, [128, QC], F32, "rs")
            oTt = [self.sb(st, [128, QC], BF16, "oTt") for _ in range(2)]
            sc = [self.ps(st, [128, 2, 512], F32, "sc") for _ in range(2)]
            O = [self.ps(st, [128, 512], F32, "O") for _ in range(2)]
            Lp = [self.ps(st, [128, 512], F32, "Lp") for _ in range(2)]
            acc = [self.sb(st, [128, 512], F32, "acc") for _ in range(2)]
            lsum = self.sb(st, [128, 512], F32, "lsum")
            em.I("pool", "memset", writes=["onesb"], ap=onesb[:], constant=1.0)
            em.I("pool", "memset", writes=["onesf"], ap=onesf[:], constant=1.0 / 128)
            onesf1 = self.sb(st, [128, 128], F32, "onesf1")
            em.I("pool", "memset", writes=["onesf1"], ap=onesf1[:], constant=1.0)
            em.dma("sp", lv[:], w["attn_lambda"][j].partition_broadcast(128), writes=["lv"], slow=True)
            em.dma("sp", gsc[:], w["attn_subln"][j].rearrange("(p o) -> p o", o=1), writes=["gsc"], slow=True)
            for k in range(2):
                em.I("dve", "tensor_tensor", reads=["lv"], writes=["lp"], out=lp[:], in0=lv[:, 2 * k, :], in1=lv[:, 2 * k + 1, :], op=ALU.mult)
                em.I("dve", "tensor_reduce", reads=["lp"], writes=["lsm"], out=lsm[:, k:k + 1], in_=lp[:], axis=AX.X, op=ALU.add)
            em.I("act", "activation", reads=["lsm"], writes=["lex"], out=lex[:], in_=lsm[:], func=AF.Exp)
            em.I("dve", "tensor_tensor", reads=["lex"], writes=["neglam"], out=neglam[:], in0=lex[:, 1:2], in1=lex[:, 0:1], op=ALU.subtract)
            em.I("dve", "tensor_scalar", reads=["neglam"], writes=["neglam"], out=neglam[:], in0=neglam[:], scalar1=-lam_init, scalar2=None, op0=ALU.add)
            em.I("dve", "tensor_scalar", reads=["gsc"], writes=["gsc"], out=gsc[:], in0=gsc[:], scalar1=(1.0 - lam_init), scalar2=None, op0=ALU.mult)

            def loadh(h):
                hb_ = h % 2
                em.dma("sp", kTh[hb_][:], self.kT_s[h], reads=[f"kTs{i}" for i in range(NT)], writes=[f"kTh{hb_}"])
                em.dma("sp", qTh[hb_][:], self.qT_s[h], reads=[f"qTs{i}" for i in range(NT)], writes=[f"qTh{hb_}"])
                em.dma("sp", vh[hb_][:], self.v_s[:, h * 128:(h + 1) * 128].rearrange("(kt p) e -> p kt e", p=128),
                       reads=[f"vs{i}" for i in range(NT)], writes=[f"vh{hb_}"], slow=True)

            loadh(0)
            it = 0
            for h in range(NH):
                hb_ = h % 2
                if h + 1 < NH:
                    loadh(h + 1)
                for qc in range(NQC):
                    qs = slice(qc * QC, (qc + 1) * QC)

                    def emit_S(kt, itn):
                        for c in range(2):
                            em.I("pe", "matmul", reads=[f"kTh{hb_}", f"qTh{hb_}"], writes=[f"sc{itn % 2}"], out=sc[itn % 2][:, c, :],
                                 lhsT=kTh[hb_][64 * c:64 * c + 64, kt * 128:(kt + 1) * 128], rhs=qTh[hb_][64 * c:64 * c + 64, qs],
                                 start=True, stop=True)

                    emit_S(0, it)
                    for kt in range(NT):
                        if kt + 1 < NT:
                            emit_S(kt + 1, it + 1)
                        pi = it % 3
                        em.I("act", "activation", reads=[f"sc{it % 2}"], writes=[f"pT{pi}"], out=pT[pi][:],
                             in_=sc[it % 2][:].rearrange("p c q -> p (c q)"), func=AF.Exp, scale=0.125)
                        for c in range(2):
                            em.I("pe", "matmul", reads=[f"vh{hb_}", f"pT{pi}"], writes=[f"O{c}"], out=O[c][:], lhsT=vh[hb_][:, kt, :],
                                 rhs=pT[pi][:, c * 512:(c + 1) * 512], start=(kt == 0), stop=(kt == NT - 1))
                        em.I("pe", "matmul", reads=["onesb", f"pT{pi}"], writes=["L0"], out=Lp[0][:], lhsT=onesb[:],
                             rhs=pT[pi][:, 0:512], start=(kt == 0), stop=(kt == NT - 1))
                        par = kt % 2
                        if kt < 2:
                            em.I("dve", "tensor_copy", reads=[f"pT{pi}"], writes=[f"acc{par}"], out=acc[par][:], in_=pT[pi][:, 512:1024])
                        else:
                            em.I("dve", "tensor_tensor", reads=[f"pT{pi}", f"acc{par}"], writes=[f"acc{par}"], nosync=True,
                                 out=acc[par][:], in0=acc[par][:], in1=pT[pi][:, 512:1024], op=ALU.add)
                        it += 1
                    if NT >= 2:
                        em.I("dve", "tensor_tensor", reads=["acc0", "acc1"], writes=["lsum"], out=lsum[:], in0=acc[0][:], in1=acc[1][:], op=ALU.add)
                    else:
                        em.I("dve", "tensor_copy", reads=["acc0"], writes=["lsum"], out=lsum[:], in_=acc[0][:])
                    em.I("pe", "matmul", reads=["onesf1", "lsum"], writes=["L1"], out=Lp[1][:], lhsT=onesf1[:], rhs=lsum[:], start=True, stop=True)
                    for c in range(2):
                        em.I("dve", "reciprocal", reads=[f"L{c}"], writes=[f"R{c}"], out=R[:, c * 512:(c + 1) * 512], in_=Lp[c][:])
                    em.I("dve", "tensor_tensor", reads=["O0", "R0"], writes=["o0"], out=o0[:], in0=O[0][:], in1=R[:, 0:512], op=ALU.mult)
                    em.I("dve", "tensor_tensor", reads=["O1", "R1"], writes=["o1"], out=o1[:], in0=O[1][:], in1=R[:, 512:1024], op=ALU.mult)
                    em.I("dve", "scalar_tensor_tensor", reads=["o0", "o1", "neglam"], writes=["od"], out=od[:], in0=o1[:], scalar=neglam[:, 0:1],
                         in1=o0[:], op0=ALU.mult, op1=ALU.add)
                    em.I("pool", "tensor_tensor", reads=["od"], writes=["osq"], out=osq[:], in0=od[:], in1=od[:], op=ALU.mult)
                    em.I("pe", "matmul", reads=["onesf", "osq", "R0"], writes=["L0"], out=Lp[0][:], lhsT=onesf[:], rhs=osq[:], start=True, stop=True)
                    em.I("dve", "tensor_scalar", reads=["L0"], writes=["rs"], out=rs[:], in0=Lp[0][:], scalar1=EPS, scalar2=None, op0=ALU.add)
                    em.I("act", "activation", reads=["rs"], writes=["rs"], out=rs[:], in_=rs[:], func=AF.Ln)
                    em.I("act", "activation", reads=["rs"], writes=["rs"], out=rs[:], in_=rs[:], func=AF.Exp, scale=-0.5)
                    ob = (h * NQC + qc) % 2
                    em.I("dve", "scalar_tensor_tensor", reads=["od", "gsc", "rs"], writes=[f"oTt{ob}"], out=oTt[ob][:], in0=od[:], scalar=gsc[:, 0:1],
                         in1=rs[:], op0=ALU.mult, op1=ALU.mult)
                    em.dma(STORE_Q, self.oT_s[h, :, qs], oTt[ob][:], reads=[f"oTt{ob}"], writes=[f"oTs{h}_{qc}"])
        with contextlib.ExitStack() as st:
            self.em.barrier()
            Wo = self.sb(st, [128, 8, D], BF16, "Wo")
            stgt = self.sb(st, [128, 2, 2048], F32, "stg")
            stg3 = [stgt[:, 0, :].rearrange("p (a d) -> p a d", a=2), stgt[:, 1, :].rearrange("p (a d) -> p a d", a=2)]
            hb = [self.sb(st, [128, D], F32, "hb") for _ in range(2)]
            oTi = [self.sb(st, [128, 8, 128], BF16, "oTi") for _ in range(2)]
            wop = [self.ps(st, [128, 512], F32, "wop") for _ in range(2)]
            wo = w["attn_w_o"][j]
            self.load_weight([Wo[:, 2 * i:2 * i + 2, :] for i in range(4)],
                             [wo[i * 256:(i + 1) * 256, :].rearrange("(a p) d -> p a d", p=128) for i in range(4)], stg3, None, "Wo")

            def load3(i):
                b = i % 2
                em.dma("sp", hb[b][:], src_t[i * 128:(i + 1) * 128, :], reads=[f"out{i}"] if src == "out" else [], writes=[f"hb{b}"])
                qcs = (i * 128) // QC
                em.dma("sp", oTi[b][:], self.oT_s[:, :, i * 128:(i + 1) * 128].rearrange("h p t -> p h t"),
                       reads=[f"oTs{h}_{qcs}" for h in range(NH)], writes=[f"oTi{b}"])

            load3(0)
            for i in range(NT):
                b = i % 2
                if i + 1 < NT:
                    load3(i + 1)
                for nch in range(2):
                    for h in range(NH):
                        em.I("pe", "matmul", reads=[f"Wo{h // 2}", f"oTi{b}"], writes=[f"wop{nch}"], out=wop[nch][:], lhsT=oTi[b][:, h, :],
                             rhs=Wo[:, h, nch * 512:(nch + 1) * 512], start=(h == 0), stop=(h == NH - 1))
                    em.I("dve", "tensor_tensor", reads=[f"wop{nch}", f"hb{b}"], writes=[f"hb{b}"], out=hb[b][:, nch * 512:(nch + 1) * 512], in0=wop[nch][:],
                         in1=hb[b][:, nch * 512:(nch + 1) * 512], op=ALU.add)
                em.dma(STORE_Q, self.out[i * 128:(i + 1) * 128, :], hb[b][:], reads=[f"hb{b}"], writes=[f"out{i}"])

    def s5_prep_dir(self, st, j, r, T):
        em, nc, w = self.em, self.nc, self.w
        kp = "s5p_"
        are, aim, ldt = T["are"], T["aim"], T["ldt"]
        em.dma("sp", are[:], w["ssm_a_re"][j, r].rearrange("(gp gpar) p -> (gpar p) gp", gpar=2), writes=[kp + "are"], slow=True)
        em.dma("sp", aim[:], w["ssm_a_im"][j, r].rearrange("(gp gpar) p -> (gpar p) gp", gpar=2), writes=[kp + "aim"], slow=True)
        for gpar in range(2):
            em.dma("sp", ldt[gpar * 64:(gpar + 1) * 64, :],
                   w["ssm_log_dt"][j, r].rearrange("(gp gpar) -> gpar gp", gpar=2)[gpar].partition_broadcast(64),
                   writes=[kp + f"ldt{gpar}"], slow=True)
        dt_, rho, th = T["dt"], T["rho"], T["th"]
        em.I("act", "activation", reads=[kp + "ldt0", kp + "ldt1"], writes=[kp + "dt"], out=dt_[:], in_=ldt[:], func=AF.Exp)
        em.I("dve", "tensor_tensor", reads=[kp + "dt", kp + "are"], writes=[kp + "rho"], out=rho[:], in0=are[:], in1=dt_[:], op=ALU.mult)
        em.I("dve", "tensor_tensor", reads=[kp + "dt", kp + "aim"], writes=[kp + "th"], out=th[:], in0=aim[:], in1=dt_[:], op=ALU.mult)
        mag = T["mag"]
        em.I("act", "activation", reads=[kp + "rho"], writes=[kp + "mag"], out=mag[:], in_=rho[:], func=AF.Exp)
        kf, z = T["kf"], T["z"]
        MAGIC = 12582912.0
        em.I("dve", "tensor_scalar", reads=[kp + "th"], writes=[kp + "kf"], out=kf[:], in0=th[:], scalar1=1.0 / (2 * math.pi), scalar2=MAGIC,
             op0=ALU.mult, op1=ALU.add)
        em.I("dve", "tensor_scalar", reads=[kp + "kf"], writes=[kp + "kf"], out=kf[:], in0=kf[:], scalar1=-MAGIC, scalar2=None, op0=ALU.add)
        em.I("dve", "scalar_tensor_tensor", reads=[kp + "kf", kp + "th"], writes=[kp + "z"], out=z[:], in0=kf[:], scalar=-2 * math.pi, in1=th[:],
             op0=ALU.mult, op1=ALU.add)
        sw, sw2, cw = T["sw"], T["sw2"], T["cw"]
        em.I("act", "activation", reads=[kp + "z"], writes=[kp + "sw"], out=sw[:], in_=z[:], func=AF.Sin, scale=0.5)
        em.I("act", "activation", reads=[kp + "z"], writes=[kp + "sw2"], out=sw2[:], in_=z[:], func=AF.Sin, scale=0.25)
        em.I("dve", "tensor_tensor", reads=[kp + "sw2"], writes=[kp + "cw"], out=cw[:], in0=sw2[:], in1=sw2[:], op=ALU.mult)
        em.I("dve", "tensor_scalar", reads=[kp + "cw"], writes=[kp + "cw"], out=cw[:], in0=cw[:], scalar1=-2.0, scalar2=1.0, op0=ALU.mult, op1=ALU.add)
        sn, cs = T["sn"], T["cs"]
        em.I("dve", "scalar_tensor_tensor", reads=[kp + "sw", kp + "cw"], writes=[kp + "sn"], out=sn[:], in0=sw[:], scalar=2.0, in1=cw[:], op0=ALU.mult, op1=ALU.mult)
        em.I("dve", "tensor_tensor", reads=[kp + "sw"], writes=[kp + "cs"], out=cs[:], in0=sw[:], in1=sw[:], op=ALU.mult)
        em.I("dve", "tensor_scalar", reads=[kp + "cs"], writes=[kp + "cs"], out=cs[:], in0=cs[:], scalar1=-2.0, scalar2=1.0, op0=ALU.mult, op1=ALU.add)
        ar, ai = T["ar"], T["ai"]
        em.I("dve", "tensor_tensor", reads=[kp + "mag", kp + "cs"], writes=[kp + "ar"], out=ar[:], in0=mag[:], in1=cs[:], op=ALU.mult)
        em.I("dve", "tensor_tensor", reads=[kp + "mag", kp + "sn"], writes=[kp + "ai"], out=ai[:], in0=mag[:], in1=sn[:], op=ALU.mult)
        C1, C2 = T["C1"], T["C2"]
        o_ = r * 64
        em.I("dve", "tensor_copy", reads=[kp + "ar"], writes=["C1"], out=C1[:, o_:o_ + 32], in_=ar[:])
        em.I("dve", "tensor_copy", reads=[kp + "ar"], writes=["C1"], out=C1[:, o_ + 32:o_ + 64], in_=ar[:])
        em.I("dve", "tensor_scalar", reads=[kp + "ai"], writes=["C2"], out=C2[:, o_:o_ + 32], in0=ai[:], scalar1=-1.0, scalar2=None, op0=ALU.mult)
        em.I("dve", "tensor_copy", reads=[kp + "ai"], writes=["C2"], out=C2[:, o_ + 32:o_ + 64], in_=ai[:])
        nr, den, t1, t2, qre, qim = T["nr"], T["den"], T["t1"], T["t2"], T["qre"], T["qim"]
        em.I("dve", "tensor_scalar", reads=[kp + "ar"], writes=[kp + "nr"], out=nr[:], in0=ar[:], scalar1=-1.0, scalar2=None, op0=ALU.add)
        em.I("dve", "tensor_tensor", reads=[kp + "are"], writes=[kp + "den"], out=den[:], in0=are[:], in1=are[:], op=ALU.mult)
        em.I("dve", "tensor_tensor", reads=[kp + "aim"], writes=[kp + "t1"], out=t1[:], in0=aim[:], in1=aim[:], op=ALU.mult)
        em.I("dve", "tensor_tensor", reads=[kp + "den", kp + "t1"], writes=[kp + "den"], out=den[:], in0=den[:], in1=t1[:], op=ALU.add)
        em.I("dve", "reciprocal", reads=[kp + "den"], writes=[kp + "den"], out=den[:], in_=den[:])
        em.I("dve", "tensor_tensor", reads=[kp + "nr"], writes=[kp + "t1"], out=t1[:], in0=nr[:], in1=are[:], op=ALU.mult)
        em.I("dve", "tensor_tensor", reads=[kp + "ai"], writes=[kp + "t2"], out=t2[:], in0=ai[:], in1=aim[:], op=ALU.mult)
        em.I("dve", "tensor_tensor", reads=[kp + "t1", kp + "t2"], writes=[kp + "qre"], out=qre[:], in0=t1[:], in1=t2[:], op=ALU.add)
        em.I("dve", "tensor_tensor", reads=[kp + "qre", kp + "den"], writes=[kp + "qre"], out=qre[:], in0=qre[:], in1=den[:], op=ALU.mult)
        em.I("dve", "tensor_tensor", reads=[kp + "ai"], writes=[kp + "t1"], out=t1[:], in0=ai[:], in1=are[:], op=ALU.mult)
        em.I("dve", "tensor_tensor", reads=[kp + "nr"], writes=[kp + "t2"], out=t2[:], in0=nr[:], in1=aim[:], op=ALU.mult)
        em.I("dve", "tensor_tensor", reads=[kp + "t1", kp + "t2"], writes=[kp + "qim"], out=qim[:], in0=t1[:], in1=t2[:], op=ALU.subtract)
        em.I("dve", "tensor_tensor", reads=[kp + "qim", kp + "den"], writes=[kp + "qim"], out=qim[:], in0=qim[:], in1=den[:], op=ALU.mult)
        if r == 1 and self.dbg == 3:
            for nm in ("th", "z", "sn", "cs", "ar", "ai", "qre", "qim", "mag", "dt"):
                self.dump(nm, T[nm][:], [kp + nm])
            self.dump("C1", C1[:], ["C1"])
            self.dump("C2", C2[:], ["C2"])
        bre, bim, bbr, bbi, tb = T["bre"], T["bim"], T["bbr"], T["bbi"], T["tb"]
        em.dma("sp", bre[:], w["ssm_b_re"][j, r].rearrange("(gp gpar) p hi -> (gpar p) gp hi", gpar=2), writes=[kp + "bre"], slow=True)
        em.dma("sp", bim[:], w["ssm_b_im"][j, r].rearrange("(gp gpar) p hi -> (gpar p) gp hi", gpar=2), writes=[kp + "bim"], slow=True)
        qre_b = qre[:].unsqueeze(2).to_broadcast([128, 32, 16])
        qim_b = qim[:].unsqueeze(2).to_broadcast([128, 32, 16])
        em.I("dve", "tensor_tensor", reads=[kp + "bre", kp + "qre"], writes=[kp + "bbr"], out=bbr[:], in0=bre[:], in1=qre_b, op=ALU.mult)
        em.I("dve", "tensor_tensor", reads=[kp + "bim", kp + "qim"], writes=[kp + "tb"], out=tb[:], in0=bim[:], in1=qim_b, op=ALU.mult)
        em.I("dve", "tensor_tensor", reads=[kp + "bbr", kp + "tb"], writes=[kp + "bbr"], out=bbr[:], in0=bbr[:], in1=tb[:], op=ALU.subtract)
        em.I("dve", "tensor_tensor", reads=[kp + "bim", kp + "qre"], writes=[kp + "bbi"], out=bbi[:], in0=bim[:], in1=qre_b, op=ALU.mult)
        em.I("dve", "tensor_tensor", reads=[kp + "bre", kp + "qim"], writes=[kp + "tb"], out=tb[:], in0=bre[:], in1=qim_b, op=ALU.mult)
        em.I("dve", "tensor_tensor", reads=[kp + "bbi", kp + "tb"], writes=[kp + "bbi"], out=bbi[:], in0=bbi[:], in1=tb[:], op=ALU.add)
        Bq, WB, tps = T["Bq"], T["WB"][r], T["tps"]
        for ri, bb in enumerate((bbr, bbi)):
            for gp in range(32):
                k = gp % 4
                col = ri * 32 + gp
                em.I("dve", "tensor_copy", reads=[kp + "bbr", kp + "bbi", f"Bqz{k}"], writes=[f"Bq{k}"],
                     out=Bq[k][0:64, 32 * k:32 * k + 16], in_=bb[0:64, gp, :])
                em.I("dve", "tensor_copy", reads=[kp + "bbr", kp + "bbi"], writes=[f"Bq{k}"],
                     out=Bq[k][64:128, 32 * k + 16:32 * k + 32], in_=bb[64:128, gp, :])
                pb = col % 2
                em.I("pe", "transpose", reads=[f"Bq{k}", "identf"], writes=[f"tps{pb}"], out=tps[pb][:, 0:128], in_=Bq[k][:], identity=self.ident_f[:])
                em.I("act", "activation", reads=[f"tps{pb}"], writes=[f"WB{r}_{col}"], out=WB[:, col, :], in_=tps[pb][:, 0:128], func=AF.Copy)
        if r == 1 and self.dbg == 3:
            self.dump("bbr", bbr[:], [kp + "bbr"])
            self.dump("bbi", bbi[:], [kp + "bbi"])
            self.dump("WB", WB[:], [f"WB{r}_{c}" for c in range(64)])
        cin, Cq, WC = T["cin"], T["Cq"], T["WC"][r]
        for ri, nm in enumerate(("ssm_c_re", "ssm_c_im")):
            csrc = w[nm][j, r].rearrange("(gp gpar) ho p -> gp ho gpar p", gpar=2)
            for i in range(4):
                b = (ri * 4 + i) % 2
                for gl in range(8):
                    em.dma("sp", cin[b][gl * 16:(gl + 1) * 16, :].rearrange("ho (gpar p) -> ho gpar p", gpar=2), csrc[8 * i + gl],
                           writes=[f"cin{b}_{gl}"])
                em.I("pe", "transpose", reads=[f"cin{b}_{gl}" for gl in range(8)] + ["identf"], writes=[f"tps{b}"], out=tps[b][:, 0:128], in_=cin[b][:], identity=self.ident_f[:])
                em.I("act", "activation", reads=[f"tps{b}"], writes=[kp + f"Cq{ri}"], out=Cq[ri][:, 8 * i:8 * i + 8, :],
                     in_=tps[b][:, 0:128].rearrange("q (g h) -> q g h", h=16), func=AF.Copy, scale=(1.0 if ri == 0 else -1.0))
            for k in range(4):
                c0 = 32 * k
                em.I("dve", "tensor_copy", reads=[kp + f"Cq{ri}", "WCz"], writes=[f"WC{r}_{ri}"],
                     out=WC[0:64, ri * 32 + k:ri * 32 + 32:4, c0:c0 + 16], in_=Cq[ri][0:64, k:32:4, :])
                em.I("dve", "tensor_copy", reads=[kp + f"Cq{ri}", "WCz"], writes=[f"WC{r}_{ri}"],
                     out=WC[64:128, ri * 32 + k:ri * 32 + 32:4, c0 + 16:c0 + 32], in_=Cq[ri][64:128, k:32:4, :])

    def phase_s5(self, L):
        em, nc, S, w = self.em, self.nc, self.S, self.w
        j = L // 2
        NT = self.NT
        GC = 2.0 * math.sqrt(2.0 / math.pi)
        with contextlib.ExitStack() as st:
            self.em.barrier()
            G = self.sb(st, [128, D], F32, "G")
            hb = [self.sb(st, [128, D], F32, "hb") for _ in range(2)]
            un = self.sb(st, [128, D], F32, "un")
            u = self.sb(st, [128, D], BF16, "u")
            uTt = [self.sb(st, [128, 8, 128], BF16, "uTt") for _ in range(2)]
            junk = self.sb(st, [128, D], BF16, "junk")
            ss = self.sb(st, [128, 1], F32, "ss")
            ms = self.sb(st, [128, 1], F32, "ms")
            sd = self.sb(st, [128, 1], F32, "sd")
            rstd = self.sb(st, [128, 1], F32, "rstd")
            tpp = [self.ps(st, [128, 1024], BF16, "tpp") for _ in range(2)]
            rvp = [self.ps(st, [128, 1024], F32, "rvp") for _ in range(2)]
            uTrt = [self.sb(st, [128, 8, 128], BF16, "uTrt") for _ in range(2)]
            em.dma("sp", G[:], w["norm_mix"][L].partition_broadcast(128), writes=["G"], slow=True)
            em.dma("sp", hb[0][:], self.out[0:128, :], reads=["out0"], writes=["hb0"])
            for i in range(NT):
                b = i % 2
                if i + 1 < NT:
                    em.dma("sp", hb[1 - b][:], self.out[(i + 1) * 128:(i + 2) * 128, :], reads=[f"out{i + 1}"], writes=[f"hb{1 - b}"])
                self.rms_rstd([hb[b][:]], junk, ss, ms, sd, rstd, [f"hb{b}"], "p")
                em.I("act", "activation", reads=[f"hb{b}", "prstd"], writes=["un"], out=un[:], in_=hb[b][:], func=AF.Copy, scale=rstd[:, 0:1])
                em.I("dve", "tensor_tensor", reads=["un", "G"], writes=["u"], out=u[:], in0=un[:], in1=G[:], op=ALU.mult)
                for kt in range(8):
                    em.I("pe", "transpose", reads=["u", "identb"], writes=[f"tpp{b}"], out=tpp[b][:, kt * 128:(kt + 1) * 128],
                         in_=u[:, kt * 128:(kt + 1) * 128], identity=self.ident_b[:])
                em.I("dve", "tensor_copy", reads=[f"tpp{b}"], writes=[f"uTt{b}"], out=uTt[b][:], in_=tpp[b][:].rearrange("p (k t) -> p k t", k=8))
                em.dma(STORE_Q, self.uT_s[:, :, i * 128:(i + 1) * 128].rearrange("k p t -> p k t"), uTt[b][:],
                       reads=[f"uTt{b}"], writes=[f"uTs{i}"])
                for kt in range(8):
                    em.I("pe", "matmul", reads=["u", "antib"], writes=[f"rvp{b}"], out=rvp[b][:, kt * 128:(kt + 1) * 128],
                         lhsT=u[:, kt * 128:(kt + 1) * 128], rhs=self.anti_b[:], start=True, stop=True)
                em.I("act", "activation", reads=[f"rvp{b}"], writes=[f"uTr{b}"], out=uTrt[b][:], in_=rvp[b][:].rearrange("p (k t) -> p k t", k=8), func=AF.Copy)
                ir = NT - 1 - i
                if i == NT - 1:
                    self.dump("uTrt", uTrt[b][:], [f"uTr{b}"])
                    self.dump("uTt", uTt[b][:], [f"uTt{b}"])
                em.dma(STORE_Q, self.uTr_s[:, :, ir * 128:(ir + 1) * 128].rearrange("k p t -> p k t"), uTrt[b][:],
                       reads=[f"uTr{b}"], writes=[f"uTrs{ir}"])
        TB = 64
        NB = S // TB
        with contextlib.ExitStack() as st:
            self.em.barrier()
            T = {}
            for nm in ("are", "aim", "ldt", "dt", "rho", "th", "mag", "kf", "z", "sw", "sw2", "cw", "sn", "cs", "ar", "ai",
                       "nr", "den", "t1", "t2", "qre", "qim"):
                T[nm] = self.sb(st, [128, 32], F32, nm)
            T["C1"] = self.sb(st, [128, 128], F32, "C1")
            T["C2"] = self.sb(st, [128, 128], F32, "C2")
            for nm in ("bre", "bim", "bbr", "bbi", "tb"):
                T[nm] = self.sb(st, [128, 32, 16], F32, nm)
            T["Bq"] = [self.sb(st, [128, 128], F32, "Bq") for _ in range(4)]
            T["WB"] = [self.sb(st, [128, 64, 128], BF16, "WB") for _ in range(2)]
            T["WC"] = [self.sb(st, [128, 64, 128], BF16, "WC") for _ in range(2)]
            T["cin"] = [self.sb(st, [128, 128], F32, "cin") for _ in range(2)]
            T["Cq"] = [self.sb(st, [128, 32, 16], F32, "Cq") for _ in range(2)]
            T["tps"] = [self.ps(st, [128, 512], F32, "tps") for _ in range(2)]
            for k in range(4):
                em.I("pool", "memset", writes=[f"Bqz{k}", f"Bq{k}"], ap=T["Bq"][k][:], constant=0.0)
            for r in range(2):
                em.I("pool", "memset", writes=["WCz", f"WC{r}_0", f"WC{r}_1"], ap=T["WC"][r][:], constant=0.0)
            for r in range(2):
                self.s5_prep_dir(st, j, r, T)
            C1, C2, WB, WC = T["C1"], T["C2"], T["WB"], T["WC"]
            BUH = [self.sb(st, [128, TB, 128], F32, "BUH") for _ in range(2)]
            HB = self.sb(st, [128, TB, 128], BF16, "HB")
            uTb = [[self.sb(st, [128, 8, TB], BF16, "uTb") for _ in range(2)] for _ in range(2)]
            P1 = self.sb(st, [128, 128], F32, "P1")
            P2 = self.sb(st, [128, 128], F32, "P2")
            carry = self.sb(st, [128, 128], F32, "carry")
            yt = [[self.sb(st, [128, 8, TB], F32, "yt") for _ in range(2)] for _ in range(2)]
            bup = [self.ps(st, [128, 8, TB], F32, "bup") for _ in range(2)]
            yp = [self.ps(st, [128, 512], F32, "yp") for _ in range(2)]
            yq = [self.ps(st, [128, 512], F32, "yq") for _ in range(2)]
            ytm = [self.sb(st, [TB, 128], F32, "ytm") for _ in range(2)]
            em.I("pool", "memset", writes=["carry"], ap=carry[:], constant=0.0)
            ysc = (self.yf_s, self.yb_s)

            def blk(r, n):
                return n if r == 0 else NB - 1 - n

            def emit_load(n):
                b = n % 2
                for r in range(2):
                    srcT = self.uT_s if r == 0 else self.uTr_s
                    kn = f"uTs{n * TB // 128}" if r == 0 else f"uTrs{n * TB // 128}"
                    em.dma("sp", uTb[r][b][:], srcT[:, :, n * TB:(n + 1) * TB].rearrange("k p t -> p k t"),
                           reads=[kn], writes=[f"uTb{r}{b}"], slow=True)

            def emit_bu(n):
                b = n % 2
                for r in range(2):
                    for c8 in range(8):
                        pb = c8 % 2
                        for cc in range(8):
                            col = c8 * 8 + cc
                            kt = (col % 32) // 4
                            em.I("pe", "matmul", reads=[f"WB{r}_{col}", f"uTb{r}{b}"], writes=[f"bup{pb}"],
                                 out=bup[pb][:, cc, :], lhsT=WB[r][:, col, :], rhs=uTb[r][b][:, kt, :], start=True, stop=True)
                        dst = BUH[b][:, :, r * 64 + c8 * 8:r * 64 + c8 * 8 + 8]
                        em.I("act", "activation", reads=[f"bup{pb}"], writes=[f"buh{b}"],
                             out=dst.rearrange("p t c -> p c t"), in_=bup[pb][:], func=AF.Copy)

            emit_load(0)
            emit_bu(0)
            for n in range(NB):
                b = n % 2
                if n + 1 < NB:
                    emit_load(n + 1)
                    emit_bu(n + 1)
                for t in range(TB):
                    if t == 0:
                        hp = carry[:]
                        rk = [f"buh{b}", "carry", "C1", "C2"]
                    else:
                        hp = BUH[b][:, t - 1, :]
                        rk = [f"buh{b}", "C1", "C2"]
                    hps = hp.rearrange("p (d two c) -> p d two c", d=2, two=2)[:, :, ::-1, :]
                    em.I("dve", "tensor_tensor", reads=rk, writes=["P1"], nosync=(t > 0), out=P1[:], in0=hp, in1=C1[:], op=ALU.mult)
                    em.I("dve", "tensor_tensor", reads=rk, writes=["P2"], nosync=True,
                         out=P2[:].rearrange("p (d two c) -> p d two c", d=2, two=2), in0=hps,
                         in1=C2[:].rearrange("p (d two c) -> p d two c", d=2, two=2), op=ALU.mult)
                    em.I("dve", "tensor_tensor", reads=["P1", "P2"], writes=["P1"], nosync=True, out=P1[:], in0=P1[:], in1=P2[:], op=ALU.add)
                    em.I("dve", "tensor_tensor", reads=["P1", f"buh{b}"], writes=[f"buh{b}"], nosync=True, out=BUH[b][:, t, :], in0=P1[:],
                         in1=BUH[b][:, t, :], op=ALU.add)
                em.I("dve", "tensor_copy", reads=[f"buh{b}"], writes=["carry"], out=carry[:], in_=BUH[b][:, TB - 1, :])
                if n == 0:
                    self.dump("BUH0", BUH[0][:], ["buh0"])
                    self.dump("BUH1pre", BUH[1][:], ["buh1"])
                    self.dump("uTb10", uTb[1][0][:], ["uTb10"])
                    self.dump("Wc0", WC[0][:], ["WC0_0", "WC0_1"])
                em.I("act", "activation", reads=[f"buh{b}"], writes=["HB"], out=HB[:], in_=BUH[b][:], func=AF.Copy)
                for r in range(2):
                    i = blk(r, n)
                    for kt in range(8):
                        pb = kt % 2
                        n_mm = 0
                        for ri in range(2):
                            for g4 in range(4):
                                col = ri * 32 + kt * 4 + g4
                                if r == 0:
                                    em.I("pe", "matmul", reads=[f"WC{r}_{ri}", "HB"], writes=[f"yp{pb}"], out=yp[pb][:, 0:TB],
                                         lhsT=WC[r][:, col, :], rhs=HB[:, :, col], start=(n_mm == 0), stop=(n_mm == 7))
                                else:
                                    em.I("pe", "matmul", reads=[f"WC{r}_{ri}", "HB"], writes=[f"yp{pb}"], out=yp[pb][0:TB, 0:128],
                                         lhsT=HB[:, :, 64 + col], rhs=WC[r][:, col, :], start=(n_mm == 0), stop=(n_mm == 7))
                                n_mm += 1
                        if r == 0:
                            em.I("act", "activation", reads=[f"yp{pb}"], writes=[f"yt{r}{b}"], out=yt[r][b][:, kt, :], in_=yp[pb][:, 0:TB], func=AF.Copy)
                        else:
                            em.I("act", "activation", reads=[f"yp{pb}"], writes=[f"ytm{pb}"], out=ytm[pb][:], in_=yp[pb][0:TB, 0:128], func=AF.Copy)
                            em.I("pe", "matmul", reads=[f"ytm{pb}", "antif"], writes=[f"yq{pb}"], out=yq[pb][:, 0:TB], lhsT=ytm[pb][:],
                                 rhs=self.anti_f[:], start=True, stop=True)
                            em.I("act", "activation", reads=[f"yq{pb}"], writes=[f"yt{r}{b}"], out=yt[r][b][:, kt, :], in_=yq[pb][:, 0:TB], func=AF.Copy)
                    if n == 0:
                        self.dump(f"yt{r}", yt[r][b][:], [f"yt{r}{b}"])
                    if n == NB - 1:
                        self.dump(f"ytL{r}", yt[r][b][:], [f"yt{r}{b}"])
                        if r == 1:
                            self.dump("BUHL", BUH[b][:], [f"buh{b}"])
                    em.dma(STORE_Q, ysc[r][:, :, i * TB:(i + 1) * TB].rearrange("k p t -> p k t"), yt[r][b][:],
                           reads=[f"yt{r}{b}"], writes=[f"ys{r}_{i}"], slow=True)
        with contextlib.ExitStack() as st:
            self.em.barrier()
            Wg = self.sb(st, [128, 8, 2048], BF16, "Wg")
            stgt = self.sb(st, [128, 2, 2048], F32, "stg")
            stg = [stgt[:, 0, :], stgt[:, 1, :]]
            uTt = [self.sb(st, [128, 8, 128], BF16, "uTt") for _ in range(2)]
            yft = [self.sb(st, [128, 8, 128], F32, "yft") for _ in range(2)]
            ybt = [self.sb(st, [128, 8, 128], F32, "ybt") for _ in range(2)]
            hb = [self.sb(st, [128, D], F32, "hb") for _ in range(2)]
            dT = self.sb(st, [128, 8], F32, "dT")
            yall = self.sb(st, [128, 8, 128], F32, "yall")
            g2 = self.sb(st, [128, 8, 128], F32, "g2")
            gT = self.sb(st, [128, 8, 128], BF16, "gT")
            sig = self.sb(st, [128, D], F32, "sig")
            mix = self.sb(st, [128, D], F32, "mix")
            glp = [self.ps(st, [128, 512], F32, "glp") for _ in range(4)]
            wg = w["ssm_w_glu"][j]
            em.dma("sp", dT[:], w["ssm_d"][j].rearrange("(kt p) -> p kt", p=128), writes=["dT"], slow=True)
            self.load_weight([Wg[:, i, :] for i in range(8)], [wg[i * 128:(i + 1) * 128, :] for i in range(8)], stg, None, "Wg")

            def load(i):
                b = i % 2
                sl = slice(i * 128, (i + 1) * 128)
                ykeys = [f"ys{r}_{k}" for r in range(2) for k in (2 * i, 2 * i + 1)]
                em.dma("sp", uTt[b][:], self.uT_s[:, :, sl].rearrange("k p t -> p k t"), reads=[f"uTs{i}"], writes=[f"uTt{b}"], slow=True)
                em.dma("sp", yft[b][:], self.yf_s[:, :, sl].rearrange("k p t -> p k t"), reads=ykeys, writes=[f"yft{b}"], slow=True)
                ix = em.dma("sp", ybt[b][:], self.yb_s[:, :, sl].rearrange("k p t -> p k t"), reads=ykeys, writes=[f"ybt{b}"], slow=True)
                if i == 0:
                    em.debug_ops = {"ybt_load0": ix, "yft_load0": ix - 1}
                em.dma("sp", hb[b][:], self.out[sl, :], reads=[f"out{i}"], writes=[f"hb{b}"])

            load(0)
            for i in range(NT):
                b = i % 2
                if i + 1 < NT:
                    load(i + 1)
                em.I("pool", "tensor_tensor", reads=[f"yft{b}", f"ybt{b}"], writes=["yall"], out=yall[:], in0=yft[b][:], in1=ybt[b][:], op=ALU.add)
                em.I("dve", "tensor_tensor", reads=[f"uTt{b}", "dT"], writes=["g2"], out=g2[:], in0=uTt[b][:],
                     in1=dT[:].unsqueeze(2).to_broadcast([128, 8, 128]), op=ALU.mult)
                em.I("dve", "tensor_tensor", reads=["g2", "yall"], writes=["yall"], out=yall[:], in0=yall[:], in1=g2[:], op=ALU.add)
                em.I("dve", "tensor_tensor", reads=["yall"], writes=["g2"], out=g2[:], in0=yall[:], in1=yall[:], op=ALU.mult)
                em.I("dve", "tensor_scalar", reads=["g2"], writes=["g2"], out=g2[:], in0=g2[:], scalar1=0.044715, scalar2=1.0, op0=ALU.mult, op1=ALU.add)
                em.I("dve", "tensor_tensor", reads=["g2", "yall"], writes=["g2"], out=g2[:], in0=g2[:], in1=yall[:], op=ALU.mult)
                em.I("act", "activation", reads=["g2"], writes=["g2"], out=g2[:], in_=g2[:], func=AF.Sigmoid, scale=GC)
                em.I("dve", "tensor_tensor", reads=["g2", "yall"], writes=["gT"], out=gT[:], in0=g2[:], in1=yall[:], op=ALU.mult)
                if i == 0:
                    self.dump("yall", yall[:], ["yall"])
                    self.dump("yft", yft[b][:], [f"yft{b}"])
                    self.dump("ybt", ybt[b][:], [f"ybt{b}"])
                for c in range(4):
                    for kt in range(8):
                        em.I("pe", "matmul", reads=[f"Wg{kt}", "gT"], writes=[f"glp{c}"], out=glp[c][:], lhsT=gT[:, kt, :],
                             rhs=Wg[:, kt, c * 512:(c + 1) * 512], start=(kt == 0), stop=(kt == 7))
                for c in range(2):
                    em.I("act", "activation", reads=[f"glp{2 + c}"], writes=[f"sig{c}"], out=sig[:, c * 512:(c + 1) * 512], in_=glp[2 + c][:], func=AF.Sigmoid)
                    em.I("dve", "tensor_tensor", reads=[f"glp{c}", f"sig{c}"], writes=[f"mix{c}"], out=mix[:, c * 512:(c + 1) * 512], in0=glp[c][:],
                         in1=sig[:, c * 512:(c + 1) * 512], op=ALU.mult)
                em.I("pool", "tensor_tensor", reads=["mix0", "mix1", f"hb{b}"], writes=[f"hb{b}"], out=hb[b][:], in0=hb[b][:], in1=mix[:], op=ALU.add)
                em.dma(STORE_Q, self.out[i * 128:(i + 1) * 128, :], hb[b][:], reads=[f"hb{b}"], writes=[f"out{i}"])
            if "ybsdump" in FLAGS:
                self.dump("ybs_dram", self.yb_s[:, :, :], [f"out{NT - 1}"])

def make_consts(S):
    ident = np.eye(128, dtype=np.float32)
    pos = np.arange(S, dtype=np.float32)
    inv_freq = (10000.0 ** (-np.arange(0, 64, 2, dtype=np.float32) / 64)).astype(np.float32)
    ang = pos[:, None] * inv_freq[None, :]
    rope = np.concatenate([np.cos(ang), np.sin(ang)], axis=1).astype(np.float32)
    anti = np.ascontiguousarray(ident[::-1])
    return {"ident_f": ident, "ident_b": ident.astype(ml_dtypes.bfloat16), "rope": rope,
            "anti_b": anti.astype(ml_dtypes.bfloat16), "anti_f": np.ascontiguousarray(np.eye(64, dtype=np.float32)[::-1])}


_CACHE = {}
DUMPS = None
FLAGS = set()
DBG = 0
STORE_Q = "pool"


def get_prog(S, phases):
    key = (S, tuple(phases))
    if key not in _CACHE:
        p = Prog(S, phases)
        p.dbg = DBG
        p.build()
        _CACHE[key] = p
    return _CACHE[key]


FULL_PHASES = (("attn", 0, "x"), ("ffn", 0), ("s5", 1), ("ffn", 1), ("attn", 2), ("ffn", 2), ("s5", 3), ("ffn", 3))


def run(inputs, phases, n_cores=8, trace=False):
    x = np.asarray(inputs["x"], dtype=np.float32)
    B, S, _ = x.shape
    prog = get_prog(S, phases)
    consts = make_consts(S)
    base = {n: np.ascontiguousarray(np.asarray(inputs[n], dtype=np.float32)) for n, _ in INPUT_SPECS}
    base.update(consts)
    in_maps = []
    for c in range(n_cores):
        m = dict(base)
        m["x"] = np.ascontiguousarray(x[c])
        in_maps.append(m)
    res = run_bass_kernel_spmd(prog.nc, in_maps, core_ids=list(range(n_cores)), trace=trace)
    outs = np.stack([np.asarray(r["out"]) for r in res.results], axis=0)
    return outs, res


def kernel(**inputs):
    outs, _ = run(inputs, FULL_PHASES, n_cores=8)
    return outs.astype(np.float32)
```

```python
import contextlib
import math
import numpy as np
import ml_dtypes
import concourse.bass as bass
import concourse.mybir as mybir
from concourse.bass_utils import run_bass_kernel_spmd

F32 = mybir.dt.float32
BF16 = mybir.dt.bfloat16
I32 = mybir.dt.int32
AF = mybir.ActivationFunctionType
ALU = mybir.AluOpType
AX = mybir.AxisListType

D = 1024
DFF = 4096
NH = 8
EPS = 1e-6
ENGS = ("pe", "act", "dve", "pool", "sp")
SEM_MAX = 30000
NDMA_SLOTS = 24
FENCE_DIST = 0
FENCE_REPS = 2


class Op:
    __slots__ = ("eng", "fn", "deps", "idx", "need_inc", "is_dma", "semid", "semval", "slot_prev")

    def __init__(self, eng, fn, deps, is_dma):
        self.eng = eng
        self.fn = fn
        self.deps = deps
        self.is_dma = is_dma
        self.need_inc = False
        self.semid = None
        self.semval = None
        self.slot_prev = None


class Emitter:
    def __init__(self, nc, same_engine_sync=True):
        self.nc = nc
        self.ops = []
        self.last_w = {}
        self.readers = {}
        self.same_engine_sync = same_engine_sync
        self.nosync = set()
        self.cur_barrier = None
        self.dma_since_barrier = []
        self.last_on_eng = {}
        self.bar_a = nc.dram_tensor("bar_a", [1, 16], F32).ap()
        self.bar_b = nc.dram_tensor("bar_b", [1, 16], F32).ap()

    def barrier(self):
        deps = set(self.last_on_eng.values()) | set(self.dma_since_barrier)
        if self.cur_barrier is not None:
            deps.add(self.cur_barrier)
        a, b = self.bar_a, self.bar_b
        idx = self.op("sp", lambda e: e.dma_start(out=b, in_=a), (), (), is_dma=True)
        self.ops[idx].deps |= deps
        self.cur_barrier = idx
        self.dma_since_barrier = []

    def I(self, eng, name, reads=(), writes=(), nosync=False, **kw):
        idx = self.op(eng, (name, kw), reads, writes)
        if nosync:
            self.nosync.add(idx)
        return idx

    def op(self, eng, fn, reads=(), writes=(), is_dma=False):
        deps = set()
        for k in reads:
            w = self.last_w.get(k)
            if w is not None:
                deps.add(w)
        for k in writes:
            w = self.last_w.get(k)
            if w is not None:
                deps.add(w)
            for r in self.readers.get(k, ()):
                deps.add(r)
        if self.cur_barrier is not None:
            deps.add(self.cur_barrier)
        idx = len(self.ops)
        o = Op(eng, fn, deps, is_dma)
        o.idx = idx
        self.ops.append(o)
        if is_dma:
            self.dma_since_barrier.append(idx)
        elif fn is not None:
            self.last_on_eng[eng] = idx
        for k in reads:
            self.readers.setdefault(k, []).append(idx)
        for k in writes:
            self.last_w[k] = idx
            self.readers[k] = []
        return idx

    def dma(self, eng, out, in_, reads=(), writes=(), slow=False):
        if slow:
            fn = lambda e: e.dma_start(out=out, in_=in_, allow_slow_non_contiguous=True)
        else:
            fn = lambda e: e.dma_start(out=out, in_=in_)
        return self.op(eng, fn, reads, writes, is_dma=True)

    def finalize(self):
        nc = self.nc
        ops = self.ops
        per_eng = {e: [] for e in ENGS}
        for o in ops:
            per_eng[o.eng].append(o)
        pos = {}
        for e in ENGS:
            for i, o in enumerate(per_eng[e]):
                pos[o.idx] = i
        waited = {e: {p: -1 for p in ENGS} for e in ENGS}
        waited_dma = {e: set() for e in ENGS}
        wait_lists = {}
        for o in ops:
            wl = {}
            dl = []
            for d in o.deps:
                po = ops[d]
                if po.is_dma:
                    if d not in waited_dma[o.eng]:
                        waited_dma[o.eng].add(d)
                        dl.append(d)
                    continue
                pe_ = po.eng
                if pe_ == o.eng and not o.is_dma:
                    if pe_ == "pe" or not self.same_engine_sync or o.idx in self.nosync:
                        continue
                if pos[d] <= waited[o.eng][pe_]:
                    continue
                if pe_ not in wl or pos[d] > pos[wl[pe_]]:
                    wl[pe_] = d
            for pe_, d in wl.items():
                waited[o.eng][pe_] = pos[d]
                ops[d].need_inc = True
            wait_lists[o.idx] = list(wl.values()) + dl
        n_sems = {}
        dma_cnt = {e: 0 for e in ENGS}
        dma_last = {}
        for e in ENGS:
            cnt = 0
            semi = 0
            for o in per_eng[e]:
                if o.is_dma:
                    k = dma_cnt[e]
                    dma_cnt[e] += 1
                    slot = k % NDMA_SLOTS
                    o.semid = ("dma", e, slot)
                    o.semval = 16 * (k // NDMA_SLOTS + 1)
                    o.slot_prev = dma_last.get((e, slot))
                    dma_last[(e, slot)] = o
                    o.need_inc = True
                elif o.need_inc:
                    if cnt >= SEM_MAX:
                        semi += 1
                        cnt = 0
                    cnt += 1
                    o.semid = (e, semi)
                    o.semval = cnt
            n_sems[e] = semi + 1
        self.n_inst = {e: len(per_eng[e]) for e in ENGS}
        fence_need = {}
        for o in ops:
            if o.is_dma:
                for d in wait_lists[o.idx]:
                    if ops[d].is_dma and (o.idx - d) < FENCE_DIST:
                        fence_need[o.idx] = True
        self.n_fence = len(fence_need)
        for nm, ix in getattr(self, "debug_ops", {}).items():
            o = ops[ix]
            print("DEBUGOP", nm, "idx", ix, "eng", o.eng, "deps", sorted(o.deps), "waits", [(d, ops[d].eng, ops[d].is_dma, ops[d].semid, ops[d].semval) for d in wait_lists[ix]], "fenced", ix in fence_need, flush=True)
        fa = nc.dram_tensor("fence_a", [1, 16], F32).ap()
        fb = {e: nc.dram_tensor(f"fence_b_{e}", [1, 16], F32).ap() for e in ENGS}
        fence_cnt = {e: 0 for e in ENGS}
        with contextlib.ExitStack() as st:
            sems = {}
            fsem = {e: st.enter_context(nc.semaphore(f"fence_{e}")) for e in ("sp", "pool")}
            for e in ENGS:
                for i in range(n_sems[e]):
                    sems[(e, i)] = st.enter_context(nc.semaphore(f"s_{e}_{i}"))
                for s in range(min(NDMA_SLOTS, dma_cnt[e])):
                    sems[("dma", e, s)] = st.enter_context(nc.semaphore(f"d_{e}_{s}"))
            block = st.enter_context(nc.Block())

            def run(e):
                def body(eng):
                    for o in per_eng[e]:
                        if o.is_dma and o.slot_prev is not None:
                            p = o.slot_prev
                            eng.wait_ge(sems[p.semid], p.semval)
                        for d in wait_lists[o.idx]:
                            po = ops[d]
                            eng.wait_ge(sems[po.semid], po.semval)
                        if o.idx in fence_need:
                            for _ in range(FENCE_REPS):
                                fence_cnt[e] += 1
                                eng.dma_start(out=fb[e], in_=fa).then_inc(fsem[e], 16)
                                eng.wait_ge(fsem[e], 16 * fence_cnt[e])
                        if o.fn is None:
                            continue
                        if isinstance(o.fn, tuple):
                            ins = getattr(eng, o.fn[0])(**o.fn[1])
                        else:
                            ins = o.fn(eng)
                        if o.need_inc:
                            ins.then_inc(sems[o.semid], 16 if o.is_dma else 1)
                return body

            block.tensor(run("pe"))
            block.scalar(run("act"))
            block.vector(run("dve"))
            block.gpsimd(run("pool"))
            block.sync(run("sp"))


INPUT_SPECS = [
    ("norm_mix", (4, 1024)), ("norm_ffn", (4, 1024)), ("attn_w_qkv", (2, 1024, 3072)),
    ("attn_q_gain", (2, 64)), ("attn_k_gain", (2, 64)), ("attn_lambda", (2, 4, 64)),
    ("attn_subln", (2, 128)), ("attn_w_o", (2, 1024, 1024)), ("ssm_a_re", (2, 2, 64, 64)),
    ("ssm_a_im", (2, 2, 64, 64)), ("ssm_log_dt", (2, 2, 64)), ("ssm_b_re", (2, 2, 64, 64, 16)),
    ("ssm_b_im", (2, 2, 64, 64, 16)), ("ssm_c_re", (2, 2, 64, 16, 64)), ("ssm_c_im", (2, 2, 64, 16, 64)),
    ("ssm_d", (2, 1024)), ("ssm_w_glu", (2, 1024, 2048)), ("ffn_w_up", (4, 1024, 4096)),
    ("ffn_w_down", (4, 4096, 1024)),
]


class Prog:
    def __init__(self, S, phases):
        self.S = S
        self.NT = S // 128
        self.phases = phases
        self.nc = bass.Bass("TRN2", target_bir_lowering=False)
        self.em = Emitter(self.nc)
        self.uid = 0
        self.dbg_names = []

    def dram_in(self, name, shape, dt=F32):
        return self.nc.dram_tensor(name, list(shape), dt, kind="ExternalInput").ap()

    def sb(self, st, shape, dt, name=None):
        self.uid += 1
        return st.enter_context(self.nc.sbuf_tensor(f"{name or 't'}_{self.uid}", list(shape), dt))

    def ps(self, st, shape, dt, name=None):
        self.uid += 1
        return st.enter_context(self.nc.psum_tensor(f"{name or 'p'}_{self.uid}", list(shape), dt))

    def build(self):
        nc, S = self.nc, self.S
        self.x = self.dram_in("x", (S, D))
        self.w = {n: self.dram_in(n, shp) for n, shp in INPUT_SPECS}
        self.ident_f_d = self.dram_in("ident_f", (128, 128))
        self.ident_b_d = self.dram_in("ident_b", (128, 128), BF16)
        self.rope_d = self.dram_in("rope", (S, 64))
        self.jb_d = self.dram_in("anti_b", (128, 128), BF16)
        self.jf_d = self.dram_in("anti_f", (64, 64))
        self.out = nc.dram_tensor("out", [S, D], F32, kind="ExternalOutput").ap()
        self.qT_s = nc.dram_tensor("qT_s", [NH, 128, S], BF16).ap()
        self.kT_s = nc.dram_tensor("kT_s", [NH, 128, S], BF16).ap()
        self.v_s = nc.dram_tensor("v_s", [S, D], BF16).ap()
        self.oT_s = nc.dram_tensor("oT_s", [NH, 128, S], BF16).ap()
        self.uT_s = nc.dram_tensor("uT_s", [8, 128, S], BF16).ap()
        self.yb_s = nc.dram_tensor("yb_s", [8, 128, S], F32).ap()
        self.yf_s = nc.dram_tensor("yf_s", [8, 128, S], F32).ap()
        self.uTr_s = nc.dram_tensor("uTr_s", [8, 128, S], BF16).ap()
        em = self.em
        with contextlib.ExitStack() as st:
            self.ident_f = self.sb(st, [128, 128], F32, "identf")
            self.ident_b = self.sb(st, [128, 128], BF16, "identb")
            em.dma("sp", self.ident_f[:], self.ident_f_d, writes=["identf"])
            em.dma("sp", self.ident_b[:], self.ident_b_d, writes=["identb"])
            self.anti_b = self.sb(st, [128, 128], BF16, "antib")
            self.anti_f = self.sb(st, [64, 64], F32, "antif")
            em.dma("sp", self.anti_b[:], self.jb_d, writes=["antib"])
            em.dma("sp", self.anti_f[:], self.jf_d, writes=["antif"])
            for ph in self.phases:
                kind = ph[0]
                if kind == "copy":
                    self.phase_copy()
                elif kind == "ffn":
                    self.phase_ffn(ph[1], ph[2] if len(ph) > 2 else "out")
                elif kind == "attn":
                    self.phase_attn(ph[1], ph[2] if len(ph) > 2 else "out")
                elif kind == "s5":
                    self.phase_s5(ph[1])
            em.op("sp", None, reads=[f"out{i}" for i in range(self.NT)] + ["dbgout_" + n for n in self.dbg_names])
            em.finalize()
        return nc

    def dump(self, name, ap, keys):
        if not getattr(self, "dbg", 0):
            return
        if DUMPS is not None and name not in DUMPS:
            return
        shp = list(ap.shape)
        d = self.nc.dram_tensor("dbg_" + name, shp, ap.dtype, kind="ExternalOutput").ap()
        self.em.dma("sp", d, ap, reads=keys, writes=["dbgout_" + name])
        self.dbg_names.append(name)

    def src_ap(self, src):
        return self.x if src == "x" else self.out

    def phase_copy(self):
        em = self.em
        for i in range(self.NT):
            em.dma("sp", self.out[i * 128:(i + 1) * 128, :], self.x[i * 128:(i + 1) * 128, :],
                   writes=[f"out{i}"])


    def rms_rstd(self, hb_aps, junk, ss, ms, sd, rstd, keys_in, kp, dim=1024):
        em = self.em
        n = len(hb_aps)
        for a, hap in enumerate(hb_aps):
            em.I("act", "activation", reads=keys_in, writes=[kp + "junk", kp + "ss"],
                 out=junk[:], in_=hap, func=AF.Square, accum_out=ss[:, a:a + 1])
        em.I("dve", "tensor_scalar", reads=[kp + "ss"], writes=[kp + "ms"],
             out=ms[:, 0:n], in0=ss[:, 0:n], scalar1=1.0 / dim, scalar2=EPS, op0=ALU.mult, op1=ALU.add)
        em.I("act", "activation", reads=[kp + "ms"], writes=[kp + "sd"], out=sd[:, 0:n], in_=ms[:, 0:n], func=AF.Sqrt)
        em.I("dve", "reciprocal", reads=[kp + "sd"], writes=[kp + "rstd"], out=rstd[:, 0:n], in_=sd[:, 0:n])

    def load_weight(self, dsts, srcs, stg, gains, key_dst, engs=("act", "pool", "dve")):
        em = self.em
        for i, (dst, sa) in enumerate(zip(dsts, srcs)):
            b = i % 2
            em.dma("sp", stg[b], sa, writes=[f"stg{b}"])
            eng = engs[i % len(engs)]
            g = gains[i] if gains is not None else None
            rk = [f"stg{b}"] + (["gain"] if g is not None else [])
            wk = [f"{key_dst}{i}"]
            if eng == "act":
                if g is not None:
                    em.I("act", "activation", reads=rk, writes=wk, out=dst, in_=stg[b], func=AF.Copy, scale=g)
                else:
                    em.I("act", "activation", reads=rk, writes=wk, out=dst, in_=stg[b], func=AF.Copy)
            else:
                if g is not None:
                    em.I(eng, "tensor_scalar", reads=rk, writes=wk, out=dst, in0=stg[b], scalar1=g, scalar2=None, op0=ALU.mult)
                else:
                    em.I(eng, "tensor_copy", reads=rk, writes=wk, out=dst, in_=stg[b])

    def phase_ffn(self, L, src="out"):
        em, nc, S = self.em, self.nc, self.S
        ST = 256
        NST = S // ST
        src_t = self.src_ap(src)
        with contextlib.ExitStack() as st:
            self.em.barrier()
            Wup = self.sb(st, [128, 8, DFF], BF16, "wup")
            Wdn = self.sb(st, [128, 32, D], BF16, "wdn")
            gain = self.sb(st, [128, 8], F32, "gain")
            stgt = self.sb(st, [128, 2, 2048], F32, "stg")
            hb = [self.sb(st, [128, 2, D], F32, "hb") for _ in range(2)]
            u = self.sb(st, [128, 2, D], BF16, "u")
            uT = self.sb(st, [128, 8, ST], BF16, "uT")
            hid = self.sb(st, [128, 32, ST], BF16, "hid")
            r = [self.sb(st, [128, ST], BF16, "r") for _ in range(2)]
            junk = self.sb(st, [128, D], BF16, "junk")
            ss = self.sb(st, [128, 2], F32, "ss")
            ms = self.sb(st, [128, 2], F32, "ms")
            sd = self.sb(st, [128, 2], F32, "sd")
            rstd = self.sb(st, [128, 2], F32, "rstd")
            tpp = [self.ps(st, [128, 1024], BF16, "tpp") for _ in range(2)]
            upp = [self.ps(st, [128, 512], F32, "upp") for _ in range(2)]
            dnp = [self.ps(st, [128, 512], F32, "dnp") for _ in range(2)]
            stg = [stgt[:, 0, :], stgt[:, 1, :]]
            em.dma("sp", gain[:], self.w["norm_ffn"][L].rearrange("(kt p) -> p kt", p=128), writes=["gain"], slow=True)
            wu = self.w["ffn_w_up"][L]
            wd = self.w["ffn_w_down"][L]
            self.load_weight([Wup[:, i // 2, (i % 2) * 2048:(i % 2 + 1) * 2048] for i in range(16)],
                             [wu[(i // 2) * 128:(i // 2 + 1) * 128, (i % 2) * 2048:(i % 2 + 1) * 2048] for i in range(16)],
                             stg, [gain[:, i // 2:i // 2 + 1] for i in range(16)], "Wup")
            stg3 = [stgt[:, 0, :].rearrange("p (a d) -> p a d", a=2), stgt[:, 1, :].rearrange("p (a d) -> p a d", a=2)]
            self.load_weight([Wdn[:, 2 * i:2 * i + 2, :] for i in range(16)],
                             [wd[i * 256:(i + 1) * 256, :].rearrange("(a p) d -> p a d", p=128) for i in range(16)],
                             stg3, None, "Wdn")

            def load(sti):
                b = sti % 2
                em.dma("sp", hb[b][:], src_t[sti * ST:(sti + 1) * ST, :].rearrange("(a p) d -> p a d", p=128),
                       reads=[f"out{2 * sti}", f"out{2 * sti + 1}"], writes=[f"hb{b}"])

            load(0)
            for sti in range(NST):
                b = sti % 2
                if sti + 1 < NST:
                    load(sti + 1)
                self.rms_rstd([hb[b][:, a, :] for a in range(2)], junk, ss, ms, sd, rstd, [f"hb{b}"], "f")
                for a in range(2):
                    em.I("act", "activation", reads=[f"hb{b}", "frstd"], writes=[f"u{a}"],
                         out=u[:, a, :], in_=hb[b][:, a, :], func=AF.Copy, scale=rstd[:, a:a + 1])
                for a in range(2):
                    tp = tpp[a]
                    for kt in range(8):
                        em.I("pe", "transpose", reads=[f"u{a}", "identb"], writes=[f"tpp{a}"],
                             out=tp[:, kt * 128:(kt + 1) * 128], in_=u[:, a, kt * 128:(kt + 1) * 128], identity=self.ident_b[:])
                    em.I("dve", "tensor_copy", reads=[f"tpp{a}"], writes=["uT"],
                         out=uT[:, :, a * 128:(a + 1) * 128], in_=tp[:].rearrange("p (k t) -> p k t", k=8))
                for ft in range(32):
                    pb = ft % 2
                    for kt in range(8):
                        em.I("pe", "matmul", reads=[f"Wup{kt * 2 + ft // 16}", "uT"], writes=[f"upp{pb}"],
                             out=upp[pb][:, 0:ST], lhsT=Wup[:, kt, ft * 128:(ft + 1) * 128], rhs=uT[:, kt, :],
                             start=(kt == 0), stop=(kt == 7))
                    em.I("act", "activation", reads=[f"upp{pb}"], writes=[f"r{pb}"], out=r[pb][:], in_=upp[pb][:, 0:ST], func=AF.Relu)
                    em.I("dve", "tensor_tensor", reads=[f"r{pb}"], writes=[f"hid{ft}"],
                         out=hid[:, ft, :], in0=r[pb][:], in1=r[pb][:], op=ALU.mult)
                for a in range(2):
                    for nch in range(2):
                        pb = nch
                        for ft in range(32):
                            em.I("pe", "matmul", reads=[f"Wdn{ft // 2}", f"hid{ft}"], writes=[f"dnp{pb}"],
                                 out=dnp[pb][:], lhsT=hid[:, ft, a * 128:(a + 1) * 128],
                                 rhs=Wdn[:, ft, nch * 512:(nch + 1) * 512], start=(ft == 0), stop=(ft == 31))
                        em.I("dve", "tensor_tensor", reads=[f"dnp{pb}", f"hb{b}"], writes=[f"hb{b}"],
                             out=hb[b][:, a, nch * 512:(nch + 1) * 512], in0=dnp[pb][:],
                             in1=hb[b][:, a, nch * 512:(nch + 1) * 512], op=ALU.add)
                em.dma(STORE_Q, self.out[sti * ST:(sti + 1) * ST, :].rearrange("(a p) d -> p a d", p=128), hb[b][:],
                       reads=[f"hb{b}"], writes=[f"out{2 * sti}", f"out{2 * sti + 1}"])


    def phase_attn(self, L, src="out"):
        em, nc, S, w, NT = self.em, self.nc, self.S, self.w, self.NT
        j = L // 2
        lam_init = 0.8 - 0.6 * math.exp(-0.3 * L)
        src_t = self.src_ap(src)
        with contextlib.ExitStack() as st:
            self.em.barrier()
            Wq = self.sb(st, [128, 8, 3072], BF16, "Wq")
            gain = self.sb(st, [128, 8], F32, "gain")
            stgt = self.sb(st, [128, 2, 1536], F32, "stg")
            stg = [stgt[:, 0, :], stgt[:, 1, :]]
            hb = [self.sb(st, [128, D], F32, "hb") for _ in range(2)]
            rp = [self.sb(st, [128, 64], F32, "rp") for _ in range(2)]
            gq = self.sb(st, [128, 2, 64], F32, "gq")
            u_ = [self.sb(st, [128, D], BF16, "u") for _ in range(2)]
            uT_ = [self.sb(st, [128, 8, 128], BF16, "uT") for _ in range(2)]
            qk_ = [self.sb(st, [128, 2048], F32, "qk") for _ in range(2)]
            sq_ = [self.sb(st, [128, 2048], F32, "sq") for _ in range(2)]
            qr_ = [self.sb(st, [128, 2048], BF16, "qr") for _ in range(2)]
            vt = [self.sb(st, [128, D], BF16, "vt") for _ in range(2)]
            qkT = [self.sb(st, [128, 16, 128], BF16, "qkT") for _ in range(2)]
            t1_ = [self.sb(st, [128, 32, 32], F32, "t1") for _ in range(2)]
            t2_ = [self.sb(st, [128, 32, 32], F32, "t2") for _ in range(2)]
            t3_ = [self.sb(st, [128, 32, 32], F32, "t3") for _ in range(2)]
            t4_ = [self.sb(st, [128, 32, 32], F32, "t4") for _ in range(2)]
            junk = self.sb(st, [128, D], BF16, "junk")
            ss = self.sb(st, [128, 1], F32, "ss")
            ms = self.sb(st, [128, 1], F32, "ms")
            sd = self.sb(st, [128, 1], F32, "sd")
            rstd = self.sb(st, [128, 1], F32, "rstd")
            m32_ = [self.sb(st, [128, 32], F32, "m32") for _ in range(2)]
            s32_ = [self.sb(st, [128, 32], F32, "s32") for _ in range(2)]
            r32_ = [self.sb(st, [128, 32], F32, "r32") for _ in range(2)]
            tpp_ = [self.ps(st, [128, 1024], BF16, "tpp") for _ in range(2)]
            mp = [self.ps(st, [128, 512], F32, "mp") for _ in range(2)]
            tq = self.ps(st, [128, 2048], BF16, "tq")
            em.dma("sp", gain[:], w["norm_mix"][L].rearrange("(kt p) -> p kt", p=128), writes=["gain"], slow=True)
            em.dma("sp", gq[:, 0, :], w["attn_q_gain"][j].partition_broadcast(128), writes=["gq0"], slow=True)
            em.dma("sp", gq[:, 1, :], w["attn_k_gain"][j].partition_broadcast(128), writes=["gq1"], slow=True)
            wq = w["attn_w_qkv"][j]
            self.load_weight([Wq[:, i // 2, (i % 2) * 1536:(i % 2 + 1) * 1536] for i in range(16)],
                             [wq[(i // 2) * 128:(i // 2 + 1) * 128, (i % 2) * 1536:(i % 2 + 1) * 1536] for i in range(16)],
                             stg, [gain[:, i // 2:i // 2 + 1] for i in range(16)], "Wq")

            def load(i):
                b = i % 2
                em.dma("sp", hb[b][:], src_t[i * 128:(i + 1) * 128, :], reads=[f"out{i}"] if src == "out" else [], writes=[f"hb{b}"])
                em.dma("sp", rp[b][:], self.rope_d[i * 128:(i + 1) * 128, :], writes=[f"rp{b}"])

            def stage_A(i):
                    b = i % 2
                    if i + 1 < NT:
                        load(i + 1)
                    u, uT, qk, sq, qr, tpp = u_[b], uT_[b], qk_[b], sq_[b], qr_[b], tpp_[b]
                    t1, t2, t3, t4, m32, s32, r32 = t1_[b], t2_[b], t3_[b], t4_[b], m32_[b], s32_[b], r32_[b]
                    self.rms_rstd([hb[b][:]], junk, ss, ms, sd, rstd, [f"hb{b}"], "a")
                    em.I("act", "activation", reads=[f"hb{b}", "arstd"], writes=[f"u{b}"], out=u[:], in_=hb[b][:], func=AF.Copy, scale=rstd[:, 0:1])
                    for kt in range(8):
                        em.I("pe", "transpose", reads=[f"u{b}", "identb"], writes=[f"tpp{b}"], out=tpp[:, kt * 128:(kt + 1) * 128],
                             in_=u[:, kt * 128:(kt + 1) * 128], identity=self.ident_b[:])
                    em.I("dve", "tensor_copy", reads=[f"tpp{b}"], writes=[f"uT{b}"], out=uT[:], in_=tpp[:].rearrange("p (k t) -> p k t", k=8))
                    for c in range(6):
                        pb = c % 2
                        for kt in range(8):
                            em.I("pe", "matmul", reads=[f"Wq{kt * 2 + c // 3}", f"uT{b}"], writes=[f"mp{pb}"], out=mp[pb][:], lhsT=uT[:, kt, :],
                                 rhs=Wq[:, kt, c * 512:(c + 1) * 512], start=(kt == 0), stop=(kt == 7))
                        if c < 4:
                            em.I("act", "activation", reads=[f"mp{pb}"], writes=[f"qk{b}_{c}"], out=qk[:, c * 512:(c + 1) * 512], in_=mp[pb][:], func=AF.Copy)
                        else:
                            em.I("act", "activation", reads=[f"mp{pb}"], writes=[f"vt{b}"], out=vt[b][:, (c - 4) * 512:(c - 3) * 512], in_=mp[pb][:], func=AF.Copy)
                    em.dma(STORE_Q, self.v_s[i * 128:(i + 1) * 128, :], vt[b][:], reads=[f"vt{b}"], writes=[f"vs{i}"])

            def stage_B(i):
                    b = i % 2
                    u, uT, qk, sq, qr, tpp = u_[b], uT_[b], qk_[b], sq_[b], qr_[b], tpp_[b]
                    t1, t2, t3, t4, m32, s32, r32 = t1_[b], t2_[b], t3_[b], t4_[b], m32_[b], s32_[b], r32_[b]
                    qkk = [f"qk{b}_{c}" for c in range(4)]
                    em.I("act", "activation", reads=qkk, writes=[f"sq{b}"], out=sq[:], in_=qk[:], func=AF.Square)
                    em.I("dve", "tensor_reduce", reads=[f"sq{b}"], writes=[f"m32{b}"], out=m32[:], in_=sq[:].rearrange("p (g d) -> p g d", d=64), axis=AX.X, op=ALU.add)
                    em.I("dve", "tensor_scalar", reads=[f"m32{b}"], writes=[f"m32{b}"], out=m32[:], in0=m32[:], scalar1=1.0 / 64, scalar2=EPS, op0=ALU.mult, op1=ALU.add)
                    em.I("act", "activation", reads=[f"m32{b}"], writes=[f"s32{b}"], out=s32[:], in_=m32[:], func=AF.Sqrt)
                    em.I("dve", "reciprocal", reads=[f"s32{b}"], writes=[f"r32{b}"], out=r32[:], in_=s32[:])
                    qk3 = qk[:].rearrange("p (g d) -> p g d", d=64)
                    em.I("dve", "tensor_tensor", reads=qkk + [f"r32{b}"], writes=[f"qkn{b}"], out=qk3, in0=qk3, in1=r32[:].unsqueeze(2).to_broadcast([128, 32, 64]), op=ALU.mult)
                    qk4 = qk[:].rearrange("p (a g d) -> p a g d", a=2, d=64)
                    em.I("dve", "tensor_tensor", reads=[f"qkn{b}", "gq0", "gq1"], writes=[f"qkn{b}"], out=qk4, in0=qk4, in1=gq[:].unsqueeze(2).to_broadcast([128, 2, 16, 64]), op=ALU.mult)
                    x1 = qk3[:, :, 0:32]
                    x2 = qk3[:, :, 32:64]
                    cosb = rp[b][:, 0:32].unsqueeze(1).to_broadcast([128, 32, 32])
                    sinb = rp[b][:, 32:64].unsqueeze(1).to_broadcast([128, 32, 32])
                    qr3 = qr[:].rearrange("p (g d) -> p g d", d=64)
                    em.I("dve", "tensor_tensor", reads=[f"qkn{b}", f"rp{b}"], writes=[f"t1{b}"], out=t1[:], in0=x1, in1=cosb, op=ALU.mult)
                    em.I("dve", "tensor_tensor", reads=[f"qkn{b}", f"rp{b}"], writes=[f"t2{b}"], out=t2[:], in0=x2, in1=sinb, op=ALU.mult)
                    em.I("dve", "tensor_tensor", reads=[f"t1{b}", f"t2{b}"], writes=[f"qr1{b}"], out=qr3[:, :, 0:32], in0=t1[:], in1=t2[:], op=ALU.subtract)
                    em.I("pool", "tensor_tensor", reads=[f"qkn{b}", f"rp{b}"], writes=[f"t3{b}"], out=t3[:], in0=x2, in1=cosb, op=ALU.mult)
                    em.I("dve", "tensor_tensor", reads=[f"qkn{b}", f"rp{b}"], writes=[f"t4{b}"], out=t4[:], in0=x1, in1=sinb, op=ALU.mult)
                    em.I("dve", "tensor_tensor", reads=[f"t3{b}", f"t4{b}"], writes=[f"qr2{b}"], out=qr3[:, :, 32:64], in0=t3[:], in1=t4[:], op=ALU.add)

            def stage_C(i):
                    b = i % 2
                    u, uT, qk, sq, qr, tpp = u_[b], uT_[b], qk_[b], sq_[b], qr_[b], tpp_[b]
                    t1, t2, t3, t4, m32, s32, r32 = t1_[b], t2_[b], t3_[b], t4_[b], m32_[b], s32_[b], r32_[b]
                    for jj in range(16):
                        em.I("pe", "transpose", reads=[f"qr1{b}", f"qr2{b}", "identb"], writes=["tq"], out=tq[:, jj * 128:(jj + 1) * 128],
                             in_=qr[:, jj * 128:(jj + 1) * 128], identity=self.ident_b[:])
                    em.I("act", "activation", reads=["tq"], writes=[f"qkT{b}"], out=qkT[b][:], in_=tq[:].rearrange("p (k t) -> p k t", k=16), func=AF.Copy)
                    em.dma(STORE_Q, self.qT_s[:, :, i * 128:(i + 1) * 128].rearrange("h p t -> p h t"), qkT[b][:, 0:8, :],
                           reads=[f"qkT{b}"], writes=[f"qTs{i}"])
                    em.dma(STORE_Q, self.kT_s[:, :, i * 128:(i + 1) * 128].rearrange("h p t -> p h t"), qkT[b][:, 8:16, :],
                           reads=[f"qkT{b}"], writes=[f"kTs{i}"])

            load(0)
            for i in range(NT + 1):
                if i < NT:
                    stage_A(i)
                if i >= 1:
                    stage_C(i - 1)
                if i < NT:
                    stage_B(i)
        QC = 512
        NQC = S // QC
        with contextlib.ExitStack() as st:
            self.em.barrier()
            kTh = [self.sb(st, [128, S], BF16, "kTh") for _ in range(2)]
            qTh = [self.sb(st, [128, S], BF16, "qTh") for _ in range(2)]
            vh = [self.sb(st, [128, NT, 128], BF16, "vh") for _ in range(2)]
            pT = [self.sb(st, [128, 1024], BF16, "pT") for _ in range(3)]
            onesb = self.sb(st, [128, 128], BF16, "onesb")
            onesf = self.sb(st, [128, 128], F32, "onesf")
            lv = self.sb(st, [128, 4, 64], F32, "lv")
            lp = self.sb(st, [128, 64], F32, "lp")
            lsm = self.sb(st, [128, 2], F32, "lsm")
            lex = self.sb(st, [128, 2], F32, "lex")
            neglam = self.sb(st, [128, 1], F32, "neglam")
            gsc = self.sb(st, [128, 1], F32, "gsc")
            R = self.sb(st, [128, 1024], F32, "R")
            o0 = self.sb(st, [128, QC], F32, "o0")
            o1 = self.sb(st, [128, QC], F32, "o1")
            od = self.sb(st, [128, QC], F32, "od")
            osq = self.sb(st, [128, QC], F32, "osq")
            rs = self.sb(st, [128, QC], F32, "rs")
            oTt = [self.sb(st, [128, QC], BF16, "oTt") for _ in range(2)]
            sc = [self.ps(st, [128, 2, 512], F32, "sc") for _ in range(2)]
            O = [self.ps(st, [128, 512], F32, "O") for _ in range(2)]
            Lp = [self.ps(st, [128, 512], F32, "Lp") for _ in range(2)]
            acc = [self.sb(st, [128, 512], F32, "acc") for _ in range(2)]
            lsum = self.sb(st, [128, 512], F32, "lsum")
            em.I("pool", "memset", writes=["onesb"], ap=onesb[:], constant=1.0)
            em.I("pool", "memset", writes=["onesf"], ap=onesf[:], constant=1.0 / 128)
            onesf1 = self.sb(st, [128, 128], F32, "onesf1")
            em.I("pool", "memset", writes=["onesf1"], ap=onesf1[:], constant=1.0)
            em.dma("sp", lv[:], w["attn_lambda"][j].partition_broadcast(128), writes=["lv"], slow=True)
            em.dma("sp", gsc[:], w["attn_subln"][j].rearrange("(p o) -> p o", o=1), writes=["gsc"], slow=True)
            for k in range(2):
                em.I("dve", "tensor_tensor", reads=["lv"], writes=["lp"], out=lp[:], in0=lv[:, 2 * k, :], in1=lv[:, 2 * k + 1, :], op=ALU.mult)
                em.I("dve", "tensor_reduce", reads=["lp"], writes=["lsm"], out=lsm[:, k:k + 1], in_=lp[:], axis=AX.X, op=ALU.add)
            em.I("act", "activation", reads=["lsm"], writes=["lex"], out=lex[:], in_=lsm[:], func=AF.Exp)
            em.I("dve", "tensor_tensor", reads=["lex"], writes=["neglam"], out=neglam[:], in0=lex[:, 1:2], in1=lex[:, 0:1], op=ALU.subtract)
            em.I("dve", "tensor_scalar", reads=["neglam"], writes=["neglam"], out=neglam[:], in0=neglam[:], scalar1=-lam_init, scalar2=None, op0=ALU.add)
            em.I("dve", "tensor_scalar", reads=["gsc"], writes=["gsc"], out=gsc[:], in0=gsc[:], scalar1=(1.0 - lam_init), scalar2=None, op0=ALU.mult)

            def loadh(h):
                hb_ = h % 2
                em.dma("sp", kTh[hb_][:], self.kT_s[h], reads=[f"kTs{i}" for i in range(NT)], writes=[f"kTh{hb_}"])
                em.dma("sp", qTh[hb_][:], self.qT_s[h], reads=[f"qTs{i}" for i in range(NT)], writes=[f"qTh{hb_}"])
                em.dma("sp", vh[hb_][:], self.v_s[:, h * 128:(h + 1) * 128].rearrange("(kt p) e -> p kt e", p=128),
                       reads=[f"vs{i}" for i in range(NT)], writes=[f"vh{hb_}"], slow=True)

            loadh(0)
            it = 0
            for h in range(NH):
                hb_ = h % 2
                if h + 1 < NH:
                    loadh(h + 1)
                for qc in range(NQC):
                    qs = slice(qc * QC, (qc + 1) * QC)

                    def emit_S(kt, itn):
                        for c in range(2):
                            em.I("pe", "matmul", reads=[f"kTh{hb_}", f"qTh{hb_}"], writes=[f"sc{itn % 2}"], out=sc[itn % 2][:, c, :],
                                 lhsT=kTh[hb_][64 * c:64 * c + 64, kt * 128:(kt + 1) * 128], rhs=qTh[hb_][64 * c:64 * c + 64, qs],
                                 start=True, stop=True)

                    emit_S(0, it)
                    for kt in range(NT):
                        if kt + 1 < NT:
                            emit_S(kt + 1, it + 1)
                        pi = it % 3
                        em.I("act", "activation", reads=[f"sc{it % 2}"], writes=[f"pT{pi}"], out=pT[pi][:],
                             in_=sc[it % 2][:].rearrange("p c q -> p (c q)"), func=AF.Exp, scale=0.125)
                        for c in range(2):
                            em.I("pe", "matmul", reads=[f"vh{hb_}", f"pT{pi}"], writes=[f"O{c}"], out=O[c][:], lhsT=vh[hb_][:, kt, :],
                                 rhs=pT[pi][:, c * 512:(c + 1) * 512], start=(kt == 0), stop=(kt == NT - 1))
                        em.I("pe", "matmul", reads=["onesb", f"pT{pi}"], writes=["L0"], out=Lp[0][:], lhsT=onesb[:],
                             rhs=pT[pi][:, 0:512], start=(kt == 0), stop=(kt == NT - 1))
                        par = kt % 2
                        if kt < 2:
                            em.I("dve", "tensor_copy", reads=[f"pT{pi}"], writes=[f"acc{par}"], out=acc[par][:], in_=pT[pi][:, 512:1024])
                        else:
                            em.I("dve", "tensor_tensor", reads=[f"pT{pi}", f"acc{par}"], writes=[f"acc{par}"], nosync=True,
                                 out=acc[par][:], in0=acc[par][:], in1=pT[pi][:, 512:1024], op=ALU.add)
                        it += 1
                    if NT >= 2:
                        em.I("dve", "tensor_tensor", reads=["acc0", "acc1"], writes=["lsum"], out=lsum[:], in0=acc[0][:], in1=acc[1][:], op=ALU.add)
                    else:
                        em.I("dve", "tensor_copy", reads=["acc0"], writes=["lsum"], out=lsum[:], in_=acc[0][:])
                    em.I("pe", "matmul", reads=["onesf1", "lsum"], writes=["L1"], out=Lp[1][:], lhsT=onesf1[:], rhs=lsum[:], start=True, stop=True)
                    for c in range(2):
                        em.I("dve", "reciprocal", reads=[f"L{c}"], writes=[f"R{c}"], out=R[:, c * 512:(c + 1) * 512], in_=Lp[c][:])
                    em.I("dve", "tensor_tensor", reads=["O0", "R0"], writes=["o0"], out=o0[:], in0=O[0][:], in1=R[:, 0:512], op=ALU.mult)
                    em.I("dve", "tensor_tensor", reads=["O1", "R1"], writes=["o1"], out=o1[:], in0=O[1][:], in1=R[:, 512:1024], op=ALU.mult)
                    em.I("dve", "scalar_tensor_tensor", reads=["o0", "o1", "neglam"], writes=["od"], out=od[:], in0=o1[:], scalar=neglam[:, 0:1],
                         in1=o0[:], op0=ALU.mult, op1=ALU.add)
                    em.I("pool", "tensor_tensor", reads=["od"], writes=["osq"], out=osq[:], in0=od[:], in1=od[:], op=ALU.mult)
                    em.I("pe", "matmul", reads=["onesf", "osq", "R0"], writes=["L0"], out=Lp[0][:], lhsT=onesf[:], rhs=osq[:], start=True, stop=True)
                    em.I("dve", "tensor_scalar", reads=["L0"], writes=["rs"], out=rs[:], in0=Lp[0][:], scalar1=EPS, scalar2=None, op0=ALU.add)
                    em.I("act", "activation", reads=["rs"], writes=["rs"], out=rs[:], in_=rs[:], func=AF.Ln)
                    em.I("act", "activation", reads=["rs"], writes=["rs"], out=rs[:], in_=rs[:], func=AF.Exp, scale=-0.5)
                    ob = (h * NQC + qc) % 2
                    em.I("dve", "scalar_tensor_tensor", reads=["od", "gsc", "rs"], writes=[f"oTt{ob}"], out=oTt[ob][:], in0=od[:], scalar=gsc[:, 0:1],
                         in1=rs[:], op0=ALU.mult, op1=ALU.mult)
                    em.dma(STORE_Q, self.oT_s[h, :, qs], oTt[ob][:], reads=[f"oTt{ob}"], writes=[f"oTs{h}_{qc}"])
        with contextlib.ExitStack() as st:
            self.em.barrier()
            Wo = self.sb(st, [128, 8, D], BF16, "Wo")
            stgt = self.sb(st, [128, 2, 2048], F32, "stg")
            stg3 = [stgt[:, 0, :].rearrange("p (a d) -> p a d", a=2), stgt[:, 1, :].rearrange("p (a d) -> p a d", a=2)]
            hb = [self.sb(st, [128, D], F32, "hb") for _ in range(2)]
            oTi = [self.sb(st, [128, 8, 128], BF16, "oTi") for _ in range(2)]
            wop = [self.ps(st, [128, 512], F32, "wop") for _ in range(2)]
            wo = w["attn_w_o"][j]
            self.load_weight([Wo[:, 2 * i:2 * i + 2, :] for i in range(4)],
                             [wo[i * 256:(i + 1) * 256, :].rearrange("(a p) d -> p a d", p=128) for i in range(4)], stg3, None, "Wo")

            def load3(i):
                b = i % 2
                em.dma("sp", hb[b][:], src_t[i * 128:(i + 1) * 128, :], reads=[f"out{i}"] if src == "out" else [], writes=[f"hb{b}"])
                qcs = (i * 128) // QC
                em.dma("sp", oTi[b][:], self.oT_s[:, :, i * 128:(i + 1) * 128].rearrange("h p t -> p h t"),
                       reads=[f"oTs{h}_{qcs}" for h in range(NH)], writes=[f"oTi{b}"])

            load3(0)
            for i in range(NT):
                b = i % 2
                if i + 1 < NT:
                    load3(i + 1)
                for nch in range(2):
                    for h in range(NH):
                        em.I("pe", "matmul", reads=[f"Wo{h // 2}", f"oTi{b}"], writes=[f"wop{nch}"], out=wop[nch][:], lhsT=oTi[b][:, h, :],
                             rhs=Wo[:, h, nch * 512:(nch + 1) * 512], start=(h == 0), stop=(h == NH - 1))
                    em.I("dve", "tensor_tensor", reads=[f"wop{nch}", f"hb{b}"], writes=[f"hb{b}"], out=hb[b][:, nch * 512:(nch + 1) * 512], in0=wop[nch][:],
                         in1=hb[b][:, nch * 512:(nch + 1) * 512], op=ALU.add)
                em.dma(STORE_Q, self.out[i * 128:(i + 1) * 128, :], hb[b][:], reads=[f"hb{b}"], writes=[f"out{i}"])

    def s5_prep_dir(self, st, j, r, T):
        em, nc, w = self.em, self.nc, self.w
        kp = "s5p_"
        are, aim, ldt = T["are"], T["aim"], T["ldt"]
        em.dma("sp", are[:], w["ssm_a_re"][j, r].rearrange("(gp gpar) p -> (gpar p) gp", gpar=2), writes=[kp + "are"], slow=True)
        em.dma("sp", aim[:], w["ssm_a_im"][j, r].rearrange("(gp gpar) p -> (gpar p) gp", gpar=2), writes=[kp + "aim"], slow=True)
        for gpar in range(2):
            em.dma("sp", ldt[gpar * 64:(gpar + 1) * 64, :],
                   w["ssm_log_dt"][j, r].rearrange("(gp gpar) -> gpar gp", gpar=2)[gpar].partition_broadcast(64),
                   writes=[kp + f"ldt{gpar}"], slow=True)
        dt_, rho, th = T["dt"], T["rho"], T["th"]
        em.I("act", "activation", reads=[kp + "ldt0", kp + "ldt1"], writes=[kp + "dt"], out=dt_[:], in_=ldt[:], func=AF.Exp)
        em.I("dve", "tensor_tensor", reads=[kp + "dt", kp + "are"], writes=[kp + "rho"], out=rho[:], in0=are[:], in1=dt_[:], op=ALU.mult)
        em.I("dve", "tensor_tensor", reads=[kp + "dt", kp + "aim"], writes=[kp + "th"], out=th[:], in0=aim[:], in1=dt_[:], op=ALU.mult)
        mag = T["mag"]
        em.I("act", "activation", reads=[kp + "rho"], writes=[kp + "mag"], out=mag[:], in_=rho[:], func=AF.Exp)
        kf, z = T["kf"], T["z"]
        MAGIC = 12582912.0
        em.I("dve", "tensor_scalar", reads=[kp + "th"], writes=[kp + "kf"], out=kf[:], in0=th[:], scalar1=1.0 / (2 * math.pi), scalar2=MAGIC,
             op0=ALU.mult, op1=ALU.add)
        em.I("dve", "tensor_scalar", reads=[kp + "kf"], writes=[kp + "kf"], out=kf[:], in0=kf[:], scalar1=-MAGIC, scalar2=None, op0=ALU.add)
        em.I("dve", "scalar_tensor_tensor", reads=[kp + "kf", kp + "th"], writes=[kp + "z"], out=z[:], in0=kf[:], scalar=-2 * math.pi, in1=th[:],
             op0=ALU.mult, op1=ALU.add)
        sw, sw2, cw = T["sw"], T["sw2"], T["cw"]
        em.I("act", "activation", reads=[kp + "z"], writes=[kp + "sw"], out=sw[:], in_=z[:], func=AF.Sin, scale=0.5)
        em.I("act", "activation", reads=[kp + "z"], writes=[kp + "sw2"], out=sw2[:], in_=z[:], func=AF.Sin, scale=0.25)
        em.I("dve", "tensor_tensor", reads=[kp + "sw2"], writes=[kp + "cw"], out=cw[:], in0=sw2[:], in1=sw2[:], op=ALU.mult)
        em.I("dve", "tensor_scalar", reads=[kp + "cw"], writes=[kp + "cw"], out=cw[:], in0=cw[:], scalar1=-2.0, scalar2=1.0, op0=ALU.mult, op1=ALU.add)
        sn, cs = T["sn"], T["cs"]
        em.I("dve", "scalar_tensor_tensor", reads=[kp + "sw", kp + "cw"], writes=[kp + "sn"], out=sn[:], in0=sw[:], scalar=2.0, in1=cw[:], op0=ALU.mult, op1=ALU.mult)
        em.I("dve", "tensor_tensor", reads=[kp + "sw"], writes=[kp + "cs"], out=cs[:], in0=sw[:], in1=sw[:], op=ALU.mult)
        em.I("dve", "tensor_scalar", reads=[kp + "cs"], writes=[kp + "cs"], out=cs[:], in0=cs[:], scalar1=-2.0, scalar2=1.0, op0=ALU.mult, op1=ALU.add)
        ar, ai = T["ar"], T["ai"]
        em.I("dve", "tensor_tensor", reads=[kp + "mag", kp + "cs"], writes=[kp + "ar"], out=ar[:], in0=mag[:], in1=cs[:], op=ALU.mult)
        em.I("dve", "tensor_tensor", reads=[kp + "mag", kp + "sn"], writes=[kp + "ai"], out=ai[:], in0=mag[:], in1=sn[:], op=ALU.mult)
        C1, C2 = T["C1"], T["C2"]
        o_ = r * 64
        em.I("dve", "tensor_copy", reads=[kp + "ar"], writes=["C1"], out=C1[:, o_:o_ + 32], in_=ar[:])
        em.I("dve", "tensor_copy", reads=[kp + "ar"], writes=["C1"], out=C1[:, o_ + 32:o_ + 64], in_=ar[:])
        em.I("dve", "tensor_scalar", reads=[kp + "ai"], writes=["C2"], out=C2[:, o_:o_ + 32], in0=ai[:], scalar1=-1.0, scalar2=None, op0=ALU.mult)
        em.I("dve", "tensor_copy", reads=[kp + "ai"], writes=["C2"], out=C2[:, o_ + 32:o_ + 64], in_=ai[:])
        nr, den, t1, t2, qre, qim = T["nr"], T["den"], T["t1"], T["t2"], T["qre"], T["qim"]
        em.I("dve", "tensor_scalar", reads=[kp + "ar"], writes=[kp + "nr"], out=nr[:], in0=ar[:], scalar1=-1.0, scalar2=None, op0=ALU.add)
        em.I("dve", "tensor_tensor", reads=[kp + "are"], writes=[kp + "den"], out=den[:], in0=are[:], in1=are[:], op=ALU.mult)
        em.I("dve", "tensor_tensor", reads=[kp + "aim"], writes=[kp + "t1"], out=t1[:], in0=aim[:], in1=aim[:], op=ALU.mult)
        em.I("dve", "tensor_tensor", reads=[kp + "den", kp + "t1"], writes=[kp + "den"], out=den[:], in0=den[:], in1=t1[:], op=ALU.add)
        em.I("dve", "reciprocal", reads=[kp + "den"], writes=[kp + "den"], out=den[:], in_=den[:])
        em.I("dve", "tensor_tensor", reads=[kp + "nr"], writes=[kp + "t1"], out=t1[:], in0=nr[:], in1=are[:], op=ALU.mult)
        em.I("dve", "tensor_tensor", reads=[kp + "ai"], writes=[kp + "t2"], out=t2[:], in0=ai[:], in1=aim[:], op=ALU.mult)
        em.I("dve", "tensor_tensor", reads=[kp + "t1", kp + "t2"], writes=[kp + "qre"], out=qre[:], in0=t1[:], in1=t2[:], op=ALU.add)
        em.I("dve", "tensor_tensor", reads=[kp + "qre", kp + "den"], writes=[kp + "qre"], out=qre[:], in0=qre[:], in1=den[:], op=ALU.mult)
        em.I("dve", "tensor_tensor", reads=[kp + "ai"], writes=[kp + "t1"], out=t1[:], in0=ai[:], in1=are[:], op=ALU.mult)
        em.I("dve", "tensor_tensor", reads=[kp + "nr"], writes=[kp + "t2"], out=t2[:], in0=nr[:], in1=aim[:], op=ALU.mult)
        em.I("dve", "tensor_tensor", reads=[kp + "t1", kp + "t2"], writes=[kp + "qim"], out=qim[:], in0=t1[:], in1=t2[:], op=ALU.subtract)
        em.I("dve", "tensor_tensor", reads=[kp + "qim", kp + "den"], writes=[kp + "qim"], out=qim[:], in0=qim[:], in1=den[:], op=ALU.mult)
        if r == 1 and self.dbg == 3:
            for nm in ("th", "z", "sn", "cs", "ar", "ai", "qre", "qim", "mag", "dt"):
                self.dump(nm, T[nm][:], [kp + nm])
            self.dump("C1", C1[:], ["C1"])
            self.dump("C2", C2[:], ["C2"])
        bre, bim, bbr, bbi, tb = T["bre"], T["bim"], T["bbr"], T["bbi"], T["tb"]
        em.dma("sp", bre[:], w["ssm_b_re"][j, r].rearrange("(gp gpar) p hi -> (gpar p) gp hi", gpar=2), writes=[kp + "bre"], slow=True)
        em.dma("sp", bim[:], w["ssm_b_im"][j, r].rearrange("(gp gpar) p hi -> (gpar p) gp hi", gpar=2), writes=[kp + "bim"], slow=True)
        qre_b = qre[:].unsqueeze(2).to_broadcast([128, 32, 16])
        qim_b = qim[:].unsqueeze(2).to_broadcast([128, 32, 16])
        em.I("dve", "tensor_tensor", reads=[kp + "bre", kp + "qre"], writes=[kp + "bbr"], out=bbr[:], in0=bre[:], in1=qre_b, op=ALU.mult)
        em.I("dve", "tensor_tensor", reads=[kp + "bim", kp + "qim"], writes=[kp + "tb"], out=tb[:], in0=bim[:], in1=qim_b, op=ALU.mult)
        em.I("dve", "tensor_tensor", reads=[kp + "bbr", kp + "tb"], writes=[kp + "bbr"], out=bbr[:], in0=bbr[:], in1=tb[:], op=ALU.subtract)
        em.I("dve", "tensor_tensor", reads=[kp + "bim", kp + "qre"], writes=[kp + "bbi"], out=bbi[:], in0=bim[:], in1=qre_b, op=ALU.mult)
        em.I("dve", "tensor_tensor", reads=[kp + "bre", kp + "qim"], writes=[kp + "tb"], out=tb[:], in0=bre[:], in1=qim_b, op=ALU.mult)
        em.I("dve", "tensor_tensor", reads=[kp + "bbi", kp + "tb"], writes=[kp + "bbi"], out=bbi[:], in0=bbi[:], in1=tb[:], op=ALU.add)
        Bq, WB, tps = T["Bq"], T["WB"][r], T["tps"]
        for ri, bb in enumerate((bbr, bbi)):
            for gp in range(32):
                k = gp % 4
                col = ri * 32 + gp
                em.I("dve", "tensor_copy", reads=[kp + "bbr", kp + "bbi", f"Bqz{k}"], writes=[f"Bq{k}"],
                     out=Bq[k][0:64, 32 * k:32 * k + 16], in_=bb[0:64, gp, :])
                em.I("dve", "tensor_copy", reads=[kp + "bbr", kp + "bbi"], writes=[f"Bq{k}"],
                     out=Bq[k][64:128, 32 * k + 16:32 * k + 32], in_=bb[64:128, gp, :])
                pb = col % 2
                em.I("pe", "transpose", reads=[f"Bq{k}", "identf"], writes=[f"tps{pb}"], out=tps[pb][:, 0:128], in_=Bq[k][:], identity=self.ident_f[:])
                em.I("act", "activation", reads=[f"tps{pb}"], writes=[f"WB{r}_{col}"], out=WB[:, col, :], in_=tps[pb][:, 0:128], func=AF.Copy)
        if r == 1 and self.dbg == 3:
            self.dump("bbr", bbr[:], [kp + "bbr"])
            self.dump("bbi", bbi[:], [kp + "bbi"])
            self.dump("WB", WB[:], [f"WB{r}_{c}" for c in range(64)])
        cin, Cq, WC = T["cin"], T["Cq"], T["WC"][r]
        for ri, nm in enumerate(("ssm_c_re", "ssm_c_im")):
            csrc = w[nm][j, r].rearrange("(gp gpar) ho p -> gp ho gpar p", gpar=2)
            for i in range(4):
                b = (ri * 4 + i) % 2
                for gl in range(8):
                    em.dma("sp", cin[b][gl * 16:(gl + 1) * 16, :].rearrange("ho (gpar p) -> ho gpar p", gpar=2), csrc[8 * i + gl],
                           writes=[f"cin{b}_{gl}"])
                em.I("pe", "transpose", reads=[f"cin{b}_{gl}" for gl in range(8)] + ["identf"], writes=[f"tps{b}"], out=tps[b][:, 0:128], in_=cin[b][:], identity=self.ident_f[:])
                em.I("act", "activation", reads=[f"tps{b}"], writes=[kp + f"Cq{ri}"], out=Cq[ri][:, 8 * i:8 * i + 8, :],
                     in_=tps[b][:, 0:128].rearrange("q (g h) -> q g h", h=16), func=AF.Copy, scale=(1.0 if ri == 0 else -1.0))
            for k in range(4):
                c0 = 32 * k
                em.I("dve", "tensor_copy", reads=[kp + f"Cq{ri}", "WCz"], writes=[f"WC{r}_{ri}"],
                     out=WC[0:64, ri * 32 + k:ri * 32 + 32:4, c0:c0 + 16], in_=Cq[ri][0:64, k:32:4, :])
                em.I("dve", "tensor_copy", reads=[kp + f"Cq{ri}", "WCz"], writes=[f"WC{r}_{ri}"],
                     out=WC[64:128, ri * 32 + k:ri * 32 + 32:4, c0 + 16:c0 + 32], in_=Cq[ri][64:128, k:32:4, :])

    def phase_s5(self, L):
        em, nc, S, w = self.em, self.nc, self.S, self.w
        j = L // 2
        NT = self.NT
        GC = 2.0 * math.sqrt(2.0 / math.pi)
        with contextlib.ExitStack() as st:
            self.em.barrier()
            G = self.sb(st, [128, D], F32, "G")
            hb = [self.sb(st, [128, D], F32, "hb") for _ in range(2)]
            un = self.sb(st, [128, D], F32, "un")
            u = self.sb(st, [128, D], BF16, "u")
            uTt = [self.sb(st, [128, 8, 128], BF16, "uTt") for _ in range(2)]
            junk = self.sb(st, [128, D], BF16, "junk")
            ss = self.sb(st, [128, 1], F32, "ss")
            ms = self.sb(st, [128, 1], F32, "ms")
            sd = self.sb(st, [128, 1], F32, "sd")
            rstd = self.sb(st, [128, 1], F32, "rstd")
            tpp = [self.ps(st, [128, 1024], BF16, "tpp") for _ in range(2)]
            rvp = [self.ps(st, [128, 1024], F32, "rvp") for _ in range(2)]
            uTrt = [self.sb(st, [128, 8, 128], BF16, "uTrt") for _ in range(2)]
            em.dma("sp", G[:], w["norm_mix"][L].partition_broadcast(128), writes=["G"], slow=True)
            em.dma("sp", hb[0][:], self.out[0:128, :], reads=["out0"], writes=["hb0"])
            for i in range(NT):
                b = i % 2
                if i + 1 < NT:
                    em.dma("sp", hb[1 - b][:], self.out[(i + 1) * 128:(i + 2) * 128, :], reads=[f"out{i + 1}"], writes=[f"hb{1 - b}"])
                self.rms_rstd([hb[b][:]], junk, ss, ms, sd, rstd, [f"hb{b}"], "p")
                em.I("act", "activation", reads=[f"hb{b}", "prstd"], writes=["un"], out=un[:], in_=hb[b][:], func=AF.Copy, scale=rstd[:, 0:1])
                em.I("dve", "tensor_tensor", reads=["un", "G"], writes=["u"], out=u[:], in0=un[:], in1=G[:], op=ALU.mult)
                for kt in range(8):
                    em.I("pe", "transpose", reads=["u", "identb"], writes=[f"tpp{b}"], out=tpp[b][:, kt * 128:(kt + 1) * 128],
                         in_=u[:, kt * 128:(kt + 1) * 128], identity=self.ident_b[:])
                em.I("dve", "tensor_copy", reads=[f"tpp{b}"], writes=[f"uTt{b}"], out=uTt[b][:], in_=tpp[b][:].rearrange("p (k t) -> p k t", k=8))
                em.dma(STORE_Q, self.uT_s[:, :, i * 128:(i + 1) * 128].rearrange("k p t -> p k t"), uTt[b][:],
                       reads=[f"uTt{b}"], writes=[f"uTs{i}"])
                for kt in range(8):
                    em.I("pe", "matmul", reads=["u", "antib"], writes=[f"rvp{b}"], out=rvp[b][:, kt * 128:(kt + 1) * 128],
                         lhsT=u[:, kt * 128:(kt + 1) * 128], rhs=self.anti_b[:], start=True, stop=True)
                em.I("act", "activation", reads=[f"rvp{b}"], writes=[f"uTr{b}"], out=uTrt[b][:], in_=rvp[b][:].rearrange("p (k t) -> p k t", k=8), func=AF.Copy)
                ir = NT - 1 - i
                if i == NT - 1:
                    self.dump("uTrt", uTrt[b][:], [f"uTr{b}"])
                    self.dump("uTt", uTt[b][:], [f"uTt{b}"])
                em.dma(STORE_Q, self.uTr_s[:, :, ir * 128:(ir + 1) * 128].rearrange("k p t -> p k t"), uTrt[b][:],
                       reads=[f"uTr{b}"], writes=[f"uTrs{ir}"])
        TB = 64
        NB = S // TB
        with contextlib.ExitStack() as st:
            self.em.barrier()
            T = {}
            for nm in ("are", "aim", "ldt", "dt", "rho", "th", "mag", "kf", "z", "sw", "sw2", "cw", "sn", "cs", "ar", "ai",
                       "nr", "den", "t1", "t2", "qre", "qim"):
                T[nm] = self.sb(st, [128, 32], F32, nm)
            T["C1"] = self.sb(st, [128, 128], F32, "C1")
            T["C2"] = self.sb(st, [128, 128], F32, "C2")
            for nm in ("bre", "bim", "bbr", "bbi", "tb"):
                T[nm] = self.sb(st, [128, 32, 16], F32, nm)
            T["Bq"] = [self.sb(st, [128, 128], F32, "Bq") for _ in range(4)]
            T["WB"] = [self.sb(st, [128, 64, 128], BF16, "WB") for _ in range(2)]
            T["WC"] = [self.sb(st, [128, 64, 128], BF16, "WC") for _ in range(2)]
            T["cin"] = [self.sb(st, [128, 128], F32, "cin") for _ in range(2)]
            T["Cq"] = [self.sb(st, [128, 32, 16], F32, "Cq") for _ in range(2)]
            T["tps"] = [self.ps(st, [128, 512], F32, "tps") for _ in range(2)]
            for k in range(4):
                em.I("pool", "memset", writes=[f"Bqz{k}", f"Bq{k}"], ap=T["Bq"][k][:], constant=0.0)
            for r in range(2):
                em.I("pool", "memset", writes=["WCz", f"WC{r}_0", f"WC{r}_1"], ap=T["WC"][r][:], constant=0.0)
            for r in range(2):
                self.s5_prep_dir(st, j, r, T)
            C1, C2, WB, WC = T["C1"], T["C2"], T["WB"], T["WC"]
            BUH = [self.sb(st, [128, TB, 128], F32, "BUH") for _ in range(2)]
            HB = self.sb(st, [128, TB, 128], BF16, "HB")
            uTb = [[self.sb(st, [128, 8, TB], BF16, "uTb") for _ in range(2)] for _ in range(2)]
            P1 = self.sb(st, [128, 128], F32, "P1")
            P2 = self.sb(st, [128, 128], F32, "P2")
            carry = self.sb(st, [128, 128], F32, "carry")
            yt = [[self.sb(st, [128, 8, TB], F32, "yt") for _ in range(2)] for _ in range(2)]
            bup = [self.ps(st, [128, 8, TB], F32, "bup") for _ in range(2)]
            yp = [self.ps(st, [128, 512], F32, "yp") for _ in range(2)]
            yq = [self.ps(st, [128, 512], F32, "yq") for _ in range(2)]
            ytm = [self.sb(st, [TB, 128], F32, "ytm") for _ in range(2)]
            em.I("pool", "memset", writes=["carry"], ap=carry[:], constant=0.0)
            ysc = (self.yf_s, self.yb_s)

            def blk(r, n):
                return n if r == 0 else NB - 1 - n

            def emit_load(n):
                b = n % 2
                for r in range(2):
                    srcT = self.uT_s if r == 0 else self.uTr_s
                    kn = f"uTs{n * TB // 128}" if r == 0 else f"uTrs{n * TB // 128}"
                    em.dma("sp", uTb[r][b][:], srcT[:, :, n * TB:(n + 1) * TB].rearrange("k p t -> p k t"),
                           reads=[kn], writes=[f"uTb{r}{b}"], slow=True)

            def emit_bu(n):
                b = n % 2
                for r in range(2):
                    for c8 in range(8):
                        pb = c8 % 2
                        for cc in range(8):
                            col = c8 * 8 + cc
                            kt = (col % 32) // 4
                            em.I("pe", "matmul", reads=[f"WB{r}_{col}", f"uTb{r}{b}"], writes=[f"bup{pb}"],
                                 out=bup[pb][:, cc, :], lhsT=WB[r][:, col, :], rhs=uTb[r][b][:, kt, :], start=True, stop=True)
                        dst = BUH[b][:, :, r * 64 + c8 * 8:r * 64 + c8 * 8 + 8]
                        em.I("act", "activation", reads=[f"bup{pb}"], writes=[f"buh{b}"],
                             out=dst.rearrange("p t c -> p c t"), in_=bup[pb][:], func=AF.Copy)

            emit_load(0)
            emit_bu(0)
            for n in range(NB):
                b = n % 2
                if n + 1 < NB:
                    emit_load(n + 1)
                    emit_bu(n + 1)
                for t in range(TB):
                    if t == 0:
                        hp = carry[:]
                        rk = [f"buh{b}", "carry", "C1", "C2"]
                    else:
                        hp = BUH[b][:, t - 1, :]
                        rk = [f"buh{b}", "C1", "C2"]
                    hps = hp.rearrange("p (d two c) -> p d two c", d=2, two=2)[:, :, ::-1, :]
                    em.I("dve", "tensor_tensor", reads=rk, writes=["P1"], nosync=(t > 0), out=P1[:], in0=hp, in1=C1[:], op=ALU.mult)
                    em.I("dve", "tensor_tensor", reads=rk, writes=["P2"], nosync=True,
                         out=P2[:].rearrange("p (d two c) -> p d two c", d=2, two=2), in0=hps,
                         in1=C2[:].rearrange("p (d two c) -> p d two c", d=2, two=2), op=ALU.mult)
                    em.I("dve", "tensor_tensor", reads=["P1", "P2"], writes=["P1"], nosync=True, out=P1[:], in0=P1[:], in1=P2[:], op=ALU.add)
                    em.I("dve", "tensor_tensor", reads=["P1", f"buh{b}"], writes=[f"buh{b}"], nosync=True, out=BUH[b][:, t, :], in0=P1[:],
                         in1=BUH[b][:, t, :], op=ALU.add)
                em.I("dve", "tensor_copy", reads=[f"buh{b}"], writes=["carry"], out=carry[:], in_=BUH[b][:, TB - 1, :])
                if n == 0:
                    self.dump("BUH0", BUH[0][:], ["buh0"])
                    self.dump("BUH1pre", BUH[1][:], ["buh1"])
                    self.dump("uTb10", uTb[1][0][:], ["uTb10"])
                    self.dump("Wc0", WC[0][:], ["WC0_0", "WC0_1"])
                em.I("act", "activation", reads=[f"buh{b}"], writes=["HB"], out=HB[:], in_=BUH[b][:], func=AF.Copy)
                for r in range(2):
                    i = blk(r, n)
                    for kt in range(8):
                        pb = kt % 2
                        n_mm = 0
                        for ri in range(2):
                            for g4 in range(4):
                                col = ri * 32 + kt * 4 + g4
                                if r == 0:
                                    em.I("pe", "matmul", reads=[f"WC{r}_{ri}", "HB"], writes=[f"yp{pb}"], out=yp[pb][:, 0:TB],
                                         lhsT=WC[r][:, col, :], rhs=HB[:, :, col], start=(n_mm == 0), stop=(n_mm == 7))
                                else:
                                    em.I("pe", "matmul", reads=[f"WC{r}_{ri}", "HB"], writes=[f"yp{pb}"], out=yp[pb][0:TB, 0:128],
                                         lhsT=HB[:, :, 64 + col], rhs=WC[r][:, col, :], start=(n_mm == 0), stop=(n_mm == 7))
                                n_mm += 1
                        if r == 0:
                            em.I("act", "activation", reads=[f"yp{pb}"], writes=[f"yt{r}{b}"], out=yt[r][b][:, kt, :], in_=yp[pb][:, 0:TB], func=AF.Copy)
                        else:
                            em.I("act", "activation", reads=[f"yp{pb}"], writes=[f"ytm{pb}"], out=ytm[pb][:], in_=yp[pb][0:TB, 0:128], func=AF.Copy)
                            em.I("pe", "matmul", reads=[f"ytm{pb}", "antif"], writes=[f"yq{pb}"], out=yq[pb][:, 0:TB], lhsT=ytm[pb][:],
                                 rhs=self.anti_f[:], start=True, stop=True)
                            em.I("act", "activation", reads=[f"yq{pb}"], writes=[f"yt{r}{b}"], out=yt[r][b][:, kt, :], in_=yq[pb][:, 0:TB], func=AF.Copy)
                    if n == 0:
                        self.dump(f"yt{r}", yt[r][b][:], [f"yt{r}{b}"])
                    if n == NB - 1:
                        self.dump(f"ytL{r}", yt[r][b][:], [f"yt{r}{b}"])
                        if r == 1:
                            self.dump("BUHL", BUH[b][:], [f"buh{b}"])
                    em.dma(STORE_Q, ysc[r][:, :, i * TB:(i + 1) * TB].rearrange("k p t -> p k t"), yt[r][b][:],
                           reads=[f"yt{r}{b}"], writes=[f"ys{r}_{i}"], slow=True)
        with contextlib.ExitStack() as st:
            self.em.barrier()
            Wg = self.sb(st, [128, 8, 2048], BF16, "Wg")
            stgt = self.sb(st, [128, 2, 2048], F32, "stg")
            stg = [stgt[:, 0, :], stgt[:, 1, :]]
            uTt = [self.sb(st, [128, 8, 128], BF16, "uTt") for _ in range(2)]
            yft = [self.sb(st, [128, 8, 128], F32, "yft") for _ in range(2)]
            ybt = [self.sb(st, [128, 8, 128], F32, "ybt") for _ in range(2)]
            hb = [self.sb(st, [128, D], F32, "hb") for _ in range(2)]
            dT = self.sb(st, [128, 8], F32, "dT")
            yall = self.sb(st, [128, 8, 128], F32, "yall")
            g2 = self.sb(st, [128, 8, 128], F32, "g2")
            gT = self.sb(st, [128, 8, 128], BF16, "gT")
            sig = self.sb(st, [128, D], F32, "sig")
            mix = self.sb(st, [128, D], F32, "mix")
            glp = [self.ps(st, [128, 512], F32, "glp") for _ in range(4)]
            wg = w["ssm_w_glu"][j]
            em.dma("sp", dT[:], w["ssm_d"][j].rearrange("(kt p) -> p kt", p=128), writes=["dT"], slow=True)
            self.load_weight([Wg[:, i, :] for i in range(8)], [wg[i * 128:(i + 1) * 128, :] for i in range(8)], stg, None, "Wg")

            def load(i):
                b = i % 2
                sl = slice(i * 128, (i + 1) * 128)
                ykeys = [f"ys{r}_{k}" for r in range(2) for k in (2 * i, 2 * i + 1)]
                em.dma("sp", uTt[b][:], self.uT_s[:, :, sl].rearrange("k p t -> p k t"), reads=[f"uTs{i}"], writes=[f"uTt{b}"], slow=True)
                em.dma("sp", yft[b][:], self.yf_s[:, :, sl].rearrange("k p t -> p k t"), reads=ykeys, writes=[f"yft{b}"], slow=True)
                ix = em.dma("sp", ybt[b][:], self.yb_s[:, :, sl].rearrange("k p t -> p k t"), reads=ykeys, writes=[f"ybt{b}"], slow=True)
                if i == 0:
                    em.debug_ops = {"ybt_load0": ix, "yft_load0": ix - 1}
                em.dma("sp", hb[b][:], self.out[sl, :], reads=[f"out{i}"], writes=[f"hb{b}"])

            load(0)
            for i in range(NT):
                b = i % 2
                if i + 1 < NT:
                    load(i + 1)
                em.I("pool", "tensor_tensor", reads=[f"yft{b}", f"ybt{b}"], writes=["yall"], out=yall[:], in0=yft[b][:], in1=ybt[b][:], op=ALU.add)
                em.I("dve", "tensor_tensor", reads=[f"uTt{b}", "dT"], writes=["g2"], out=g2[:], in0=uTt[b][:],
                     in1=dT[:].unsqueeze(2).to_broadcast([128, 8, 128]), op=ALU.mult)
                em.I("dve", "tensor_tensor", reads=["g2", "yall"], writes=["yall"], out=yall[:], in0=yall[:], in1=g2[:], op=ALU.add)
                em.I("dve", "tensor_tensor", reads=["yall"], writes=["g2"], out=g2[:], in0=yall[:], in1=yall[:], op=ALU.mult)
                em.I("dve", "tensor_scalar", reads=["g2"], writes=["g2"], out=g2[:], in0=g2[:], scalar1=0.044715, scalar2=1.0, op0=ALU.mult, op1=ALU.add)
                em.I("dve", "tensor_tensor", reads=["g2", "yall"], writes=["g2"], out=g2[:], in0=g2[:], in1=yall[:], op=ALU.mult)
                em.I("act", "activation", reads=["g2"], writes=["g2"], out=g2[:], in_=g2[:], func=AF.Sigmoid, scale=GC)
                em.I("dve", "tensor_tensor", reads=["g2", "yall"], writes=["gT"], out=gT[:], in0=g2[:], in1=yall[:], op=ALU.mult)
                if i == 0:
                    self.dump("yall", yall[:], ["yall"])
                    self.dump("yft", yft[b][:], [f"yft{b}"])
                    self.dump("ybt", ybt[b][:], [f"ybt{b}"])
                for c in range(4):
                    for kt in range(8):
                        em.I("pe", "matmul", reads=[f"Wg{kt}", "gT"], writes=[f"glp{c}"], out=glp[c][:], lhsT=gT[:, kt, :],
                             rhs=Wg[:, kt, c * 512:(c + 1) * 512], start=(kt == 0), stop=(kt == 7))
                for c in range(2):
                    em.I("act", "activation", reads=[f"glp{2 + c}"], writes=[f"sig{c}"], out=sig[:, c * 512:(c + 1) * 512], in_=glp[2 + c][:], func=AF.Sigmoid)
                    em.I("dve", "tensor_tensor", reads=[f"glp{c}", f"sig{c}"], writes=[f"mix{c}"], out=mix[:, c * 512:(c + 1) * 512], in0=glp[c][:],
                         in1=sig[:, c * 512:(c + 1) * 512], op=ALU.mult)
                em.I("pool", "tensor_tensor", reads=["mix0", "mix1", f"hb{b}"], writes=[f"hb{b}"], out=hb[b][:], in0=hb[b][:], in1=mix[:], op=ALU.add)
                em.dma(STORE_Q, self.out[i * 128:(i + 1) * 128, :], hb[b][:], reads=[f"hb{b}"], writes=[f"out{i}"])
            if "ybsdump" in FLAGS:
                self.dump("ybs_dram", self.yb_s[:, :, :], [f"out{NT - 1}"])

def make_consts(S):
    ident = np.eye(128, dtype=np.float32)
    pos = np.arange(S, dtype=np.float32)
    inv_freq = (10000.0 ** (-np.arange(0, 64, 2, dtype=np.float32) / 64)).astype(np.float32)
    ang = pos[:, None] * inv_freq[None, :]
    rope = np.concatenate([np.cos(ang), np.sin(ang)], axis=1).astype(np.float32)
    anti = np.ascontiguousarray(ident[::-1])
    return {"ident_f": ident, "ident_b": ident.astype(ml_dtypes.bfloat16), "rope": rope,
            "anti_b": anti.astype(ml_dtypes.bfloat16), "anti_f": np.ascontiguousarray(np.eye(64, dtype=np.float32)[::-1])}


_CACHE = {}
DUMPS = None
FLAGS = set()
DBG = 0
STORE_Q = "pool"


def get_prog(S, phases):
    key = (S, tuple(phases))
    if key not in _CACHE:
        p = Prog(S, phases)
        p.dbg = DBG
        p.build()
        _CACHE[key] = p
    return _CACHE[key]


FULL_PHASES = (("attn", 0, "x"), ("ffn", 0), ("s5", 1), ("ffn", 1), ("attn", 2), ("ffn", 2), ("s5", 3), ("ffn", 3))


def run(inputs, phases, n_cores=8, trace=False):
    x = np.asarray(inputs["x"], dtype=np.float32)
    B, S, _ = x.shape
    prog = get_prog(S, phases)
    consts = make_consts(S)
    base = {n: np.ascontiguousarray(np.asarray(inputs[n], dtype=np.float32)) for n, _ in INPUT_SPECS}
    base.update(consts)
    in_maps = []
    for c in range(n_cores):
        m = dict(base)
        m["x"] = np.ascontiguousarray(x[c])
        in_maps.append(m)
    res = run_bass_kernel_spmd(prog.nc, in_maps, core_ids=list(range(n_cores)), trace=trace)
    outs = np.stack([np.asarray(r["out"]) for r in res.results], axis=0)
    return outs, res


def kernel(**inputs):
    outs, _ = run(inputs, FULL_PHASES, n_cores=8)
    return outs.astype(np.float32)
```

```python
import contextlib
import math
import numpy as np
import ml_dtypes
import concourse.bass as bass
import concourse.mybir as mybir
from concourse.bass_utils import run_bass_kernel_spmd

F32 = mybir.dt.float32
BF16 = mybir.dt.bfloat16
I32 = mybir.dt.int32
AF = mybir.ActivationFunctionType
ALU = mybir.AluOpType
AX = mybir.AxisListType

D = 1024
DFF = 4096
NH = 8
EPS = 1e-6
ENGS = ("pe", "act", "dve", "pool", "sp")
SEM_MAX = 30000
NDMA_SLOTS = 24
FENCE_DIST = 0
FENCE_REPS = 2


class Op:
    __slots__ = ("eng", "fn", "deps", "idx", "need_inc", "is_dma", "semid", "semval", "slot_prev")

    def __init__(self, eng, fn, deps, is_dma):
        self.eng = eng
        self.fn = fn
        self.deps = deps
        self.is_dma = is_dma
        self.need_inc = False
        self.semid = None
        self.semval = None
        self.slot_prev = None


class Emitter:
    def __init__(self, nc, same_engine_sync=True):
        self.nc = nc
        self.ops = []
        self.last_w = {}
        self.readers = {}
        self.same_engine_sync = same_engine_sync
        self.nosync = set()
        self.cur_barrier = None
        self.dma_since_barrier = []
        self.last_on_eng = {}
        self.bar_a = nc.dram_tensor("bar_a", [1, 16], F32).ap()
        self.bar_b = nc.dram_tensor("bar_b", [1, 16], F32).ap()

    def barrier(self):
        deps = set(self.last_on_eng.values()) | set(self.dma_since_barrier)
        if self.cur_barrier is not None:
            deps.add(self.cur_barrier)
        a, b = self.bar_a, self.bar_b
        idx = self.op("sp", lambda e: e.dma_start(out=b, in_=a), (), (), is_dma=True)
        self.ops[idx].deps |= deps
        self.cur_barrier = idx
        self.dma_since_barrier = []

    def I(self, eng, name, reads=(), writes=(), nosync=False, **kw):
        idx = self.op(eng, (name, kw), reads, writes)
        if nosync:
            self.nosync.add(idx)
        return idx

    def op(self, eng, fn, reads=(), writes=(), is_dma=False):
        deps = set()
        for k in reads:
            w = self.last_w.get(k)
            if w is not None:
                deps.add(w)
        for k in writes:
            w = self.last_w.get(k)
            if w is not None:
                deps.add(w)
            for r in self.readers.get(k, ()):
                deps.add(r)
        if self.cur_barrier is not None:
            deps.add(self.cur_barrier)
        idx = len(self.ops)
        o = Op(eng, fn, deps, is_dma)
        o.idx = idx
        self.ops.append(o)
        if is_dma:
            self.dma_since_barrier.append(idx)
        elif fn is not None:
            self.last_on_eng[eng] = idx
        for k in reads:
            self.readers.setdefault(k, []).append(idx)
        for k in writes:
            self.last_w[k] = idx
            self.readers[k] = []
        return idx

    def dma(self, eng, out, in_, reads=(), writes=(), slow=False):
        if slow:
            fn = lambda e: e.dma_start(out=out, in_=in_, allow_slow_non_contiguous=True)
        else:
            fn = lambda e: e.dma_start(out=out, in_=in_)
        return self.op(eng, fn, reads, writes, is_dma=True)

    def finalize(self):
        nc = self.nc
        ops = self.ops
        per_eng = {e: [] for e in ENGS}
        for o in ops:
            per_eng[o.eng].append(o)
        pos = {}
        for e in ENGS:
            for i, o in enumerate(per_eng[e]):
                pos[o.idx] = i
        waited = {e: {p: -1 for p in ENGS} for e in ENGS}
        waited_dma = {e: set() for e in ENGS}
        wait_lists = {}
        for o in ops:
            wl = {}
            dl = []
            for d in o.deps:
                po = ops[d]
                if po.is_dma:
                    if d not in waited_dma[o.eng]:
                        waited_dma[o.eng].add(d)
                        dl.append(d)
                    continue
                pe_ = po.eng
                if pe_ == o.eng and not o.is_dma:
                    if pe_ == "pe" or not self.same_engine_sync or o.idx in self.nosync:
                        continue
                if pos[d] <= waited[o.eng][pe_]:
                    continue
                if pe_ not in wl or pos[d] > pos[wl[pe_]]:
                    wl[pe_] = d
            for pe_, d in wl.items():
                waited[o.eng][pe_] = pos[d]
                ops[d].need_inc = True
            wait_lists[o.idx] = list(wl.values()) + dl
        n_sems = {}
        dma_cnt = {e: 0 for e in ENGS}
        dma_last = {}
        for e in ENGS:
            cnt = 0
            semi = 0
            for o in per_eng[e]:
                if o.is_dma:
                    k = dma_cnt[e]
                    dma_cnt[e] += 1
                    slot = k % NDMA_SLOTS
                    o.semid = ("dma", e, slot)
                    o.semval = 16 * (k // NDMA_SLOTS + 1)
                    o.slot_prev = dma_last.get((e, slot))
                    dma_last[(e, slot)] = o
                    o.need_inc = True
                elif o.need_inc:
                    if cnt >= SEM_MAX:
                        semi += 1
                        cnt = 0
                    cnt += 1
                    o.semid = (e, semi)
                    o.semval = cnt
            n_sems[e] = semi + 1
        self.n_inst = {e: len(per_eng[e]) for e in ENGS}
        fence_need = {}
        for o in ops:
            if o.is_dma:
                for d in wait_lists[o.idx]:
                    if ops[d].is_dma and (o.idx - d) < FENCE_DIST:
                        fence_need[o.idx] = True
        self.n_fence = len(fence_need)
        for nm, ix in getattr(self, "debug_ops", {}).items():
            o = ops[ix]
            print("DEBUGOP", nm, "idx", ix, "eng", o.eng, "deps", sorted(o.deps), "waits", [(d, ops[d].eng, ops[d].is_dma, ops[d].semid, ops[d].semval) for d in wait_lists[ix]], "fenced", ix in fence_need, flush=True)
        fa = nc.dram_tensor("fence_a", [1, 16], F32).ap()
        fb = {e: nc.dram_tensor(f"fence_b_{e}", [1, 16], F32).ap() for e in ENGS}
        fence_cnt = {e: 0 for e in ENGS}
        with contextlib.ExitStack() as st:
            sems = {}
            fsem = {e: st.enter_context(nc.semaphore(f"fence_{e}")) for e in ("sp", "pool")}
            for e in ENGS:
                for i in range(n_sems[e]):
                    sems[(e, i)] = st.enter_context(nc.semaphore(f"s_{e}_{i}"))
                for s in range(min(NDMA_SLOTS, dma_cnt[e])):
                    sems[("dma", e, s)] = st.enter_context(nc.semaphore(f"d_{e}_{s}"))
            block = st.enter_context(nc.Block())

            def run(e):
                def body(eng):
                    for o in per_eng[e]:
                        if o.is_dma and o.slot_prev is not None:
                            p = o.slot_prev
                            eng.wait_ge(sems[p.semid], p.semval)
                        for d in wait_lists[o.idx]:
                            po = ops[d]
                            eng.wait_ge(sems[po.semid], po.semval)
                        if o.idx in fence_need:
                            for _ in range(FENCE_REPS):
                                fence_cnt[e] += 1
                                eng.dma_start(out=fb[e], in_=fa).then_inc(fsem[e], 16)
                                eng.wait_ge(fsem[e], 16 * fence_cnt[e])
                        if o.fn is None:
                            continue
                        if isinstance(o.fn, tuple):
                            ins = getattr(eng, o.fn[0])(**o.fn[1])
                        else:
                            ins = o.fn(eng)
                        if o.need_inc:
                            ins.then_inc(sems[o.semid], 16 if o.is_dma else 1)
                return body

            block.tensor(run("pe"))
            block.scalar(run("act"))
            block.vector(run("dve"))
            block.gpsimd(run("pool"))
            block.sync(run("sp"))


INPUT_SPECS = [
    ("norm_mix", (4, 1024)), ("norm_ffn", (4, 1024)), ("attn_w_qkv", (2, 1024, 3072)),
    ("attn_q_gain", (2, 64)), ("attn_k_gain", (2, 64)), ("attn_lambda", (2, 4, 64)),
    ("attn_subln", (2, 128)), ("attn_w_o", (2, 1024, 1024)), ("ssm_a_re", (2, 2, 64, 64)),
    ("ssm_a_im", (2, 2, 64, 64)), ("ssm_log_dt", (2, 2, 64)), ("ssm_b_re", (2, 2, 64, 64, 16)),
    ("ssm_b_im", (2, 2, 64, 64, 16)), ("ssm_c_re", (2, 2, 64, 16, 64)), ("ssm_c_im", (2, 2, 64, 16, 64)),
    ("ssm_d", (2, 1024)), ("ssm_w_glu", (2, 1024, 2048)), ("ffn_w_up", (4, 1024, 4096)),
    ("ffn_w_down", (4, 4096, 1024)),
]


class Prog:
    def __init__(self, S, phases):
        self.S = S
        self.NT = S // 128
        self.phases = phases
        self.nc = bass.Bass("TRN2", target_bir_lowering=False)
        self.em = Emitter(self.nc)
        self.uid = 0
        self.dbg_names = []

    def dram_in(self, name, shape, dt=F32):
        return self.nc.dram_tensor(name, list(shape), dt, kind="ExternalInput").ap()

    def sb(self, st, shape, dt, name=None):
        self.uid += 1
        return st.enter_context(self.nc.sbuf_tensor(f"{name or 't'}_{self.uid}", list(shape), dt))

    def ps(self, st, shape, dt, name=None):
        self.uid += 1
        return st.enter_context(self.nc.psum_tensor(f"{name or 'p'}_{self.uid}", list(shape), dt))

    def build(self):
        nc, S = self.nc, self.S
        self.x = self.dram_in("x", (S, D))
        self.w = {n: self.dram_in(n, shp) for n, shp in INPUT_SPECS}
        self.ident_f_d = self.dram_in("ident_f", (128, 128))
        self.em.bar_a = self.ident_f_d[0:1, 0:16]
        self.ident_b_d = self.dram_in("ident_b", (128, 128), BF16)
        self.rope_d = self.dram_in("rope", (S, 64))
        self.jb_d = self.dram_in("anti_b", (128, 128), BF16)
        self.jf_d = self.dram_in("anti_f", (64, 64))
        self.out = nc.dram_tensor("out", [S, D], F32, kind="ExternalOutput").ap()
        self.qT_s = nc.dram_tensor("qT_s", [NH, 128, S], BF16).ap()
        self.kT_s = nc.dram_tensor("kT_s", [NH, 128, S], BF16).ap()
        self.v_s = nc.dram_tensor("v_s", [S, D], BF16).ap()
        self.oT_s = nc.dram_tensor("oT_s", [NH, 128, S], BF16).ap()
        self.uT_s = nc.dram_tensor("uT_s", [8, 128, S], BF16).ap()
        self.yb_s = nc.dram_tensor("yb_s", [8, 128, S], F32).ap()
        self.yf_s = nc.dram_tensor("yf_s", [8, 128, S], F32).ap()
        self.uTr_s = nc.dram_tensor("uTr_s", [8, 128, S], BF16).ap()
        em = self.em
        with contextlib.ExitStack() as st:
            self.ident_f = self.sb(st, [128, 128], F32, "identf")
            self.ident_b = self.sb(st, [128, 128], BF16, "identb")
            em.dma("sp", self.ident_f[:], self.ident_f_d, writes=["identf"])
            em.dma("sp", self.ident_b[:], self.ident_b_d, writes=["identb"])
            self.anti_b = self.sb(st, [128, 128], BF16, "antib")
            self.anti_f = self.sb(st, [64, 64], F32, "antif")
            em.dma("sp", self.anti_b[:], self.jb_d, writes=["antib"])
            em.dma("sp", self.anti_f[:], self.jf_d, writes=["antif"])
            for ph in self.phases:
                kind = ph[0]
                if kind == "copy":
                    self.phase_copy()
                elif kind == "ffn":
                    self.phase_ffn(ph[1], ph[2] if len(ph) > 2 else "out")
                elif kind == "attn":
                    self.phase_attn(ph[1], ph[2] if len(ph) > 2 else "out")
                elif kind == "s5":
                    self.phase_s5(ph[1])
            em.op("sp", None, reads=[f"out{i}" for i in range(self.NT)] + ["dbgout_" + n for n in self.dbg_names])
            em.finalize()
        return nc

    def dump(self, name, ap, keys):
        if not getattr(self, "dbg", 0):
            return
        if DUMPS is not None and name not in DUMPS:
            return
        shp = list(ap.shape)
        d = self.nc.dram_tensor("dbg_" + name, shp, ap.dtype, kind="ExternalOutput").ap()
        self.em.dma("sp", d, ap, reads=keys, writes=["dbgout_" + name])
        self.dbg_names.append(name)

    def src_ap(self, src):
        return self.x if src == "x" else self.out

    def phase_copy(self):
        em = self.em
        for i in range(self.NT):
            em.dma("sp", self.out[i * 128:(i + 1) * 128, :], self.x[i * 128:(i + 1) * 128, :],
                   writes=[f"out{i}"])


    def rms_rstd(self, hb_aps, junk, ss, ms, sd, rstd, keys_in, kp, dim=1024):
        em = self.em
        n = len(hb_aps)
        for a, hap in enumerate(hb_aps):
            em.I("act", "activation", reads=keys_in, writes=[kp + "junk", kp + "ss"],
                 out=junk[:], in_=hap, func=AF.Square, accum_out=ss[:, a:a + 1])
        em.I("dve", "tensor_scalar", reads=[kp + "ss"], writes=[kp + "ms"],
             out=ms[:, 0:n], in0=ss[:, 0:n], scalar1=1.0 / dim, scalar2=EPS, op0=ALU.mult, op1=ALU.add)
        em.I("act", "activation", reads=[kp + "ms"], writes=[kp + "sd"], out=sd[:, 0:n], in_=ms[:, 0:n], func=AF.Sqrt)
        em.I("dve", "reciprocal", reads=[kp + "sd"], writes=[kp + "rstd"], out=rstd[:, 0:n], in_=sd[:, 0:n])

    def load_weight(self, dsts, srcs, stg, gains, key_dst, engs=("act", "pool", "dve")):
        em = self.em
        for i, (dst, sa) in enumerate(zip(dsts, srcs)):
            b = i % 2
            em.dma("sp", stg[b], sa, writes=[f"stg{b}"])
            eng = engs[i % len(engs)]
            g = gains[i] if gains is not None else None
            rk = [f"stg{b}"] + (["gain"] if g is not None else [])
            wk = [f"{key_dst}{i}"]
            if eng == "act":
                if g is not None:
                    em.I("act", "activation", reads=rk, writes=wk, out=dst, in_=stg[b], func=AF.Copy, scale=g)
                else:
                    em.I("act", "activation", reads=rk, writes=wk, out=dst, in_=stg[b], func=AF.Copy)
            else:
                if g is not None:
                    em.I(eng, "tensor_scalar", reads=rk, writes=wk, out=dst, in0=stg[b], scalar1=g, scalar2=None, op0=ALU.mult)
                else:
                    em.I(eng, "tensor_copy", reads=rk, writes=wk, out=dst, in_=stg[b])

    def phase_ffn(self, L, src="out"):
        em, nc, S = self.em, self.nc, self.S
        ST = 256
        NST = S // ST
        src_t = self.src_ap(src)
        with contextlib.ExitStack() as st:
            self.em.barrier()
            Wup = self.sb(st, [128, 8, DFF], BF16, "wup")
            Wdn = self.sb(st, [128, 32, D], BF16, "wdn")
            gain = self.sb(st, [128, 8], F32, "gain")
            stgt = self.sb(st, [128, 2, 2048], F32, "stg")
            hb = [self.sb(st, [128, 2, D], F32, "hb") for _ in range(2)]
            u = self.sb(st, [128, 2, D], BF16, "u")
            uT = self.sb(st, [128, 8, ST], BF16, "uT")
            hid = self.sb(st, [128, 32, ST], BF16, "hid")
            r = [self.sb(st, [128, ST], BF16, "r") for _ in range(2)]
            junk = self.sb(st, [128, D], BF16, "junk")
            ss = self.sb(st, [128, 2], F32, "ss")
            ms = self.sb(st, [128, 2], F32, "ms")
            sd = self.sb(st, [128, 2], F32, "sd")
            rstd = self.sb(st, [128, 2], F32, "rstd")
            tpp = [self.ps(st, [128, 1024], BF16, "tpp") for _ in range(2)]
            upp = [self.ps(st, [128, 512], F32, "upp") for _ in range(2)]
            dnp = [self.ps(st, [128, 512], F32, "dnp") for _ in range(2)]
            stg = [stgt[:, 0, :], stgt[:, 1, :]]
            em.dma("sp", gain[:], self.w["norm_ffn"][L].rearrange("(kt p) -> p kt", p=128), writes=["gain"], slow=True)
            wu = self.w["ffn_w_up"][L]
            wd = self.w["ffn_w_down"][L]
            self.load_weight([Wup[:, i // 2, (i % 2) * 2048:(i % 2 + 1) * 2048] for i in range(16)],
                             [wu[(i // 2) * 128:(i // 2 + 1) * 128, (i % 2) * 2048:(i % 2 + 1) * 2048] for i in range(16)],
                             stg, [gain[:, i // 2:i // 2 + 1] for i in range(16)], "Wup")
            stg3 = [stgt[:, 0, :].rearrange("p (a d) -> p a d", a=2), stgt[:, 1, :].rearrange("p (a d) -> p a d", a=2)]
            self.load_weight([Wdn[:, 2 * i:2 * i + 2, :] for i in range(16)],
                             [wd[i * 256:(i + 1) * 256, :].rearrange("(a p) d -> p a d", p=128) for i in range(16)],
                             stg3, None, "Wdn")

            def load(sti):
                b = sti % 2
                em.dma("sp", hb[b][:], src_t[sti * ST:(sti + 1) * ST, :].rearrange("(a p) d -> p a d", p=128),
                       reads=[f"out{2 * sti}", f"out{2 * sti + 1}"], writes=[f"hb{b}"])

            load(0)
            for sti in range(NST):
                b = sti % 2
                if sti + 1 < NST:
                    load(sti + 1)
                self.rms_rstd([hb[b][:, a, :] for a in range(2)], junk, ss, ms, sd, rstd, [f"hb{b}"], "f")
                for a in range(2):
                    em.I("act", "activation", reads=[f"hb{b}", "frstd"], writes=[f"u{a}"],
                         out=u[:, a, :], in_=hb[b][:, a, :], func=AF.Copy, scale=rstd[:, a:a + 1])
                for a in range(2):
                    tp = tpp[a]
                    for kt in range(8):
                        em.I("pe", "transpose", reads=[f"u{a}", "identb"], writes=[f"tpp{a}"],
                             out=tp[:, kt * 128:(kt + 1) * 128], in_=u[:, a, kt * 128:(kt + 1) * 128], identity=self.ident_b[:])
                    em.I("dve", "tensor_copy", reads=[f"tpp{a}"], writes=["uT"],
                         out=uT[:, :, a * 128:(a + 1) * 128], in_=tp[:].rearrange("p (k t) -> p k t", k=8))
                for ft in range(32):
                    pb = ft % 2
                    for kt in range(8):
                        em.I("pe", "matmul", reads=[f"Wup{kt * 2 + ft // 16}", "uT"], writes=[f"upp{pb}"],
                             out=upp[pb][:, 0:ST], lhsT=Wup[:, kt, ft * 128:(ft + 1) * 128], rhs=uT[:, kt, :],
                             start=(kt == 0), stop=(kt == 7))
                    em.I("act", "activation", reads=[f"upp{pb}"], writes=[f"r{pb}"], out=r[pb][:], in_=upp[pb][:, 0:ST], func=AF.Relu)
                    em.I("dve", "tensor_tensor", reads=[f"r{pb}"], writes=[f"hid{ft}"],
                         out=hid[:, ft, :], in0=r[pb][:], in1=r[pb][:], op=ALU.mult)
                for a in range(2):
                    for nch in range(2):
                        pb = nch
                        for ft in range(32):
                            em.I("pe", "matmul", reads=[f"Wdn{ft // 2}", f"hid{ft}"], writes=[f"dnp{pb}"],
                                 out=dnp[pb][:], lhsT=hid[:, ft, a * 128:(a + 1) * 128],
                                 rhs=Wdn[:, ft, nch * 512:(nch + 1) * 512], start=(ft == 0), stop=(ft == 31))
                        em.I("dve", "tensor_tensor", reads=[f"dnp{pb}", f"hb{b}"], writes=[f"hb{b}"],
                             out=hb[b][:, a, nch * 512:(nch + 1) * 512], in0=dnp[pb][:],
                             in1=hb[b][:, a, nch * 512:(nch + 1) * 512], op=ALU.add)
                em.dma(STORE_Q, self.out[sti * ST:(sti + 1) * ST, :].rearrange("(a p) d -> p a d", p=128), hb[b][:],
                       reads=[f"hb{b}"], writes=[f"out{2 * sti}", f"out{2 * sti + 1}"])


    def phase_attn(self, L, src="out"):
        em, nc, S, w, NT = self.em, self.nc, self.S, self.w, self.NT
        j = L // 2
        lam_init = 0.8 - 0.6 * math.exp(-0.3 * L)
        src_t = self.src_ap(src)
        with contextlib.ExitStack() as st:
            self.em.barrier()
            Wq = self.sb(st, [128, 8, 3072], BF16, "Wq")
            gain = self.sb(st, [128, 8], F32, "gain")
            stgt = self.sb(st, [128, 2, 1536], F32, "stg")
            stg = [stgt[:, 0, :], stgt[:, 1, :]]
            hb = [self.sb(st, [128, D], F32, "hb") for _ in range(2)]
            rp = [self.sb(st, [128, 64], F32, "rp") for _ in range(2)]
            gq = self.sb(st, [128, 2, 64], F32, "gq")
            u_ = [self.sb(st, [128, D], BF16, "u") for _ in range(2)]
            uT_ = [self.sb(st, [128, 8, 128], BF16, "uT") for _ in range(2)]
            qk_ = [self.sb(st, [128, 2048], F32, "qk") for _ in range(2)]
            sq_ = [self.sb(st, [128, 2048], F32, "sq") for _ in range(2)]
            qr_ = [self.sb(st, [128, 2048], BF16, "qr") for _ in range(2)]
            vt = [self.sb(st, [128, D], BF16, "vt") for _ in range(2)]
            qkT = [self.sb(st, [128, 16, 128], BF16, "qkT") for _ in range(2)]
            t1_ = [self.sb(st, [128, 32, 32], F32, "t1") for _ in range(2)]
            t2_ = [self.sb(st, [128, 32, 32], F32, "t2") for _ in range(2)]
            t3_ = [self.sb(st, [128, 32, 32], F32, "t3") for _ in range(2)]
            t4_ = [self.sb(st, [128, 32, 32], F32, "t4") for _ in range(2)]
            junk = self.sb(st, [128, D], BF16, "junk")
            ss = self.sb(st, [128, 1], F32, "ss")
            ms = self.sb(st, [128, 1], F32, "ms")
            sd = self.sb(st, [128, 1], F32, "sd")
            rstd = self.sb(st, [128, 1], F32, "rstd")
            m32_ = [self.sb(st, [128, 32], F32, "m32") for _ in range(2)]
            s32_ = [self.sb(st, [128, 32], F32, "s32") for _ in range(2)]
            r32_ = [self.sb(st, [128, 32], F32, "r32") for _ in range(2)]
            tpp_ = [self.ps(st, [128, 1024], BF16, "tpp") for _ in range(2)]
            mp = [self.ps(st, [128, 512], F32, "mp") for _ in range(2)]
            tq = self.ps(st, [128, 2048], BF16, "tq")
            em.dma("sp", gain[:], w["norm_mix"][L].rearrange("(kt p) -> p kt", p=128), writes=["gain"], slow=True)
            em.dma("sp", gq[:, 0, :], w["attn_q_gain"][j].partition_broadcast(128), writes=["gq0"], slow=True)
            em.dma("sp", gq[:, 1, :], w["attn_k_gain"][j].partition_broadcast(128), writes=["gq1"], slow=True)
            wq = w["attn_w_qkv"][j]
            self.load_weight([Wq[:, i // 2, (i % 2) * 1536:(i % 2 + 1) * 1536] for i in range(16)],
                             [wq[(i // 2) * 128:(i // 2 + 1) * 128, (i % 2) * 1536:(i % 2 + 1) * 1536] for i in range(16)],
                             stg, [gain[:, i // 2:i // 2 + 1] for i in range(16)], "Wq")

            def load(i):
                b = i % 2
                em.dma("sp", hb[b][:], src_t[i * 128:(i + 1) * 128, :], reads=[f"out{i}"] if src == "out" else [], writes=[f"hb{b}"])
                em.dma("sp", rp[b][:], self.rope_d[i * 128:(i + 1) * 128, :], writes=[f"rp{b}"])

            def stage_A(i):
                    b = i % 2
                    if i + 1 < NT:
                        load(i + 1)
                    u, uT, qk, sq, qr, tpp = u_[b], uT_[b], qk_[b], sq_[b], qr_[b], tpp_[b]
                    t1, t2, t3, t4, m32, s32, r32 = t1_[b], t2_[b], t3_[b], t4_[b], m32_[b], s32_[b], r32_[b]
                    self.rms_rstd([hb[b][:]], junk, ss, ms, sd, rstd, [f"hb{b}"], "a")
                    em.I("act", "activation", reads=[f"hb{b}", "arstd"], writes=[f"u{b}"], out=u[:], in_=hb[b][:], func=AF.Copy, scale=rstd[:, 0:1])
                    for kt in range(8):
                        em.I("pe", "transpose", reads=[f"u{b}", "identb"], writes=[f"tpp{b}"], out=tpp[:, kt * 128:(kt + 1) * 128],
                             in_=u[:, kt * 128:(kt + 1) * 128], identity=self.ident_b[:])
                    em.I("dve", "tensor_copy", reads=[f"tpp{b}"], writes=[f"uT{b}"], out=uT[:], in_=tpp[:].rearrange("p (k t) -> p k t", k=8))
                    for c in range(6):
                        pb = c % 2
                        for kt in range(8):
                            em.I("pe", "matmul", reads=[f"Wq{kt * 2 + c // 3}", f"uT{b}"], writes=[f"mp{pb}"], out=mp[pb][:], lhsT=uT[:, kt, :],
                                 rhs=Wq[:, kt, c * 512:(c + 1) * 512], start=(kt == 0), stop=(kt == 7))
                        if c < 4:
                            em.I("act", "activation", reads=[f"mp{pb}"], writes=[f"qk{b}_{c}"], out=qk[:, c * 512:(c + 1) * 512], in_=mp[pb][:], func=AF.Copy)
                        else:
                            em.I("act", "activation", reads=[f"mp{pb}"], writes=[f"vt{b}"], out=vt[b][:, (c - 4) * 512:(c - 3) * 512], in_=mp[pb][:], func=AF.Copy)
                    em.dma(STORE_Q, self.v_s[i * 128:(i + 1) * 128, :], vt[b][:], reads=[f"vt{b}"], writes=[f"vs{i}"])

            def stage_B(i):
                    b = i % 2
                    u, uT, qk, sq, qr, tpp = u_[b], uT_[b], qk_[b], sq_[b], qr_[b], tpp_[b]
                    t1, t2, t3, t4, m32, s32, r32 = t1_[b], t2_[b], t3_[b], t4_[b], m32_[b], s32_[b], r32_[b]
                    qkk = [f"qk{b}_{c}" for c in range(4)]
                    em.I("act", "activation", reads=qkk, writes=[f"sq{b}"], out=sq[:], in_=qk[:], func=AF.Square)
                    em.I("dve", "tensor_reduce", reads=[f"sq{b}"], writes=[f"m32{b}"], out=m32[:], in_=sq[:].rearrange("p (g d) -> p g d", d=64), axis=AX.X, op=ALU.add)
                    em.I("dve", "tensor_scalar", reads=[f"m32{b}"], writes=[f"m32{b}"], out=m32[:], in0=m32[:], scalar1=1.0 / 64, scalar2=EPS, op0=ALU.mult, op1=ALU.add)
                    em.I("act", "activation", reads=[f"m32{b}"], writes=[f"s32{b}"], out=s32[:], in_=m32[:], func=AF.Sqrt)
                    em.I("dve", "reciprocal", reads=[f"s32{b}"], writes=[f"r32{b}"], out=r32[:], in_=s32[:])
                    qk3 = qk[:].rearrange("p (g d) -> p g d", d=64)
                    em.I("dve", "tensor_tensor", reads=qkk + [f"r32{b}"], writes=[f"qkn{b}"], out=qk3, in0=qk3, in1=r32[:].unsqueeze(2).to_broadcast([128, 32, 64]), op=ALU.mult)
                    qk4 = qk[:].rearrange("p (a g d) -> p a g d", a=2, d=64)
                    em.I("dve", "tensor_tensor", reads=[f"qkn{b}", "gq0", "gq1"], writes=[f"qkn{b}"], out=qk4, in0=qk4, in1=gq[:].unsqueeze(2).to_broadcast([128, 2, 16, 64]), op=ALU.mult)
                    x1 = qk3[:, :, 0:32]
                    x2 = qk3[:, :, 32:64]
                    cosb = rp[b][:, 0:32].unsqueeze(1).to_broadcast([128, 32, 32])
                    sinb = rp[b][:, 32:64].unsqueeze(1).to_broadcast([128, 32, 32])
                    qr3 = qr[:].rearrange("p (g d) -> p g d", d=64)
                    em.I("dve", "tensor_tensor", reads=[f"qkn{b}", f"rp{b}"], writes=[f"t1{b}"], out=t1[:], in0=x1, in1=cosb, op=ALU.mult)
                    em.I("dve", "tensor_tensor", reads=[f"qkn{b}", f"rp{b}"], writes=[f"t2{b}"], out=t2[:], in0=x2, in1=sinb, op=ALU.mult)
                    em.I("dve", "tensor_tensor", reads=[f"t1{b}", f"t2{b}"], writes=[f"qr1{b}"], out=qr3[:, :, 0:32], in0=t1[:], in1=t2[:], op=ALU.subtract)
                    em.I("pool", "tensor_tensor", reads=[f"qkn{b}", f"rp{b}"], writes=[f"t3{b}"], out=t3[:], in0=x2, in1=cosb, op=ALU.mult)
                    em.I("dve", "tensor_tensor", reads=[f"qkn{b}", f"rp{b}"], writes=[f"t4{b}"], out=t4[:], in0=x1, in1=sinb, op=ALU.mult)
                    em.I("dve", "tensor_tensor", reads=[f"t3{b}", f"t4{b}"], writes=[f"qr2{b}"], out=qr3[:, :, 32:64], in0=t3[:], in1=t4[:], op=ALU.add)

            def stage_C(i):
                    b = i % 2
                    u, uT, qk, sq, qr, tpp = u_[b], uT_[b], qk_[b], sq_[b], qr_[b], tpp_[b]
                    t1, t2, t3, t4, m32, s32, r32 = t1_[b], t2_[b], t3_[b], t4_[b], m32_[b], s32_[b], r32_[b]
                    for jj in range(16):
                        em.I("pe", "transpose", reads=[f"qr1{b}", f"qr2{b}", "identb"], writes=["tq"], out=tq[:, jj * 128:(jj + 1) * 128],
                             in_=qr[:, jj * 128:(jj + 1) * 128], identity=self.ident_b[:])
                    em.I("act", "activation", reads=["tq"], writes=[f"qkT{b}"], out=qkT[b][:], in_=tq[:].rearrange("p (k t) -> p k t", k=16), func=AF.Copy)
                    em.dma(STORE_Q, self.qT_s[:, :, i * 128:(i + 1) * 128].rearrange("h p t -> p h t"), qkT[b][:, 0:8, :],
                           reads=[f"qkT{b}"], writes=[f"qTs{i}"])
                    em.dma(STORE_Q, self.kT_s[:, :, i * 128:(i + 1) * 128].rearrange("h p t -> p h t"), qkT[b][:, 8:16, :],
                           reads=[f"qkT{b}"], writes=[f"kTs{i}"])

            load(0)
            for i in range(NT + 1):
                if i < NT:
                    stage_A(i)
                if i >= 1:
                    stage_C(i - 1)
                if i < NT:
                    stage_B(i)
        QC = 512
        NQC = S // QC
        with contextlib.ExitStack() as st:
            self.em.barrier()
            kTh = [self.sb(st, [128, S], BF16, "kTh") for _ in range(2)]
            qTh = [self.sb(st, [128, S], BF16, "qTh") for _ in range(2)]
            vh = [self.sb(st, [128, NT, 128], BF16, "vh") for _ in range(2)]
            pT = [self.sb(st, [128, 1024], BF16, "pT") for _ in range(3)]
            onesb = self.sb(st, [128, 128], BF16, "onesb")
            onesf = self.sb(st, [128, 128], F32, "onesf")
            lv = self.sb(st, [128, 4, 64], F32, "lv")
            lp = self.sb(st, [128, 64], F32, "lp")
            lsm = self.sb(st, [128, 2], F32, "lsm")
            lex = self.sb(st, [128, 2], F32, "lex")
            neglam = self.sb(st, [128, 1], F32, "neglam")
            gsc = self.sb(st, [128, 1], F32, "gsc")
            R = self.sb(st, [128, 1024], F32, "R")
            o0 = self.sb(st, [128, QC], F32, "o0")
            o1 = self.sb(st, [128, QC], F32, "o1")
            od = self.sb(st, [128, QC], F32, "od")
            osq = self.sb(st, [128, QC], F32, "osq")
            rs = self.sb(st, [128, QC], F32, "rs")
            oTt = [self.sb(st, [128, QC], BF16, "oTt") for _ in range(2)]
            sc = [self.ps(st, [128, 2, 512], F32, "sc") for _ in range(2)]
            O = [self.ps(st, [128, 512], F32, "O") for _ in range(2)]
            Lp = [self.ps(st, [128, 512], F32, "Lp") for _ in range(2)]
            acc = [self.sb(st, [128, 512], F32, "acc") for _ in range(2)]
            lsum = self.sb(st, [128, 512], F32, "lsum")
            em.I("pool", "memset", writes=["onesb"], ap=onesb[:], constant=1.0)
            em.I("pool", "memset", writes=["onesf"], ap=onesf[:], constant=1.0 / 128)
            onesf1 = self.sb(st, [128, 128], F32, "onesf1")
            em.I("pool", "memset", writes=["onesf1"], ap=onesf1[:], constant=1.0)
            em.dma("sp", lv[:], w["attn_lambda"][j].partition_broadcast(128), writes=["lv"], slow=True)
            em.dma("sp", gsc[:], w["attn_subln"][j].rearrange("(p o) -> p o", o=1), writes=["gsc"], slow=True)
            for k in range(2):
                em.I("dve", "tensor_tensor", reads=["lv"], writes=["lp"], out=lp[:], in0=lv[:, 2 * k, :], in1=lv[:, 2 * k + 1, :], op=ALU.mult)
                em.I("dve", "tensor_reduce", reads=["lp"], writes=["lsm"], out=lsm[:, k:k + 1], in_=lp[:], axis=AX.X, op=ALU.add)
            em.I("act", "activation", reads=["lsm"], writes=["lex"], out=lex[:], in_=lsm[:], func=AF.Exp)
            em.I("dve", "tensor_tensor", reads=["lex"], writes=["neglam"], out=neglam[:], in0=lex[:, 1:2], in1=lex[:, 0:1], op=ALU.subtract)
            em.I("dve", "tensor_scalar", reads=["neglam"], writes=["neglam"], out=neglam[:], in0=neglam[:], scalar1=-lam_init, scalar2=None, op0=ALU.add)
            em.I("dve", "tensor_scalar", reads=["gsc"], writes=["gsc"], out=gsc[:], in0=gsc[:], scalar1=(1.0 - lam_init), scalar2=None, op0=ALU.mult)

            def loadh(h):
                hb_ = h % 2
                em.dma("sp", kTh[hb_][:], self.kT_s[h], reads=[f"kTs{i}" for i in range(NT)], writes=[f"kTh{hb_}"])
                em.dma("sp", qTh[hb_][:], self.qT_s[h], reads=[f"qTs{i}" for i in range(NT)], writes=[f"qTh{hb_}"])
                em.dma("sp", vh[hb_][:], self.v_s[:, h * 128:(h + 1) * 128].rearrange("(kt p) e -> p kt e", p=128),
                       reads=[f"vs{i}" for i in range(NT)], writes=[f"vh{hb_}"], slow=True)

            loadh(0)
            it = 0
            for h in range(NH):
                hb_ = h % 2
                if h + 1 < NH:
                    loadh(h + 1)
                for qc in range(NQC):
                    qs = slice(qc * QC, (qc + 1) * QC)

                    def emit_S(kt, itn):
                        for c in range(2):
                            em.I("pe", "matmul", reads=[f"kTh{hb_}", f"qTh{hb_}"], writes=[f"sc{itn % 2}"], out=sc[itn % 2][:, c, :],
                                 lhsT=kTh[hb_][64 * c:64 * c + 64, kt * 128:(kt + 1) * 128], rhs=qTh[hb_][64 * c:64 * c + 64, qs],
                                 start=True, stop=True)

                    emit_S(0, it)
                    for kt in range(NT):
                        if kt + 1 < NT:
                            emit_S(kt + 1, it + 1)
                        pi = it % 3
                        em.I("act", "activation", reads=[f"sc{it % 2}"], writes=[f"pT{pi}"], out=pT[pi][:],
                             in_=sc[it % 2][:].rearrange("p c q -> p (c q)"), func=AF.Exp, scale=0.125)
                        for c in range(2):
                            em.I("pe", "matmul", reads=[f"vh{hb_}", f"pT{pi}"], writes=[f"O{c}"], out=O[c][:], lhsT=vh[hb_][:, kt, :],
                                 rhs=pT[pi][:, c * 512:(c + 1) * 512], start=(kt == 0), stop=(kt == NT - 1))
                        em.I("pe", "matmul", reads=["onesb", f"pT{pi}"], writes=["L0"], out=Lp[0][:], lhsT=onesb[:],
                             rhs=pT[pi][:, 0:512], start=(kt == 0), stop=(kt == NT - 1))
                        par = kt % 2
                        if kt < 2:
                            em.I("dve", "tensor_copy", reads=[f"pT{pi}"], writes=[f"acc{par}"], out=acc[par][:], in_=pT[pi][:, 512:1024])
                        else:
                            em.I("dve", "tensor_tensor", reads=[f"pT{pi}", f"acc{par}"], writes=[f"acc{par}"], nosync=True,
                                 out=acc[par][:], in0=acc[par][:], in1=pT[pi][:, 512:1024], op=ALU.add)
                        it += 1
                    if NT >= 2:
                        em.I("dve", "tensor_tensor", reads=["acc0", "acc1"], writes=["lsum"], out=lsum[:], in0=acc[0][:], in1=acc[1][:], op=ALU.add)
                    else:
                        em.I("dve", "tensor_copy", reads=["acc0"], writes=["lsum"], out=lsum[:], in_=acc[0][:])
                    em.I("pe", "matmul", reads=["onesf1", "lsum"], writes=["L1"], out=Lp[1][:], lhsT=onesf1[:], rhs=lsum[:], start=True, stop=True)
                    for c in range(2):
                        em.I("dve", "reciprocal", reads=[f"L{c}"], writes=[f"R{c}"], out=R[:, c * 512:(c + 1) * 512], in_=Lp[c][:])
                    em.I("dve", "tensor_tensor", reads=["O0", "R0"], writes=["o0"], out=o0[:], in0=O[0][:], in1=R[:, 0:512], op=ALU.mult)
                    em.I("dve", "tensor_tensor", reads=["O1", "R1"], writes=["o1"], out=o1[:], in0=O[1][:], in1=R[:, 512:1024], op=ALU.mult)
                    em.I("dve", "scalar_tensor_tensor", reads=["o0", "o1", "neglam"], writes=["od"], out=od[:], in0=o1[:], scalar=neglam[:, 0:1],
                         in1=o0[:], op0=ALU.mult, op1=ALU.add)
                    em.I("pool", "tensor_tensor", reads=["od"], writes=["osq"], out=osq[:], in0=od[:], in1=od[:], op=ALU.mult)
                    em.I("pe", "matmul", reads=["onesf", "osq", "R0"], writes=["L0"], out=Lp[0][:], lhsT=onesf[:], rhs=osq[:], start=True, stop=True)
                    em.I("dve", "tensor_scalar", reads=["L0"], writes=["rs"], out=rs[:], in0=Lp[0][:], scalar1=EPS, scalar2=None, op0=ALU.add)
                    em.I("act", "activation", reads=["rs"], writes=["rs"], out=rs[:], in_=rs[:], func=AF.Ln)
                    em.I("act", "activation", reads=["rs"], writes=["rs"], out=rs[:], in_=rs[:], func=AF.Exp, scale=-0.5)
                    ob = (h * NQC + qc) % 2
                    em.I("dve", "scalar_tensor_tensor", reads=["od", "gsc", "rs"], writes=[f"oTt{ob}"], out=oTt[ob][:], in0=od[:], scalar=gsc[:, 0:1],
                         in1=rs[:], op0=ALU.mult, op1=ALU.mult)
                    em.dma(STORE_Q, self.oT_s[h, :, qs], oTt[ob][:], reads=[f"oTt{ob}"], writes=[f"oTs{h}_{qc}"])
        with contextlib.ExitStack() as st:
            self.em.barrier()
            Wo = self.sb(st, [128, 8, D], BF16, "Wo")
            stgt = self.sb(st, [128, 2, 2048], F32, "stg")
            stg3 = [stgt[:, 0, :].rearrange("p (a d) -> p a d", a=2), stgt[:, 1, :].rearrange("p (a d) -> p a d", a=2)]
            hb = [self.sb(st, [128, D], F32, "hb") for _ in range(2)]
            oTi = [self.sb(st, [128, 8, 128], BF16, "oTi") for _ in range(2)]
            wop = [self.ps(st, [128, 512], F32, "wop") for _ in range(2)]
            wo = w["attn_w_o"][j]
            self.load_weight([Wo[:, 2 * i:2 * i + 2, :] for i in range(4)],
                             [wo[i * 256:(i + 1) * 256, :].rearrange("(a p) d -> p a d", p=128) for i in range(4)], stg3, None, "Wo")

            def load3(i):
                b = i % 2
                em.dma("sp", hb[b][:], src_t[i * 128:(i + 1) * 128, :], reads=[f"out{i}"] if src == "out" else [], writes=[f"hb{b}"])
                qcs = (i * 128) // QC
                em.dma("sp", oTi[b][:], self.oT_s[:, :, i * 128:(i + 1) * 128].rearrange("h p t -> p h t"),
                       reads=[f"oTs{h}_{qcs}" for h in range(NH)], writes=[f"oTi{b}"])

            load3(0)
            for i in range(NT):
                b = i % 2
                if i + 1 < NT:
                    load3(i + 1)
                for nch in range(2):
                    for h in range(NH):
                        em.I("pe", "matmul", reads=[f"Wo{h // 2}", f"oTi{b}"], writes=[f"wop{nch}"], out=wop[nch][:], lhsT=oTi[b][:, h, :],
                             rhs=Wo[:, h, nch * 512:(nch + 1) * 512], start=(h == 0), stop=(h == NH - 1))
                    em.I("dve", "tensor_tensor", reads=[f"wop{nch}", f"hb{b}"], writes=[f"hb{b}"], out=hb[b][:, nch * 512:(nch + 1) * 512], in0=wop[nch][:],
                         in1=hb[b][:, nch * 512:(nch + 1) * 512], op=ALU.add)
                em.dma(STORE_Q, self.out[i * 128:(i + 1) * 128, :], hb[b][:], reads=[f"hb{b}"], writes=[f"out{i}"])

    def s5_prep_dir(self, st, j, r, T):
        em, nc, w = self.em, self.nc, self.w
        kp = "s5p_"
        are, aim, ldt = T["are"], T["aim"], T["ldt"]
        em.dma("sp", are[:], w["ssm_a_re"][j, r].rearrange("(gp gpar) p -> (gpar p) gp", gpar=2), writes=[kp + "are"], slow=True)
        em.dma("sp", aim[:], w["ssm_a_im"][j, r].rearrange("(gp gpar) p -> (gpar p) gp", gpar=2), writes=[kp + "aim"], slow=True)
        for gpar in range(2):
            em.dma("sp", ldt[gpar * 64:(gpar + 1) * 64, :],
                   w["ssm_log_dt"][j, r].rearrange("(gp gpar) -> gpar gp", gpar=2)[gpar].partition_broadcast(64),
                   writes=[kp + f"ldt{gpar}"], slow=True)
        dt_, rho, th = T["dt"], T["rho"], T["th"]
        em.I("act", "activation", reads=[kp + "ldt0", kp + "ldt1"], writes=[kp + "dt"], out=dt_[:], in_=ldt[:], func=AF.Exp)
        em.I("dve", "tensor_tensor", reads=[kp + "dt", kp + "are"], writes=[kp + "rho"], out=rho[:], in0=are[:], in1=dt_[:], op=ALU.mult)
        em.I("dve", "tensor_tensor", reads=[kp + "dt", kp + "aim"], writes=[kp + "th"], out=th[:], in0=aim[:], in1=dt_[:], op=ALU.mult)
        mag = T["mag"]
        em.I("act", "activation", reads=[kp + "rho"], writes=[kp + "mag"], out=mag[:], in_=rho[:], func=AF.Exp)
        kf, z = T["kf"], T["z"]
        MAGIC = 12582912.0
        em.I("dve", "tensor_scalar", reads=[kp + "th"], writes=[kp + "kf"], out=kf[:], in0=th[:], scalar1=1.0 / (2 * math.pi), scalar2=MAGIC,
             op0=ALU.mult, op1=ALU.add)
        em.I("dve", "tensor_scalar", reads=[kp + "kf"], writes=[kp + "kf"], out=kf[:], in0=kf[:], scalar1=-MAGIC, scalar2=None, op0=ALU.add)
        em.I("dve", "scalar_tensor_tensor", reads=[kp + "kf", kp + "th"], writes=[kp + "z"], out=z[:], in0=kf[:], scalar=-2 * math.pi, in1=th[:],
             op0=ALU.mult, op1=ALU.add)
        sw, sw2, cw = T["sw"], T["sw2"], T["cw"]
        em.I("act", "activation", reads=[kp + "z"], writes=[kp + "sw"], out=sw[:], in_=z[:], func=AF.Sin, scale=0.5)
        em.I("act", "activation", reads=[kp + "z"], writes=[kp + "sw2"], out=sw2[:], in_=z[:], func=AF.Sin, scale=0.25)
        em.I("dve", "tensor_tensor", reads=[kp + "sw2"], writes=[kp + "cw"], out=cw[:], in0=sw2[:], in1=sw2[:], op=ALU.mult)
        em.I("dve", "tensor_scalar", reads=[kp + "cw"], writes=[kp + "cw"], out=cw[:], in0=cw[:], scalar1=-2.0, scalar2=1.0, op0=ALU.mult, op1=ALU.add)
        sn, cs = T["sn"], T["cs"]
        em.I("dve", "scalar_tensor_tensor", reads=[kp + "sw", kp + "cw"], writes=[kp + "sn"], out=sn[:], in0=sw[:], scalar=2.0, in1=cw[:], op0=ALU.mult, op1=ALU.mult)
        em.I("dve", "tensor_tensor", reads=[kp + "sw"], writes=[kp + "cs"], out=cs[:], in0=sw[:], in1=sw[:], op=ALU.mult)
        em.I("dve", "tensor_scalar", reads=[kp + "cs"], writes=[kp + "cs"], out=cs[:], in0=cs[:], scalar1=-2.0, scalar2=1.0, op0=ALU.mult, op1=ALU.add)
        ar, ai = T["ar"], T["ai"]
        em.I("dve", "tensor_tensor", reads=[kp + "mag", kp + "cs"], writes=[kp + "ar"], out=ar[:], in0=mag[:], in1=cs[:], op=ALU.mult)
        em.I("dve", "tensor_tensor", reads=[kp + "mag", kp + "sn"], writes=[kp + "ai"], out=ai[:], in0=mag[:], in1=sn[:], op=ALU.mult)
        C1, C2 = T["C1"], T["C2"]
        o_ = r * 64
        em.I("dve", "tensor_copy", reads=[kp + "ar"], writes=["C1"], out=C1[:, o_:o_ + 32], in_=ar[:])
        em.I("dve", "tensor_copy", reads=[kp + "ar"], writes=["C1"], out=C1[:, o_ + 32:o_ + 64], in_=ar[:])
        em.I("dve", "tensor_scalar", reads=[kp + "ai"], writes=["C2"], out=C2[:, o_:o_ + 32], in0=ai[:], scalar1=-1.0, scalar2=None, op0=ALU.mult)
        em.I("dve", "tensor_copy", reads=[kp + "ai"], writes=["C2"], out=C2[:, o_ + 32:o_ + 64], in_=ai[:])
        nr, den, t1, t2, qre, qim = T["nr"], T["den"], T["t1"], T["t2"], T["qre"], T["qim"]
        em.I("dve", "tensor_scalar", reads=[kp + "ar"], writes=[kp + "nr"], out=nr[:], in0=ar[:], scalar1=-1.0, scalar2=None, op0=ALU.add)
        em.I("dve", "tensor_tensor", reads=[kp + "are"], writes=[kp + "den"], out=den[:], in0=are[:], in1=are[:], op=ALU.mult)
        em.I("dve", "tensor_tensor", reads=[kp + "aim"], writes=[kp + "t1"], out=t1[:], in0=aim[:], in1=aim[:], op=ALU.mult)
        em.I("dve", "tensor_tensor", reads=[kp + "den", kp + "t1"], writes=[kp + "den"], out=den[:], in0=den[:], in1=t1[:], op=ALU.add)
        em.I("dve", "reciprocal", reads=[kp + "den"], writes=[kp + "den"], out=den[:], in_=den[:])
        em.I("dve", "tensor_tensor", reads=[kp + "nr"], writes=[kp + "t1"], out=t1[:], in0=nr[:], in1=are[:], op=ALU.mult)
        em.I("dve", "tensor_tensor", reads=[kp + "ai"], writes=[kp + "t2"], out=t2[:], in0=ai[:], in1=aim[:], op=ALU.mult)
        em.I("dve", "tensor_tensor", reads=[kp + "t1", kp + "t2"], writes=[kp + "qre"], out=qre[:], in0=t1[:], in1=t2[:], op=ALU.add)
        em.I("dve", "tensor_tensor", reads=[kp + "qre", kp + "den"], writes=[kp + "qre"], out=qre[:], in0=qre[:], in1=den[:], op=ALU.mult)
        em.I("dve", "tensor_tensor", reads=[kp + "ai"], writes=[kp + "t1"], out=t1[:], in0=ai[:], in1=are[:], op=ALU.mult)
        em.I("dve", "tensor_tensor", reads=[kp + "nr"], writes=[kp + "t2"], out=t2[:], in0=nr[:], in1=aim[:], op=ALU.mult)
        em.I("dve", "tensor_tensor", reads=[kp + "t1", kp + "t2"], writes=[kp + "qim"], out=qim[:], in0=t1[:], in1=t2[:], op=ALU.subtract)
        em.I("dve", "tensor_tensor", reads=[kp + "qim", kp + "den"], writes=[kp + "qim"], out=qim[:], in0=qim[:], in1=den[:], op=ALU.mult)
        if r == 1 and self.dbg == 3:
            for nm in ("th", "z", "sn", "cs", "ar", "ai", "qre", "qim", "mag", "dt"):
                self.dump(nm, T[nm][:], [kp + nm])
            self.dump("C1", C1[:], ["C1"])
            self.dump("C2", C2[:], ["C2"])
        bre, bim, bbr, bbi, tb = T["bre"], T["bim"], T["bbr"], T["bbi"], T["tb"]
        em.dma("sp", bre[:], w["ssm_b_re"][j, r].rearrange("(gp gpar) p hi -> (gpar p) gp hi", gpar=2), writes=[kp + "bre"], slow=True)
        em.dma("sp", bim[:], w["ssm_b_im"][j, r].rearrange("(gp gpar) p hi -> (gpar p) gp hi", gpar=2), writes=[kp + "bim"], slow=True)
        qre_b = qre[:].unsqueeze(2).to_broadcast([128, 32, 16])
        qim_b = qim[:].unsqueeze(2).to_broadcast([128, 32, 16])
        em.I("dve", "tensor_tensor", reads=[kp + "bre", kp + "qre"], writes=[kp + "bbr"], out=bbr[:], in0=bre[:], in1=qre_b, op=ALU.mult)
        em.I("dve", "tensor_tensor", reads=[kp + "bim", kp + "qim"], writes=[kp + "tb"], out=tb[:], in0=bim[:], in1=qim_b, op=ALU.mult)
        em.I("dve", "tensor_tensor", reads=[kp + "bbr", kp + "tb"], writes=[kp + "bbr"], out=bbr[:], in0=bbr[:], in1=tb[:], op=ALU.subtract)
        em.I("dve", "tensor_tensor", reads=[kp + "bim", kp + "qre"], writes=[kp + "bbi"], out=bbi[:], in0=bim[:], in1=qre_b, op=ALU.mult)
        em.I("dve", "tensor_tensor", reads=[kp + "bre", kp + "qim"], writes=[kp + "tb"], out=tb[:], in0=bre[:], in1=qim_b, op=ALU.mult)
        em.I("dve", "tensor_tensor", reads=[kp + "bbi", kp + "tb"], writes=[kp + "bbi"], out=bbi[:], in0=bbi[:], in1=tb[:], op=ALU.add)
        Bq, WB, tps = T["Bq"], T["WB"][r], T["tps"]
        for ri, bb in enumerate((bbr, bbi)):
            for gp in range(32):
                k = gp % 4
                col = ri * 32 + gp
                em.I("dve", "tensor_copy", reads=[kp + "bbr", kp + "bbi", f"Bqz{k}"], writes=[f"Bq{k}"],
                     out=Bq[k][0:64, 32 * k:32 * k + 16], in_=bb[0:64, gp, :])
                em.I("dve", "tensor_copy", reads=[kp + "bbr", kp + "bbi"], writes=[f"Bq{k}"],
                     out=Bq[k][64:128, 32 * k + 16:32 * k + 32], in_=bb[64:128, gp, :])
                pb = col % 2
                em.I("pe", "transpose", reads=[f"Bq{k}", "identf"], writes=[f"tps{pb}"], out=tps[pb][:, 0:128], in_=Bq[k][:], identity=self.ident_f[:])
                em.I("act", "activation", reads=[f"tps{pb}"], writes=[f"WB{r}_{col}"], out=WB[:, col, :], in_=tps[pb][:, 0:128], func=AF.Copy)
        if r == 1 and self.dbg == 3:
            self.dump("bbr", bbr[:], [kp + "bbr"])
            self.dump("bbi", bbi[:], [kp + "bbi"])
            self.dump("WB", WB[:], [f"WB{r}_{c}" for c in range(64)])
        cin, Cq, WC = T["cin"], T["Cq"], T["WC"][r]
        for ri, nm in enumerate(("ssm_c_re", "ssm_c_im")):
            csrc = w[nm][j, r].rearrange("(gp gpar) ho p -> gp ho gpar p", gpar=2)
            for i in range(4):
                b = (ri * 4 + i) % 2
                for gl in range(8):
                    em.dma("sp", cin[b][gl * 16:(gl + 1) * 16, :].rearrange("ho (gpar p) -> ho gpar p", gpar=2), csrc[8 * i + gl],
                           writes=[f"cin{b}_{gl}"])
                em.I("pe", "transpose", reads=[f"cin{b}_{gl}" for gl in range(8)] + ["identf"], writes=[f"tps{b}"], out=tps[b][:, 0:128], in_=cin[b][:], identity=self.ident_f[:])
                em.I("act", "activation", reads=[f"tps{b}"], writes=[kp + f"Cq{ri}"], out=Cq[ri][:, 8 * i:8 * i + 8, :],
                     in_=tps[b][:, 0:128].rearrange("q (g h) -> q g h", h=16), func=AF.Copy, scale=(1.0 if ri == 0 else -1.0))
            for k in range(4):
                c0 = 32 * k
                em.I("dve", "tensor_copy", reads=[kp + f"Cq{ri}", "WCz"], writes=[f"WC{r}_{ri}"],
                     out=WC[0:64, ri * 32 + k:ri * 32 + 32:4, c0:c0 + 16], in_=Cq[ri][0:64, k:32:4, :])
                em.I("dve", "tensor_copy", reads=[kp + f"Cq{ri}", "WCz"], writes=[f"WC{r}_{ri}"],
                     out=WC[64:128, ri * 32 + k:ri * 32 + 32:4, c0 + 16:c0 + 32], in_=Cq[ri][64:128, k:32:4, :])

    def phase_s5(self, L):
        em, nc, S, w = self.em, self.nc, self.S, self.w
        j = L // 2
        NT = self.NT
        GC = 2.0 * math.sqrt(2.0 / math.pi)
        with contextlib.ExitStack() as st:
            self.em.barrier()
            G = self.sb(st, [128, D], F32, "G")
            hb = [self.sb(st, [128, D], F32, "hb") for _ in range(2)]
            un = self.sb(st, [128, D], F32, "un")
            u = self.sb(st, [128, D], BF16, "u")
            uTt = [self.sb(st, [128, 8, 128], BF16, "uTt") for _ in range(2)]
            junk = self.sb(st, [128, D], BF16, "junk")
            ss = self.sb(st, [128, 1], F32, "ss")
            ms = self.sb(st, [128, 1], F32, "ms")
            sd = self.sb(st, [128, 1], F32, "sd")
            rstd = self.sb(st, [128, 1], F32, "rstd")
            tpp = [self.ps(st, [128, 1024], BF16, "tpp") for _ in range(2)]
            rvp = [self.ps(st, [128, 1024], F32, "rvp") for _ in range(2)]
            uTrt = [self.sb(st, [128, 8, 128], BF16, "uTrt") for _ in range(2)]
            em.dma("sp", G[:], w["norm_mix"][L].partition_broadcast(128), writes=["G"], slow=True)
            em.dma("sp", hb[0][:], self.out[0:128, :], reads=["out0"], writes=["hb0"])
            for i in range(NT):
                b = i % 2
                if i + 1 < NT:
                    em.dma("sp", hb[1 - b][:], self.out[(i + 1) * 128:(i + 2) * 128, :], reads=[f"out{i + 1}"], writes=[f"hb{1 - b}"])
                self.rms_rstd([hb[b][:]], junk, ss, ms, sd, rstd, [f"hb{b}"], "p")
                em.I("act", "activation", reads=[f"hb{b}", "prstd"], writes=["un"], out=un[:], in_=hb[b][:], func=AF.Copy, scale=rstd[:, 0:1])
                em.I("dve", "tensor_tensor", reads=["un", "G"], writes=["u"], out=u[:], in0=un[:], in1=G[:], op=ALU.mult)
                for kt in range(8):
                    em.I("pe", "transpose", reads=["u", "identb"], writes=[f"tpp{b}"], out=tpp[b][:, kt * 128:(kt + 1) * 128],
                         in_=u[:, kt * 128:(kt + 1) * 128], identity=self.ident_b[:])
                em.I("dve", "tensor_copy", reads=[f"tpp{b}"], writes=[f"uTt{b}"], out=uTt[b][:], in_=tpp[b][:].rearrange("p (k t) -> p k t", k=8))
                em.dma(STORE_Q, self.uT_s[:, :, i * 128:(i + 1) * 128].rearrange("k p t -> p k t"), uTt[b][:],
                       reads=[f"uTt{b}"], writes=[f"uTs{i}"])
                for kt in range(8):
                    em.I("pe", "matmul", reads=["u", "antib"], writes=[f"rvp{b}"], out=rvp[b][:, kt * 128:(kt + 1) * 128],
                         lhsT=u[:, kt * 128:(kt + 1) * 128], rhs=self.anti_b[:], start=True, stop=True)
                em.I("act", "activation", reads=[f"rvp{b}"], writes=[f"uTr{b}"], out=uTrt[b][:], in_=rvp[b][:].rearrange("p (k t) -> p k t", k=8), func=AF.Copy)
                ir = NT - 1 - i
                if i == NT - 1:
                    self.dump("uTrt", uTrt[b][:], [f"uTr{b}"])
                    self.dump("uTt", uTt[b][:], [f"uTt{b}"])
                em.dma(STORE_Q, self.uTr_s[:, :, ir * 128:(ir + 1) * 128].rearrange("k p t -> p k t"), uTrt[b][:],
                       reads=[f"uTr{b}"], writes=[f"uTrs{ir}"])
        TB = 64
        NB = S // TB
        with contextlib.ExitStack() as st:
            self.em.barrier()
            T = {}
            for nm in ("are", "aim", "ldt", "dt", "rho", "th", "mag", "kf", "z", "sw", "sw2", "cw", "sn", "cs", "ar", "ai",
                       "nr", "den", "t1", "t2", "qre", "qim"):
                T[nm] = self.sb(st, [128, 32], F32, nm)
            T["C1"] = self.sb(st, [128, 128], F32, "C1")
            T["C2"] = self.sb(st, [128, 128], F32, "C2")
            for nm in ("bre", "bim", "bbr", "bbi", "tb"):
                T[nm] = self.sb(st, [128, 32, 16], F32, nm)
            T["Bq"] = [self.sb(st, [128, 128], F32, "Bq") for _ in range(4)]
            T["WB"] = [self.sb(st, [128, 64, 128], BF16, "WB") for _ in range(2)]
            T["WC"] = [self.sb(st, [128, 64, 128], BF16, "WC") for _ in range(2)]
            T["cin"] = [self.sb(st, [128, 128], F32, "cin") for _ in range(2)]
            T["Cq"] = [self.sb(st, [128, 32, 16], F32, "Cq") for _ in range(2)]
            T["tps"] = [self.ps(st, [128, 512], F32, "tps") for _ in range(2)]
            for k in range(4):
                em.I("pool", "memset", writes=[f"Bqz{k}", f"Bq{k}"], ap=T["Bq"][k][:], constant=0.0)
            for r in range(2):
                em.I("pool", "memset", writes=["WCz", f"WC{r}_0", f"WC{r}_1"], ap=T["WC"][r][:], constant=0.0)
            for r in range(2):
                self.s5_prep_dir(st, j, r, T)
            C1, C2, WB, WC = T["C1"], T["C2"], T["WB"], T["WC"]
            BUH = [self.sb(st, [128, TB, 128], F32, "BUH") for _ in range(2)]
            HB = self.sb(st, [128, TB, 128], BF16, "HB")
            uTb = [[self.sb(st, [128, 8, TB], BF16, "uTb") for _ in range(2)] for _ in range(2)]
            P1 = self.sb(st, [128, 128], F32, "P1")
            P2 = self.sb(st, [128, 128], F32, "P2")
            carry = self.sb(st, [128, 128], F32, "carry")
            yt = [[self.sb(st, [128, 8, TB], F32, "yt") for _ in range(2)] for _ in range(2)]
            bup = [self.ps(st, [128, 8, TB], F32, "bup") for _ in range(2)]
            yp = [self.ps(st, [128, 512], F32, "yp") for _ in range(2)]
            yq = [self.ps(st, [128, 512], F32, "yq") for _ in range(2)]
            ytm = [self.sb(st, [TB, 128], F32, "ytm") for _ in range(2)]
            em.I("pool", "memset", writes=["carry"], ap=carry[:], constant=0.0)
            ysc = (self.yf_s, self.yb_s)

            def blk(r, n):
                return n if r == 0 else NB - 1 - n

            def emit_load(n):
                b = n % 2
                for r in range(2):
                    srcT = self.uT_s if r == 0 else self.uTr_s
                    kn = f"uTs{n * TB // 128}" if r == 0 else f"uTrs{n * TB // 128}"
                    em.dma("sp", uTb[r][b][:], srcT[:, :, n * TB:(n + 1) * TB].rearrange("k p t -> p k t"),
                           reads=[kn], writes=[f"uTb{r}{b}"], slow=True)

            def emit_bu(n):
                b = n % 2
                for r in range(2):
                    for c8 in range(8):
                        pb = c8 % 2
                        for cc in range(8):
                            col = c8 * 8 + cc
                            kt = (col % 32) // 4
                            em.I("pe", "matmul", reads=[f"WB{r}_{col}", f"uTb{r}{b}"], writes=[f"bup{pb}"],
                                 out=bup[pb][:, cc, :], lhsT=WB[r][:, col, :], rhs=uTb[r][b][:, kt, :], start=True, stop=True)
                        dst = BUH[b][:, :, r * 64 + c8 * 8:r * 64 + c8 * 8 + 8]
                        em.I("act", "activation", reads=[f"bup{pb}"], writes=[f"buh{b}"],
                             out=dst.rearrange("p t c -> p c t"), in_=bup[pb][:], func=AF.Copy)

            emit_load(0)
            emit_bu(0)
            for n in range(NB):
                b = n % 2
                if n + 1 < NB:
                    emit_load(n + 1)
                    emit_bu(n + 1)
                for t in range(TB):
                    if t == 0:
                        hp = carry[:]
                        rk = [f"buh{b}", "carry", "C1", "C2"]
                    else:
                        hp = BUH[b][:, t - 1, :]
                        rk = [f"buh{b}", "C1", "C2"]
                    hps = hp.rearrange("p (d two c) -> p d two c", d=2, two=2)[:, :, ::-1, :]
                    em.I("dve", "tensor_tensor", reads=rk, writes=["P1"], nosync=(t > 0), out=P1[:], in0=hp, in1=C1[:], op=ALU.mult)
                    em.I("dve", "tensor_tensor", reads=rk, writes=["P2"], nosync=True,
                         out=P2[:].rearrange("p (d two c) -> p d two c", d=2, two=2), in0=hps,
                         in1=C2[:].rearrange("p (d two c) -> p d two c", d=2, two=2), op=ALU.mult)
                    em.I("dve", "tensor_tensor", reads=["P1", "P2"], writes=["P1"], nosync=True, out=P1[:], in0=P1[:], in1=P2[:], op=ALU.add)
                    em.I("dve", "tensor_tensor", reads=["P1", f"buh{b}"], writes=[f"buh{b}"], nosync=True, out=BUH[b][:, t, :], in0=P1[:],
                         in1=BUH[b][:, t, :], op=ALU.add)
                em.I("dve", "tensor_copy", reads=[f"buh{b}"], writes=["carry"], out=carry[:], in_=BUH[b][:, TB - 1, :])
                if n == 0:
                    self.dump("BUH0", BUH[0][:], ["buh0"])
                    self.dump("BUH1pre", BUH[1][:], ["buh1"])
                    self.dump("uTb10", uTb[1][0][:], ["uTb10"])
                    self.dump("Wc0", WC[0][:], ["WC0_0", "WC0_1"])
                em.I("act", "activation", reads=[f"buh{b}"], writes=["HB"], out=HB[:], in_=BUH[b][:], func=AF.Copy)
                for r in range(2):
                    i = blk(r, n)
                    for kt in range(8):
                        pb = kt % 2
                        n_mm = 0
                        for ri in range(2):
                            for g4 in range(4):
                                col = ri * 32 + kt * 4 + g4
                                if r == 0:
                                    em.I("pe", "matmul", reads=[f"WC{r}_{ri}", "HB"], writes=[f"yp{pb}"], out=yp[pb][:, 0:TB],
                                         lhsT=WC[r][:, col, :], rhs=HB[:, :, col], start=(n_mm == 0), stop=(n_mm == 7))
                                else:
                                    em.I("pe", "matmul", reads=[f"WC{r}_{ri}", "HB"], writes=[f"yp{pb}"], out=yp[pb][0:TB, 0:128],
                                         lhsT=HB[:, :, 64 + col], rhs=WC[r][:, col, :], start=(n_mm == 0), stop=(n_mm == 7))
                                n_mm += 1
                        if r == 0:
                            em.I("act", "activation", reads=[f"yp{pb}"], writes=[f"yt{r}{b}"], out=yt[r][b][:, kt, :], in_=yp[pb][:, 0:TB], func=AF.Copy)
                        else:
                            em.I("act", "activation", reads=[f"yp{pb}"], writes=[f"ytm{pb}"], out=ytm[pb][:], in_=yp[pb][0:TB, 0:128], func=AF.Copy)
                            em.I("pe", "matmul", reads=[f"ytm{pb}", "antif"], writes=[f"yq{pb}"], out=yq[pb][:, 0:TB], lhsT=ytm[pb][:],
                                 rhs=self.anti_f[:], start=True, stop=True)
                            em.I("act", "activation", reads=[f"yq{pb}"], writes=[f"yt{r}{b}"], out=yt[r][b][:, kt, :], in_=yq[pb][:, 0:TB], func=AF.Copy)
                    if n == 0:
                        self.dump(f"yt{r}", yt[r][b][:], [f"yt{r}{b}"])
                    if n == NB - 1:
                        self.dump(f"ytL{r}", yt[r][b][:], [f"yt{r}{b}"])
                        if r == 1:
                            self.dump("BUHL", BUH[b][:], [f"buh{b}"])
                    em.dma(STORE_Q, ysc[r][:, :, i * TB:(i + 1) * TB].rearrange("k p t -> p k t"), yt[r][b][:],
                           reads=[f"yt{r}{b}"], writes=[f"ys{r}_{i}"], slow=True)
        with contextlib.ExitStack() as st:
            self.em.barrier()
            Wg = self.sb(st, [128, 8, 2048], BF16, "Wg")
            stgt = self.sb(st, [128, 2, 2048], F32, "stg")
            stg = [stgt[:, 0, :], stgt[:, 1, :]]
            uTt = [self.sb(st, [128, 8, 128], BF16, "uTt") for _ in range(2)]
            yft = [self.sb(st, [128, 8, 128], F32, "yft") for _ in range(2)]
            ybt = [self.sb(st, [128, 8, 128], F32, "ybt") for _ in range(2)]
            hb = [self.sb(st, [128, D], F32, "hb") for _ in range(2)]
            dT = self.sb(st, [128, 8], F32, "dT")
            yall = self.sb(st, [128, 8, 128], F32, "yall")
            g2 = self.sb(st, [128, 8, 128], F32, "g2")
            gT = self.sb(st, [128, 8, 128], BF16, "gT")
            sig = self.sb(st, [128, D], F32, "sig")
            mix = self.sb(st, [128, D], F32, "mix")
            glp = [self.ps(st, [128, 512], F32, "glp") for _ in range(4)]
            wg = w["ssm_w_glu"][j]
            em.dma("sp", dT[:], w["ssm_d"][j].rearrange("(kt p) -> p kt", p=128), writes=["dT"], slow=True)
            self.load_weight([Wg[:, i, :] for i in range(8)], [wg[i * 128:(i + 1) * 128, :] for i in range(8)], stg, None, "Wg")

            def load(i):
                b = i % 2
                sl = slice(i * 128, (i + 1) * 128)
                ykeys = [f"ys{r}_{k}" for r in range(2) for k in (2 * i, 2 * i + 1)]
                em.dma("sp", uTt[b][:], self.uT_s[:, :, sl].rearrange("k p t -> p k t"), reads=[f"uTs{i}"], writes=[f"uTt{b}"], slow=True)
                em.dma("sp", yft[b][:], self.yf_s[:, :, sl].rearrange("k p t -> p k t"), reads=ykeys, writes=[f"yft{b}"], slow=True)
                ix = em.dma("sp", ybt[b][:], self.yb_s[:, :, sl].rearrange("k p t -> p k t"), reads=ykeys, writes=[f"ybt{b}"], slow=True)
                if i == 0:
                    em.debug_ops = {"ybt_load0": ix, "yft_load0": ix - 1}
                em.dma("sp", hb[b][:], self.out[sl, :], reads=[f"out{i}"], writes=[f"hb{b}"])

            load(0)
            for i in range(NT):
                b = i % 2
                if i + 1 < NT:
                    load(i + 1)
                em.I("pool", "tensor_tensor", reads=[f"yft{b}", f"ybt{b}"], writes=["yall"], out=yall[:], in0=yft[b][:], in1=ybt[b][:], op=ALU.add)
                em.I("dve", "tensor_tensor", reads=[f"uTt{b}", "dT"], writes=["g2"], out=g2[:], in0=uTt[b][:],
                     in1=dT[:].unsqueeze(2).to_broadcast([128, 8, 128]), op=ALU.mult)
                em.I("dve", "tensor_tensor", reads=["g2", "yall"], writes=["yall"], out=yall[:], in0=yall[:], in1=g2[:], op=ALU.add)
                em.I("dve", "tensor_tensor", reads=["yall"], writes=["g2"], out=g2[:], in0=yall[:], in1=yall[:], op=ALU.mult)
                em.I("dve", "tensor_scalar", reads=["g2"], writes=["g2"], out=g2[:], in0=g2[:], scalar1=0.044715, scalar2=1.0, op0=ALU.mult, op1=ALU.add)
                em.I("dve", "tensor_tensor", reads=["g2", "yall"], writes=["g2"], out=g2[:], in0=g2[:], in1=yall[:], op=ALU.mult)
                em.I("act", "activation", reads=["g2"], writes=["g2"], out=g2[:], in_=g2[:], func=AF.Sigmoid, scale=GC)
                em.I("dve", "tensor_tensor", reads=["g2", "yall"], writes=["gT"], out=gT[:], in0=g2[:], in1=yall[:], op=ALU.mult)
                if i == 0:
                    self.dump("yall", yall[:], ["yall"])
                    self.dump("yft", yft[b][:], [f"yft{b}"])
                    self.dump("ybt", ybt[b][:], [f"ybt{b}"])
                for c in range(4):
                    for kt in range(8):
                        em.I("pe", "matmul", reads=[f"Wg{kt}", "gT"], writes=[f"glp{c}"], out=glp[c][:], lhsT=gT[:, kt, :],
                             rhs=Wg[:, kt, c * 512:(c + 1) * 512], start=(kt == 0), stop=(kt == 7))
                for c in range(2):
                    em.I("act", "activation", reads=[f"glp{2 + c}"], writes=[f"sig{c}"], out=sig[:, c * 512:(c + 1) * 512], in_=glp[2 + c][:], func=AF.Sigmoid)
                    em.I("dve", "tensor_tensor", reads=[f"glp{c}", f"sig{c}"], writes=[f"mix{c}"], out=mix[:, c * 512:(c + 1) * 512], in0=glp[c][:],
                         in1=sig[:, c * 512:(c + 1) * 512], op=ALU.mult)
                em.I("pool", "tensor_tensor", reads=["mix0", "mix1", f"hb{b}"], writes=[f"hb{b}"], out=hb[b][:], in0=hb[b][:], in1=mix[:], op=ALU.add)
                em.dma(STORE_Q, self.out[i * 128:(i + 1) * 128, :], hb[b][:], reads=[f"hb{b}"], writes=[f"out{i}"])
            if "ybsdump" in FLAGS:
                self.dump("ybs_dram", self.yb_s[:, :, :], [f"out{NT - 1}"])

def make_consts(S):
    ident = np.eye(128, dtype=np.float32)
    pos = np.arange(S, dtype=np.float32)
    inv_freq = (10000.0 ** (-np.arange(0, 64, 2, dtype=np.float32) / 64)).astype(np.float32)
    ang = pos[:, None] * inv_freq[None, :]
    rope = np.concatenate([np.cos(ang), np.sin(ang)], axis=1).astype(np.float32)
    anti = np.ascontiguousarray(ident[::-1])
    return {"ident_f": ident, "ident_b": ident.astype(ml_dtypes.bfloat16), "rope": rope,
            "anti_b": anti.astype(ml_dtypes.bfloat16), "anti_f": np.ascontiguousarray(np.eye(64, dtype=np.float32)[::-1])}


_CACHE = {}
DUMPS = None
FLAGS = set()
DBG = 0
STORE_Q = "pool"


def get_prog(S, phases):
    key = (S, tuple(phases))
    if key not in _CACHE:
        p = Prog(S, phases)
        p.dbg = DBG
        p.build()
        _CACHE[key] = p
    return _CACHE[key]


FULL_PHASES = (("attn", 0, "x"), ("ffn", 0), ("s5", 1), ("ffn", 1), ("attn", 2), ("ffn", 2), ("s5", 3), ("ffn", 3))


def run(inputs, phases, n_cores=8, trace=False):
    x = np.asarray(inputs["x"], dtype=np.float32)
    B, S, _ = x.shape
    prog = get_prog(S, phases)
    consts = make_consts(S)
    base = {n: np.ascontiguousarray(np.asarray(inputs[n], dtype=np.float32)) for n, _ in INPUT_SPECS}
    base.update(consts)
    in_maps = []
    for c in range(n_cores):
        m = dict(base)
        m["x"] = np.ascontiguousarray(x[c])
        in_maps.append(m)
    res = run_bass_kernel_spmd(prog.nc, in_maps, core_ids=list(range(n_cores)), trace=trace)
    outs = np.stack([np.asarray(r["out"]) for r in res.results], axis=0)
    return outs, res


def kernel(**inputs):
    outs, _ = run(inputs, FULL_PHASES, n_cores=8)
    return outs.astype(np.float32)
```
